# Optimizing a Trainium2 kernel written in Bass

```python
import math
import jax, jax.numpy as jnp
from jax import lax
import numpy as np

D_MODEL = 1024
BATCH = 1
SEQ = 16384
DEPTH = 1

D_MIX = D_MODEL
CONV_WIDTH = D_MIX // 2
CONV_K = 3
RWKV_HEAD = 64
RWKV_WIDTH = D_MIX - CONV_WIDTH
RWKV_HEADS = RWKV_WIDTH // RWKV_HEAD
DECAY_LORA = max(32, int(round(1.8 * math.sqrt(RWKV_WIDTH) / 32)) * 32)
AAA_LORA = max(32, int(round(1.8 * math.sqrt(RWKV_WIDTH) / 32)) * 32)
GATE_LORA = max(32, int(round(0.6 * RWKV_WIDTH ** 0.8 / 32)) * 32)
RWKV_COLS = 3 * RWKV_WIDTH + DECAY_LORA + AAA_LORA + GATE_LORA
IN_COLS = 3 * CONV_WIDTH + RWKV_COLS
D_FF = 128 * ((8 * D_MODEL // 3 + 127) // 128)
MACARON_W = 0.5
NORM_EPS = 1e-6
GN_EPS = 64e-5
N_ADA = 9

kernel_name = "hybrid_conv_rwkv7_macaron_adaln"


def rms_norm(x, g):
    xf = x.astype(jnp.float32)
    y = xf * lax.rsqrt(jnp.mean(xf * xf, axis=-1, keepdims=True) + NORM_EPS)
    return (y * g.astype(jnp.float32)).astype(x.dtype)


def modulate(x, g_pre, shift, scale):
    return rms_norm(x, g_pre) * (1 + scale[:, None, :]) + shift[:, None, :]


def swiglu(u, w1, w3, w2):
    return (jax.nn.silu(u @ w1) * (u @ w3)) @ w2


def time_shift(u):
    return jnp.pad(u, ((0, 0), (1, 0), (0, 0)))[:, :-1]


def causal_depthwise_conv(u, w):
    return lax.conv_general_dilated(
        u, w[:, None, :].astype(u.dtype), window_strides=(1,),
        padding=((CONV_K - 1, 0),), dimension_numbers=('NWC', 'WIO', 'NWC'),
        feature_group_count=u.shape[-1])


def rwkv7_scan(r, decay, k, v, kk, a):
    def step(S, inp):
        r_t, w_t, k_t, v_t, kk_t, a_t = inp
        sa = jnp.einsum('bhvk,bhk->bhv', S, -kk_t)
        S = (S * w_t[:, :, None, :]
             + sa[..., None] * (kk_t * a_t)[:, :, None, :]
             + v_t[..., None] * k_t[:, :, None, :])
        y = jnp.einsum('bhvk,bhk->bhv', S, r_t)
        return S, y
    Bsz = r.shape[0]
    xs = tuple(jnp.moveaxis(t.astype(jnp.float32), 1, 0) for t in (r, decay, k, v, kk, a))
    S0 = jnp.zeros((Bsz, RWKV_HEADS, RWKV_HEAD, RWKV_HEAD), jnp.float32)
    _, ys = lax.scan(step, S0, xs)
    return jnp.moveaxis(ys, 0, 1)


def mixer(h, w_in, conv_w, mu_shift, w0, w_up, a0, a_up, g_up, k_k, k_a, r_k,
          ln_x_w, ln_x_b, w_out):
    Bsz, T, _ = h.shape
    f = lambda t: t.astype(jnp.float32)
    p = h @ w_in
    C = CONV_WIDTH
    c_pre, c_post, c_val, rw = jnp.split(p, [C, 2 * C, 3 * C], axis=-1)
    y_conv = c_post * causal_depthwise_conv(c_pre * c_val, conv_w)
    rw = f(rw + (time_shift(rw) - rw) * mu_shift)
    R = RWKV_WIDTH
    xr, xk, xv, xw, xa, xg = jnp.split(
        rw, [R, 2 * R, 3 * R, 3 * R + DECAY_LORA, 3 * R + DECAY_LORA + AAA_LORA], axis=-1)
    w_raw = -jax.nn.softplus(-(f(w0) + jnp.tanh(xw) @ f(w_up))) - 0.5
    decay = jnp.exp(-jnp.exp(w_raw))
    a = jax.nn.sigmoid(f(a0) + xa @ f(a_up))
    g = jax.nn.sigmoid(xg) @ f(g_up)
    hs = lambda t: t.reshape(Bsz, T, RWKV_HEADS, RWKV_HEAD)
    hp = lambda t: f(t).reshape(RWKV_HEADS, RWKV_HEAD)
    r, k, v, decay, a = hs(xr), hs(xk), hs(xv), hs(decay), hs(a)
    kk = k * hp(k_k)
    kk = kk / jnp.maximum(jnp.sqrt(jnp.sum(kk * kk, axis=-1, keepdims=True)), 1e-12)
    k = k * (1 + (a - 1) * hp(k_a))
    o = rwkv7_scan(r, decay, k, v, kk, a)
    o_mean = jnp.mean(o, axis=-1, keepdims=True)
    o_var = jnp.mean(jnp.square(o - o_mean), axis=-1, keepdims=True)
    o = ((o - o_mean) * lax.rsqrt(o_var + GN_EPS)).reshape(Bsz, T, R) * f(ln_x_w) + f(ln_x_b)
    bonus = (jnp.sum(r * k * f(r_k), axis=-1, keepdims=True) * v).reshape(Bsz, T, R)
    y_rwkv = ((o + bonus) * g).astype(h.dtype)
    return jnp.concatenate([y_conv, y_rwkv], axis=-1) @ w_out


def setup_inputs(seed: int = 0) -> dict:
    key = jax.random.key(seed)
    keys = jax.random.split(key, 32)
    counter = [0]

    def nk():
        kk = keys[counter[0]]
        counter[0] += 1
        return kk

    nrm = lambda shape, s: jax.random.normal(nk(), shape, jnp.float32) * s
    uni = lambda shape, lo, hi: jax.random.uniform(nk(), shape, jnp.float32, lo, hi)
    L, D = DEPTH, D_MODEL
    return {
        "x": nrm((BATCH, SEQ, D), 1.0),
        "c": nrm((BATCH, D), 1.0),
        "w_ada": nrm((L, D, N_ADA * D), 0.5 * D ** -0.5),
        "b_ada": nrm((L, N_ADA * D), 0.02),
        "ffn1_g_pre": 1.0 + nrm((L, D), 0.02),
        "ffn1_w1": nrm((L, D, D_FF), D ** -0.5),
        "ffn1_w3": nrm((L, D, D_FF), D ** -0.5),
        "ffn1_w2": nrm((L, D_FF, D), D_FF ** -0.5),
        "ffn1_g_post": 1.0 + nrm((L, D), 0.02),
        "mix_g_pre": 1.0 + nrm((L, D), 0.02),
        "w_in": nrm((L, D, IN_COLS), D ** -0.5),
        "conv_w": nrm((L, CONV_K, CONV_WIDTH), CONV_K ** -0.5),
        "mu_shift": uni((L, RWKV_COLS), 0.0, 1.0),
        "w0": uni((L, RWKV_WIDTH), -5.0, 0.0),
        "w_up": nrm((L, DECAY_LORA, RWKV_WIDTH), 0.1 * DECAY_LORA ** -0.5),
        "a0": nrm((L, RWKV_WIDTH), 0.1),
        "a_up": nrm((L, AAA_LORA, RWKV_WIDTH), 0.1 * AAA_LORA ** -0.5),
        "g_up": nrm((L, GATE_LORA, RWKV_WIDTH), GATE_LORA ** -0.5),
        "k_k": 0.85 + nrm((L, RWKV_WIDTH), 0.02),
        "k_a": 1.0 + nrm((L, RWKV_WIDTH), 0.02),
        "r_k": nrm((L, RWKV_HEADS, RWKV_HEAD), 0.1),
        "ln_x_w": 1.0 + nrm((L, RWKV_WIDTH), 0.02),
        "ln_x_b": nrm((L, RWKV_WIDTH), 0.02),
        "w_out": nrm((L, D_MIX, D), D_MIX ** -0.5),
        "mix_g_post": 1.0 + nrm((L, D), 0.02),
        "ffn2_g_pre": 1.0 + nrm((L, D), 0.02),
        "ffn2_w1": nrm((L, D, D_FF), D ** -0.5),
        "ffn2_w3": nrm((L, D, D_FF), D ** -0.5),
        "ffn2_w2": nrm((L, D_FF, D), D_FF ** -0.5),
        "ffn2_g_post": 1.0 + nrm((L, D), 0.02),
    }


def reference(x, c, w_ada, b_ada,
              ffn1_g_pre, ffn1_w1, ffn1_w3, ffn1_w2, ffn1_g_post,
              mix_g_pre, w_in, conv_w, mu_shift, w0, w_up, a0, a_up, g_up,
              k_k, k_a, r_k, ln_x_w, ln_x_b, w_out, mix_g_post,
              ffn2_g_pre, ffn2_w1, ffn2_w3, ffn2_w2, ffn2_g_post):
    Bsz = x.shape[0]
    cond = jax.nn.silu(c)
    h = x
    for l in range(DEPTH):
        ada = (cond @ w_ada[l] + b_ada[l]).reshape(Bsz, N_ADA, D_MODEL)
        sh1, sc1, gt1, sh2, sc2, gt2, sh3, sc3, gt3 = [ada[:, i] for i in range(N_ADA)]
        u = modulate(h, ffn1_g_pre[l], sh1, sc1)
        y = swiglu(u, ffn1_w1[l], ffn1_w3[l], ffn1_w2[l])
        h = h + MACARON_W * gt1[:, None, :] * rms_norm(y, ffn1_g_post[l])
        u = modulate(h, mix_g_pre[l], sh2, sc2)
        y = mixer(u, w_in[l], conv_w[l], mu_shift[l], w0[l], w_up[l], a0[l], a_up[l],
                  g_up[l], k_k[l], k_a[l], r_k[l], ln_x_w[l], ln_x_b[l], w_out[l])
        h = h + gt2[:, None, :] * rms_norm(y, mix_g_post[l])
        u = modulate(h, ffn2_g_pre[l], sh3, sc3)
        y = swiglu(u, ffn2_w1[l], ffn2_w3[l], ffn2_w2[l])
        h = h + MACARON_W * gt3[:, None, :] * rms_norm(y, ffn2_g_post[l])
    return h
```

```python
from contextlib import ExitStack
import numpy as np
import concourse.bass as bass
import concourse.mybir as mybir
from concourse.bass_utils import run_bass_kernel_spmd

F32 = mybir.dt.float32
BF16 = mybir.dt.bfloat16
ALU = mybir.AluOpType
AF = mybir.ActivationFunctionType

NCORES = 8
D = 1024
KC = 8
DFF = 2816
JC = 22
SEQ = 16384
NTOK = SEQ // NCORES
TB = 512
NT = NTOK // TB
CH = 64
NCH = TB // CH
C0 = float(np.exp(-0.5))
NORM_EPS = 1e-6
GN_EPS = 64e-5
STAGE = 3
DEBUG = False
NRUN = 8
FAKE_CC = False

ENGS = ("pe", "act", "dve", "pool", "sp")


class Op:
    __slots__ = ("eng", "fn", "reads", "writes", "signal", "dsem", "idx", "waits", "sigval", "nd", "deps", "inc")

    def __init__(self, eng, fn, reads, writes, signal, dsem, nd):
        self.eng, self.fn, self.reads, self.writes = eng, fn, reads, writes
        self.signal, self.dsem, self.nd = signal, dsem, nd
        self.waits = []
        self.sigval = None


class Prog:
    def __init__(self, nc):
        self.nc = nc
        self.ops = {e: [] for e in ENGS}
        self.order = []
        self.last_writer = {}
        self.readers = {}
        self.dsem_count = {}

    def op(self, eng, fn, reads=(), writes=(), signal=True):
        o = Op(eng, fn, tuple(reads), tuple(writes), signal, None, 0)
        self._add(o)
        return o

    def dma(self, eng, fn, reads=(), writes=(), dsem=None, nd=1, inc=16):
        o = Op(eng, fn, tuple(reads), tuple(writes), True, dsem, nd)
        o.inc = inc
        self._add(o)
        return o

    def _add(self, o):
        o.idx = len(self.ops[o.eng])
        self.ops[o.eng].append(o)
        self.order.append(o)
        deps = []
        for k in o.reads:
            w = self.last_writer.get(k)
            if w is not None:
                deps.append((w, "raw"))
        for k in o.writes:
            w = self.last_writer.get(k)
            if w is not None:
                deps.append((w, "waw"))
            for r in self.readers.get(k, ()):
                deps.append((r, "war"))
        o.deps = deps
        for k in o.writes:
            self.last_writer[k] = o
            self.readers[k] = []
        for k in o.reads:
            self.readers.setdefault(k, []).append(o)

    def finalize_and_emit(self):
        nc = self.nc
        for e in ENGS:
            cnt = 0
            for o in self.ops[e]:
                if o.dsem is not None:
                    c = self.dsem_count.get(o.dsem, 0) + o.inc * o.nd
                    self.dsem_count[o.dsem] = c
                    o.sigval = ("d_" + o.dsem, c)
                elif o.signal:
                    cnt += 1
                    o.sigval = ("e_" + e, cnt)
            nxt = None
            for o in reversed(self.ops[e]):
                if o.dsem is None:
                    if o.signal:
                        nxt = o.sigval
                    else:
                        assert nxt is not None
                        o.sigval = nxt
        for e in ENGS:
            known = {}
            for o in self.ops[e]:
                need = {}
                for (d, kind) in o.deps:
                    if d is o:
                        continue
                    if d.eng == e and d.dsem is None:
                        if kind != "raw" or e == "pe":
                            continue
                    s, v = d.sigval
                    if known.get(s, 0) >= v:
                        continue
                    if need.get(s, 0) < v:
                        need[s] = v
                for s, v in need.items():
                    known[s] = v
                o.waits = list(need.items())
        sem_names = sorted({o.sigval[0] for o in self.order})
        with ExitStack() as st:
            sems = {n: st.enter_context(nc.semaphore(n)) for n in sem_names}
            block = st.enter_context(nc.Block())
            reg = {"pe": block.tensor, "act": block.scalar, "dve": block.vector,
                   "pool": block.gpsimd, "sp": block.sync}

            def mk(e):
                def body(eng):
                    for o in self.ops[e]:
                        for s, v in o.waits:
                            eng.wait_ge(sems[s], v)
                        r = o.fn(eng)
                        if o.dsem is not None:
                            rs = r if isinstance(r, (list, tuple)) else [r]
                            assert len(rs) == o.nd, (len(rs), o.nd)
                            for ins in rs:
                                ins.then_inc(sems[o.sigval[0]], o.inc)
                        elif o.signal:
                            r.then_inc(sems[o.sigval[0]], 1)
                return body

            final = {}
            for o in self.order:
                s, v = o.sigval
                final[s] = max(final.get(s, 0), v)
            for e in ENGS:
                if e != "sp" and self.ops[e]:
                    reg[e](mk(e))

            def sp_body(eng):
                mk("sp")(eng)
                for s, v in final.items():
                    eng.wait_ge(sems[s], v)

            reg["sp"](sp_body)
        return len(sem_names)


C_IDENT = 0
C_BONES = 128
C_ONES = 256
C_MASK4 = 384
C_MASKL = 512
C_RESET = 576
C_HINIT = 1088
C_IDB = 1216
NCONST = 1216

V_C = 0
V_GPRE1 = 8
V_GPOST1 = 16
V_GPRE2 = 24
V_GPOST2 = 32
V_GPRE3 = 40
V_GPOST3 = 48
V_MU = 56
V_CONV = 70
V_W0 = 82
V_A0 = 86
V_KK = 90
V_KA = 94
V_RK = 98
V_LNW = 102
V_LNB = 106
V_FLAG = 110
V_ONEHOT = 111
NVEC = 119


def _consts():
    c = np.zeros((128, NCONST), np.float32)
    p = np.arange(128)
    c[p, C_IDENT + p] = 1.0
    c[:, C_BONES:C_BONES + 128] = (p[:, None] // 64 == p[None, :] // 64).astype(np.float32)
    c[:, C_ONES:C_ONES + 128] = 1.0
    s = (p % 64)[:, None]
    t = np.arange(64)[None, :]
    c[:, C_MASK4:C_MASK4 + 64] = (s < t)
    c[:, C_MASK4 + 64:C_MASK4 + 128] = (s <= t)
    c[:, C_MASKL:C_MASKL + 64] = (t < s)
    r = np.ones((512,), np.float32)
    r[::64] = 0.0
    c[:, C_RESET:C_RESET + 512] = r[None, :]
    hi = np.zeros((128, 128), np.float32)
    hi[np.arange(64), 64 + np.arange(64)] = 1.0
    hi[64 + np.arange(64), np.arange(64)] = 1.0
    c[:, C_HINIT:C_HINIT + 128] = hi
    return c


def _col(v, n):
    return np.ascontiguousarray(np.asarray(v, np.float32).reshape(n, 128).T)


def build():
    nc = bass.Bass("TRN2", target_bir_lowering=False)
    dt = nc.dram_tensor
    xT = dt("xT", [D, NTOK + 2], F32, kind="ExternalInput").ap()
    vec = dt("vec", [128, NVEC], F32, kind="ExternalInput").ap()
    cst = dt("cst", [128, NCONST], F32, kind="ExternalInput").ap()
    b_ada = dt("b_ada", [1, 9 * D], F32, kind="ExternalInput").ap()
    w_ada = dt("w_ada", [D, 9 * D], F32, kind="ExternalInput").ap()
    fw = {}
    for n in ("ffn1", "ffn2"):
        fw[n] = (dt(n + "_w1", [D, DFF], F32, kind="ExternalInput").ap(),
                 dt(n + "_w3", [D, DFF], F32, kind="ExternalInput").ap(),
                 dt(n + "_w2", [DFF, D], F32, kind="ExternalInput").ap())
    w_in = dt("w_in", [D, 3232], F32, kind="ExternalInput").ap()
    w_out = dt("w_out", [D, D], F32, kind="ExternalInput").ap()
    lora_wa = dt("lora_wa", [64, 512], F32, kind="ExternalInput").ap()
    lora_g = dt("lora_g", [96, 512], F32, kind="ExternalInput").ap()
    outT = dt("outT", [D, NTOK], F32, kind="ExternalOutput").ap()
    h_scr = dt("h_scr", [D, NTOK], F32).ap()
    ye_scr = dt("ye_scr", [8 * 128, NTOK], F32).ap()
    g_scr = dt("g_scr", [512, NTOK], F32).ap()
    bo_scr = dt("bo_scr", [512, NTOK], F32).ap()
    yc_scr = dt("yc_scr", [512, NTOK], BF16).ap()
    hx_loc = dt("hx_loc", [128, 512], F32).ap()
    hx_all = dt("hx_all", [NCORES * 128, 512], F32).ap()

    xTv = xT.rearrange("(kc p) n -> p kc n", p=128)
    outTv = outT.rearrange("(kc p) n -> p kc n", p=128)
    h_scrv = h_scr.rearrange("(kc p) n -> p kc n", p=128)

    st = ExitStack()
    sb = lambda name, shape, dtype=F32: st.enter_context(nc.sbuf_tensor(name, shape, dtype))
    P = Prog(nc)

    CST = sb("CST", [128, NCONST])
    VEC = sb("VEC", [128, NVEC])
    ADA = sb("ADA", [128, 72])
    DER = sb("DER", [128, 64])
    LWA = sb("LWA", [64, 512], BF16)
    LG = sb("LG", [96, 512], BF16)
    XT = [sb(f"XT{i}", [128, KC, TB]) for i in range(1)]
    U = sb("U", [128, KC, TB], BF16)
    SCR = [sb(f"SCR{i}", [128, TB]) for i in range(4)]
    SCB = [sb(f"SCB{i}", [128, TB], BF16) for i in range(2)]
    G = sb("G", [128, JC, TB], BF16)
    YSk = [sb(f"YS{m}", [128, TB]) for m in range(KC)]
    NW13 = 3
    W13 = [sb(f"W13_{i}", [128, KC, 256], BF16) for i in range(NW13)]
    W2 = [sb(f"W2_{i}", [128, JC, 128], BF16) for i in range(2)]
    RSTD = sb("RSTD", [128, TB])
    PS = [st.enter_context(nc.psum_tensor(f"ps{b}", [128, 512], F32)) for b in range(8)]
    bank_ctr = [0]

    def newbank():
        b = bank_ctr[0] % 8
        bank_ctr[0] += 1
        return b

    ident = CST[:, C_IDENT:C_IDENT + 128]
    ones = CST[:, C_ONES:C_ONES + 128]
    bones = CST[:, C_BONES:C_BONES + 128]

    def vcol(off, i=0):
        return VEC[:, off + i:off + i + 1]

    P.dma("sp", lambda e: e.dma_start(out=CST[:], in_=cst[:, :]), writes=["CST"], dsem="cst")
    P.dma("sp", lambda e: e.dma_start(out=VEC[:], in_=vec[:, :]), writes=["VEC"], dsem="vec")
    P.dma("pool", lambda e: e.dma_start(out=LWA[:], in_=lora_wa[:, :]), writes=["LWA"], dsem="lwa")
    P.dma("pool", lambda e: e.dma_start(out=LG[:], in_=lora_g[:, :]), writes=["LG"], dsem="lg")

    CONDB = sb("CONDB", [128, 8], BF16)
    BROW = sb("BROW", [1, 256])
    AROW = sb("AROW", [1, 256])
    P.op("act", lambda e: e.activation(out=CONDB[:], in_=VEC[:, V_C:V_C + 8], func=AF.Silu),
         reads=["VEC"], writes=["CONDB"])
    w_adav = w_ada.rearrange("(kc p) n -> p kc n", p=128)
    adab = newbank()
    for n in range(36):
        s = n % NW13
        P.dma("pool", lambda e, n=n, s=s: e.dma_start(out=W13[s][:], in_=w_adav[:, :, n * 256:(n + 1) * 256]),
              writes=[("W13", s)], dsem=f"w13_{s}")
        P.dma("sp", lambda e, n=n: e.dma_start(out=BROW[:], in_=b_ada[:, n * 256:(n + 1) * 256]),
              writes=["BROW"], dsem="brow")
        b = newbank()
        if b == adab:
            b = newbank()
        for kc in range(KC):
            P.op("pe", lambda e, b=b, kc=kc, s=s: e.matmul(PS[b][0:1, 0:256], lhsT=CONDB[:, kc:kc + 1],
                                                        rhs=W13[s][:, kc, :], start=(kc == 0), stop=(kc == KC - 1)),
                 reads=["CONDB", ("W13", s)], writes=[("ps", b)], signal=(kc == KC - 1))
        P.op("dve", lambda e, b=b: e.tensor_tensor(out=AROW[:], in0=PS[b][0:1, 0:256], in1=BROW[:], op=ALU.add),
             reads=[("ps", b), "BROW"], writes=["AROW"])
        for hh in range(2):
            col = n * 2 + hh
            P.op("pe", lambda e, col=col, hh=hh: e.matmul(PS[adab][:, col:col + 1],
                                                         lhsT=AROW[0:1, hh * 128:(hh + 1) * 128],
                                                         rhs=CST[0:1, C_ONES:C_ONES + 1], start=True, stop=True),
                 reads=["AROW", "CST"], writes=[("ps", adab)])
    P.op("dve", lambda e: e.tensor_copy(out=ADA[:], in_=PS[adab][:, 0:72]), reads=[("ps", adab)], writes=["ADA"])
    for (o, gpre, sc) in ((0, V_GPRE1, 8), (16, V_GPRE2, 32), (32, V_GPRE3, 56)):
        P.op("dve", lambda e, o=o, gpre=gpre, sc=sc: e.scalar_tensor_tensor(
            out=DER[:, o:o + 8], in0=ADA[:, sc:sc + 8], scalar=1.0, in1=VEC[:, gpre:gpre + 8],
            op0=ALU.add, op1=ALU.mult), reads=["ADA", "VEC"], writes=["DER"])
    for (o, gpost, gt, f) in ((8, V_GPOST1, 16, 0.5), (24, V_GPOST2, 40, 1.0), (40, V_GPOST3, 64, 0.5)):
        P.op("dve", lambda e, o=o, gpost=gpost, gt=gt, f=f: e.scalar_tensor_tensor(
            out=DER[:, o:o + 8], in0=ADA[:, gt:gt + 8], scalar=f, in1=VEC[:, gpost:gpost + 8],
            op0=ALU.mult, op1=ALU.mult), reads=["ADA", "VEC"], writes=["DER"])
    P.op("dve", lambda e: e.tensor_scalar(out=DER[:, 48:56], in0=VEC[:, V_W0:V_W0 + 8], scalar1=0.5, scalar2=None,
                                          op0=ALU.mult), reads=["VEC"], writes=["DER"])
    P.op("dve", lambda e: e.tensor_scalar(out=DER[:, 56:60], in0=VEC[:, V_KA:V_KA + 4], scalar1=-1.0, scalar2=1.0,
                                          op0=ALU.mult, op1=ALU.add), reads=["VEC"], writes=["DER"])
    OMMU = sb("OMMU", [128, 14])
    P.op("dve", lambda e: e.tensor_scalar(out=OMMU[:], in0=VEC[:, V_MU:V_MU + 14], scalar1=-1.0, scalar2=1.0,
                                          op0=ALU.mult, op1=ALU.add), reads=["VEC"], writes=["OMMU"])

    EPS = sb("EPS", [128, 2])
    P.op("pool", lambda e: e.memset(EPS[:, 0:1], NORM_EPS), writes=["EPS"])
    P.op("pool", lambda e: e.memset(EPS[:, 1:2], GN_EPS), writes=["EPS"])

    def rstd_from_sumsq(b, width, nfeat, eps_col, out_ap, out_key):
        P.op("act", lambda e: e.activation(out=SCR[2][:, 0:width], in_=PS[b][:, 0:width], func=AF.Ln,
                                           scale=1.0 / nfeat, bias=EPS[:, eps_col:eps_col + 1]),
             reads=[("ps", b), "EPS"], writes=[("SCR", 2)])
        P.op("act", lambda e: e.activation(out=out_ap, in_=SCR[2][:, 0:width], func=AF.Exp, scale=-0.5),
             reads=[("SCR", 2)], writes=[out_key])

    def sumsq(src, src_key, n, width, keyf=None):
        b = newbank()
        for kc in range(n):
            s = SCR[kc % 2]
            sk = ("SCR", kc % 2)
            rk = keyf(kc) if keyf is not None else src_key
            P.op("pool", lambda e, s=s, kc=kc: e.tensor_tensor(out=s[:, 0:width], in0=src(kc), in1=src(kc), op=ALU.mult),
                 reads=[rk], writes=[sk])
            P.op("pe", lambda e, s=s, kc=kc, b=b: e.matmul(PS[b][:, 0:width], lhsT=ones, rhs=s[:, 0:width],
                                                         start=(kc == 0), stop=(kc == n - 1)),
                 reads=[sk, "CST"], writes=[("ps", b)], signal=True)
        return b

    def modulate(xt, xkey, width, gs_off, sh_off):
        b = sumsq(lambda kc: xt[:, kc, 0:width], xkey, KC, width)
        rstd_from_sumsq(b, width, D, 0, RSTD[:, 0:width], "RSTD")
        for kc in range(KC):
            s = SCR[kc % 2]
            sk = ("SCR", kc % 2)
            P.op("dve", lambda e, s=s, kc=kc: e.scalar_tensor_tensor(
                out=s[:, 0:width], in0=xt[:, kc, 0:width], scalar=DER[:, gs_off + kc:gs_off + kc + 1],
                in1=RSTD[:, 0:width], op0=ALU.mult, op1=ALU.mult),
                reads=[xkey, "DER", "RSTD"], writes=[sk])
            P.op("pool", lambda e, s=s, kc=kc: e.tensor_scalar(
                out=U[:, kc, 0:width], in0=s[:, 0:width], scalar1=ADA[:, sh_off + kc:sh_off + kc + 1], scalar2=None,
                op0=ALU.add), reads=[sk, "ADA"], writes=["U"])

    def post_residual(xt, xkey, width, gg_off):
        b = sumsq(lambda kc: YSk[kc][:, 0:width], None, KC, width, keyf=lambda kc: ("YS", kc))
        rstd_from_sumsq(b, width, D, 0, RSTD[:, 0:width], "RSTD")
        for kc in range(KC):
            s = SCR[kc % 2]
            sk = ("SCR", kc % 2)
            P.op("dve", lambda e, s=s, kc=kc: e.scalar_tensor_tensor(
                out=s[:, 0:width], in0=YSk[kc][:, 0:width], scalar=DER[:, gg_off + kc:gg_off + kc + 1],
                in1=RSTD[:, 0:width], op0=ALU.mult, op1=ALU.mult),
                reads=[("YS", kc), "DER", "RSTD"], writes=[sk])
            P.op("pool", lambda e, s=s, kc=kc: e.tensor_tensor(
                out=xt[:, kc, 0:width], in0=xt[:, kc, 0:width], in1=s[:, 0:width], op=ALU.add),
                reads=[sk, xkey], writes=[xkey])

    w13_ctr = [36]
    w2_ctr = [0]

    def ffn(name, xt, xkey, width, gs_off, sh_off, gg_off):
        w1, w3, w2 = fw[name]
        w1v = w1.rearrange("(kc p) n -> p kc n", p=128)
        w3v = w3.rearrange("(kc p) n -> p kc n", p=128)
        w2v = w2.rearrange("(j p) n -> p j n", p=128)
        modulate(xt, xkey, width, gs_off, sh_off)
        for j in range(JC):
            s = w13_ctr[0] % NW13
            w13_ctr[0] += 1

            def ld(e, j=j, s=s):
                return [e.dma_start(out=W13[s][:, :, 0:128], in_=w1v[:, :, j * 128:(j + 1) * 128]),
                        e.dma_start(out=W13[s][:, :, 128:256], in_=w3v[:, :, j * 128:(j + 1) * 128])]
            P.dma("pool", ld, writes=[("W13", s)], dsem=f"w13_{s}", nd=2)
            b1 = newbank()
            b3 = newbank()
            for (bb, off) in ((b1, 0), (b3, 128)):
                for kc in range(KC):
                    P.op("pe", lambda e, bb=bb, off=off, kc=kc, s=s: e.matmul(
                        PS[bb][:, 0:width], lhsT=W13[s][:, kc, off:off + 128], rhs=U[:, kc, 0:width],
                        start=(kc == 0), stop=(kc == KC - 1)),
                        reads=[("W13", s), "U"], writes=[("ps", bb)], signal=(kc == KC - 1))
            sc = SCB[j % 2]
            sck = ("SCB", j % 2)
            P.op("act", lambda e, b1=b1, sc=sc: e.activation(out=sc[:, 0:width], in_=PS[b1][:, 0:width], func=AF.Silu),
                 reads=[("ps", b1)], writes=[sck])
            P.op("dve", lambda e, b3=b3, sc=sc, j=j: e.tensor_tensor(out=G[:, j, 0:width], in0=PS[b3][:, 0:width],
                                                                    in1=sc[:, 0:width], op=ALU.mult),
                 reads=[("ps", b3), sck], writes=[("G", j)])
        for m in range(KC):
            s = w2_ctr[0] % 2
            w2_ctr[0] += 1
            P.dma("pool", lambda e, m=m, s=s: e.dma_start(out=W2[s][:], in_=w2v[:, :, m * 128:(m + 1) * 128]),
                  writes=[("W2", s)], dsem=f"w2_{s}")
            b = newbank()
            for j in range(JC):
                P.op("pe", lambda e, b=b, j=j, s=s: e.matmul(PS[b][:, 0:width], lhsT=W2[s][:, j, :], rhs=G[:, j, 0:width],
                                                           start=(j == 0), stop=(j == JC - 1)),
                     reads=[("W2", s), ("G", j)], writes=[("ps", b)], signal=(j == JC - 1))
            P.op("act", lambda e, b=b, m=m: e.copy(out=YSk[m][:, 0:width], in_=PS[b][:, 0:width]),
                 reads=[("ps", b)], writes=[("YS", m)])
        post_residual(xt, xkey, width, gg_off)


    TT = [sb(f"TT{i}", [128, TB]) for i in range(10)]
    CVAL = sb("CVAL", [128, TB])
    CV = sb("CV", [128, 4, TB + 2])
    YC = sb("YC", [128, 4, TB], BF16)
    PRM = [sb(f"PRM{i}", [128, TB + 1]) for i in range(2)]
    CAR = sb("CAR", [128, 14])
    XH = sb("XH", [128, KC, 2])
    VTq = [sb(f"VT{q}", [128, NCH, 128], BF16) for q in range(2)]
    BKq = [sb(f"BK{q}", [128, NCH, 64], BF16) for q in range(2)]
    A4 = [sb(f"A4{q}", [128, NCH, 128], BF16) for q in range(2)]
    RR = [sb(f"RR{i}", [128, NCH, 128], BF16) for i in range(2)]
    XNN = [sb(f"XNN{i}", [128, NCH, 64], BF16) for i in range(2)]
    BSB = [sb(f"BSB{q}", [128, 128], BF16) for q in range(2)]
    HX = [sb(f"HX{hp}", [128, 128]) for hp in range(4)]
    HXB = [sb(f"HXB{hp}", [128, 128], BF16) for hp in range(4)]
    YES = [sb(f"YES{i}", [128, TB]) for i in range(2)]
    IDB = sb("IDB", [128, 128], BF16)
    AR = sb("AR", [128, NCH, 128], BF16)
    KBt = sb("KBt", [128, NCH, 128], BF16)
    GAM = sb("GAM", [128, NCH])
    TWA = sb("TWA", [64, TB], BF16)
    SGB = sb("SGB", [96, TB], BF16)
    EPS2 = sb("EPS2", [128, 1])
    P.op("pool", lambda e: e.memset(EPS2[:], 1e-24), writes=["EPS2"])
    P.op("pool", lambda e: e.tensor_copy(out=IDB[:], in_=ident), reads=["CST"], writes=["IDB"])
    for q in range(2):
        P.op("pool", lambda e, q=q: e.memset(VTq[q][:], 0.0), writes=[("VT", q, 0), ("VT", q, 1)])
    for hp in range(4):
        P.op("pool", lambda e, hp=hp: e.tensor_copy(out=HX[hp][:], in_=CST[:, C_HINIT:C_HINIT + 128]),
             reads=["CST"], writes=[("HX", hp, 0), ("HX", hp, 1)])
        P.op("pool", lambda e, hp=hp: e.tensor_copy(out=HXB[hp][:], in_=CST[:, C_HINIT:C_HINIT + 128]),
             reads=["CST"], writes=[("HXB", hp, 0), ("HXB", hp, 1)])
    mix_banks = [0]

    def mbank():
        b = mix_banks[0] % 6
        mix_banks[0] += 1
        return b

    w_inv = w_in.rearrange("(kc p) n -> p kc n", p=128)
    flagc = VEC[:, V_FLAG:V_FLAG + 1]

    def load_w(cols, srcv=None):
        srcv = w_inv if srcv is None else srcv
        s = w13_ctr[0] % NW13
        w13_ctr[0] += 1
        offs = []
        o = 0
        for (c0, n) in cols:
            offs.append((o, c0, n))
            o += n

        def ld(e):
            return [e.dma_start(out=W13[s][:, :, o:o + n], in_=srcv[:, :, c0:c0 + n]) for (o, c0, n) in offs]
        P.dma("pool", ld, writes=[("W13", s)], dsem=f"w13_{s}", nd=len(offs))
        return s

    def proj(s, off, m, width):
        b = mbank()
        for kc in range(KC):
            P.op("pe", lambda e, b=b, kc=kc: e.matmul(PS[b][0:m, 0:width], lhsT=W13[s][:, kc, off:off + m],
                                                    rhs=U[:, kc, 0:width], start=(kc == 0), stop=(kc == KC - 1)),
                 reads=[("W13", s), "U"], writes=[("ps", b)], signal=(kc == KC - 1))
        return b

    def mix_halo():
        P.dma("sp", lambda e: e.dma_start(out=XH[:], in_=xTv[:, :, 0:2]), writes=["XH"], dsem="xh")
        ffn("ffn1", XH, "XH", 2, 0, 0, 8)
        modulate(XH, "XH", 2, 16, 24)
        for ch in range(4):
            s = load_w([(1024 + ch * 128, 128), (ch * 128, 128)])
            b = proj(s, 0, 128, 2)
            P.op("act", lambda e, b=b: e.copy(out=CVAL[:, 0:2], in_=PS[b][:, 0:2]), reads=[("ps", b)], writes=["CVAL"])
            b = proj(s, 128, 128, 2)
            P.op("dve", lambda e, b=b, ch=ch: e.scalar_tensor_tensor(out=CV[:, ch, TB:TB + 2], in0=PS[b][:, 0:2], scalar=flagc,
                                                                    in1=CVAL[:, 0:2], op0=ALU.mult, op1=ALU.mult),
                 reads=[("ps", b), "CVAL", "VEC"], writes=[("CV", ch)])
        for i in range(14):
            if i < 12:
                s = load_w([(1536 + i * 128, 128)])
                m = 128
            elif i == 12:
                s = load_w([(3072, 64)])
                m = 64
            else:
                s = load_w([(3136, 96)])
                m = 96
            b = proj(s, 0, m, 2)
            P.op("dve", lambda e, b=b, i=i, m=m: e.tensor_scalar(out=CAR[0:m, i:i + 1], in0=PS[b][0:m, 1:2],
                                                                scalar1=VEC[0:m, V_MU + i:V_MU + i + 1], scalar2=flagc[0:m],
                                                                op0=ALU.mult, op1=ALU.mult),
                 reads=[("ps", b), "VEC"], writes=[("CAR", i)])

    prm_ctr = [0]

    def lerp_evac(b, i, m, out_ap, out_key):
        pi = prm_ctr[0] % 2
        prm_ctr[0] += 1
        pr = PRM[pi]
        pk = ("PRM", pi)
        P.op("pool", lambda e: e.tensor_copy(out=pr[0:m, 0:1], in_=CAR[0:m, i:i + 1]), reads=[("CAR", i)], writes=[pk])
        P.op("act", lambda e: e.activation(out=pr[0:m, 1:TB + 1], in_=PS[b][0:m, 0:TB], func=AF.Identity,
                                           scale=VEC[0:m, V_MU + i:V_MU + i + 1]),
             reads=[("ps", b), "VEC"], writes=[pk])
        P.op("dve", lambda e: e.scalar_tensor_tensor(out=out_ap, in0=PS[b][0:m, 0:TB], scalar=OMMU[0:m, i:i + 1],
                                                     in1=pr[0:m, 0:TB], op0=ALU.mult, op1=ALU.add),
             reads=[("ps", b), "OMMU", pk], writes=[out_key])
        P.op("pool", lambda e: e.tensor_copy(out=CAR[0:m, i:i + 1], in_=pr[0:m, TB:TB + 1]), reads=[pk], writes=[("CAR", i)])

    dbg = {}

    def mixer_tile(t, xt, xkey):
        tok = slice(t * TB, (t + 1) * TB)
        modulate(xt, xkey, TB, 16, 24)
        for ch in range(4):
            s = load_w([(1024 + ch * 128, 128), (ch * 128, 128)])
            s2 = load_w([(512 + ch * 128, 128)])
            b = proj(s, 0, 128, TB)
            P.op("act", lambda e, b=b: e.copy(out=CVAL[:], in_=PS[b][:, 0:TB]), reads=[("ps", b)], writes=["CVAL"])
            b = proj(s, 128, 128, TB)
            P.op("pool", lambda e, ch=ch: e.tensor_copy(out=CV[:, ch, 0:2], in_=CV[:, ch, TB:TB + 2]),
                 reads=[("CV", ch)], writes=[("CV", ch)])
            P.op("dve", lambda e, b=b, ch=ch: e.tensor_tensor(out=CV[:, ch, 2:TB + 2], in0=PS[b][:, 0:TB], in1=CVAL[:],
                                                             op=ALU.mult),
                 reads=[("ps", b), "CVAL"], writes=[("CV", ch)])
            acc = SCR[3]
            P.op("pool", lambda e, ch=ch: e.tensor_scalar(out=acc[:], in0=CV[:, ch, 2:TB + 2],
                                                         scalar1=VEC[:, V_CONV + ch * 3 + 2:V_CONV + ch * 3 + 3], scalar2=None,
                                                         op0=ALU.mult),
                 reads=[("CV", ch), "VEC"], writes=[("SCR", 3)])
            for (j, o) in ((1, 1), (0, 0)):
                P.op("dve", lambda e, ch=ch, j=j, o=o: e.scalar_tensor_tensor(
                    out=acc[:], in0=CV[:, ch, o:o + TB], scalar=VEC[:, V_CONV + ch * 3 + j:V_CONV + ch * 3 + j + 1],
                    in1=acc[:], op0=ALU.mult, op1=ALU.add),
                    reads=[("CV", ch), "VEC", ("SCR", 3)], writes=[("SCR", 3)])
            b = proj(s2, 0, 128, TB)
            P.op("dve", lambda e, b=b, ch=ch: e.tensor_tensor(out=YC[:, ch, :], in0=PS[b][:, 0:TB], in1=acc[:], op=ALU.mult),
                 reads=[("ps", b), ("SCR", 3)], writes=[("YC", ch)])
        P.dma("sp", lambda e: e.dma_start(out=yc_scr.rearrange("(c p) n -> p c n", p=128)[:, :, tok], in_=YC[:]),
              reads=[("YC", c) for c in range(4)], writes=[("ycs", t)], dsem="ycs")
        XWA, XG = TT[8], TT[9]
        s = load_w([(3072, 64), (3136, 96)])
        b = proj(s, 0, 64, TB)
        lerp_evac(b, 12, 64, XWA[0:64, :], ("TT", 8))
        b = proj(s, 64, 96, TB)
        lerp_evac(b, 13, 96, XG[0:96, :], ("TT", 9))
        P.op("act", lambda e: e.activation(out=TWA[0:32, :], in_=XWA[0:32, :], func=AF.Tanh), reads=[("TT", 8)], writes=["TWA0"])
        P.op("pool", lambda e: e.tensor_copy(out=TWA[32:64, :], in_=XWA[32:64, :]), reads=[("TT", 8)], writes=["TWA1"])
        P.op("act", lambda e: e.activation(out=XG[0:96, :], in_=XG[0:96, :], func=AF.Tanh, scale=0.5),
             reads=[("TT", 9)], writes=[("TT", 9)])
        P.op("pool", lambda e: e.tensor_scalar(out=SGB[:], in0=XG[0:96, :], scalar1=0.5, scalar2=0.5, op0=ALU.mult, op1=ALU.add),
             reads=[("TT", 9)], writes=["SGB"])
        def hp_body(hp):
            hcol = slice(hp * 128, (hp + 1) * 128)
            XR, XK, XV = TT[0], TT[1], TT[2]
            s = load_w([(1536 + hp * 128, 128), (2048 + hp * 128, 128)])
            s2 = load_w([(2560 + hp * 128, 128)])
            b = proj(s, 0, 128, TB)
            lerp_evac(b, hp, 128, XR[:], ("TT", 0))
            b = proj(s, 128, 128, TB)
            lerp_evac(b, 4 + hp, 128, XK[:], ("TT", 1))
            b = proj(s2, 0, 128, TB)
            lerp_evac(b, 8 + hp, 128, XV[:], ("TT", 2))
            SGW, AA, CS = YSk[0], YSk[1], YSk[2]
            bw = mbank()
            P.op("pe", lambda e, bw=bw: e.matmul(PS[bw][:, 0:TB], lhsT=LWA[0:32, hcol], rhs=TWA[0:32, :], start=True, stop=True),
                 reads=["LWA", "TWA0"], writes=[("ps", bw)])
            P.op("act", lambda e, bw=bw, hp=hp: e.activation(out=SGW[:], in_=PS[bw][:, 0:TB], func=AF.Tanh, scale=0.5,
                                                             bias=DER[:, 48 + hp:49 + hp]),
                 reads=[("ps", bw), "DER"], writes=[("YS", 0)])
            P.op("pool", lambda e: e.tensor_scalar(out=SGW[:], in0=SGW[:], scalar1=0.5, scalar2=0.5, op0=ALU.mult, op1=ALU.add),
                 reads=[("YS", 0)], writes=[("YS", 0)])
            ba = mbank()
            P.op("pe", lambda e, ba=ba: e.matmul(PS[ba][:, 0:TB], lhsT=LWA[32:64, hcol], rhs=TWA[32:64, :], start=True, stop=True),
                 reads=["LWA", "TWA1"], writes=[("ps", ba)])
            P.op("act", lambda e, ba=ba, hp=hp: e.activation(out=AA[:], in_=PS[ba][:, 0:TB], func=AF.Tanh, scale=0.5,
                                                             bias=DER[:, 52 + hp:53 + hp]),
                 reads=[("ps", ba), "DER"], writes=[("YS", 1)])
            P.op("pool", lambda e: e.tensor_scalar(out=AA[:], in0=AA[:], scalar1=0.5, scalar2=0.5, op0=ALU.mult, op1=ALU.add),
                 reads=[("YS", 1)], writes=[("YS", 1)])
            bg = mbank()
            P.op("pe", lambda e, bg=bg: e.matmul(PS[bg][:, 0:TB], lhsT=LG[0:96, hcol], rhs=SGB[0:96, :], start=True, stop=True),
                 reads=["LG", "SGB"], writes=[("ps", bg)])
            GT = TT[3]
            P.op("act", lambda e, bg=bg: e.copy(out=GT[:], in_=PS[bg][:, 0:TB]), reads=[("ps", bg)], writes=[("TT", 3)])
            P.dma("sp", lambda e, hp=hp: e.dma_start(out=g_scr[hp * 128:(hp + 1) * 128, tok], in_=GT[:]),
                  reads=[("TT", 3)], writes=[("gs", hp, t)], dsem="gts")
            P.op("dve", lambda e: e.tensor_tensor_scan(out=CS[:], data0=CST[:, C_RESET:C_RESET + TB], data1=SGW[:], initial=0.0,
                                                       op0=ALU.mult, op1=ALU.add),
                 reads=["CST", ("YS", 0)], writes=[("YS", 2)])
            E1, E2, E3, E4 = YSk[3], YSk[4], YSk[5], YSk[6]
            P.op("act", lambda e: e.activation(out=E1[:], in_=CS[:], func=AF.Exp, scale=-C0), reads=[("YS", 2)], writes=[("YS", 3)])
            P.op("act", lambda e: e.activation(out=E3[:], in_=CS[:], func=AF.Exp, scale=C0), reads=[("YS", 2)], writes=[("YS", 5)])
            P.op("pool", lambda e: e.tensor_tensor(out=E2[:], in0=CS[:], in1=SGW[:], op=ALU.subtract),
                 reads=[("YS", 2), ("YS", 0)], writes=[("YS", 4)])
            P.op("act", lambda e: e.activation(out=E2[:], in_=E2[:], func=AF.Exp, scale=-C0), reads=[("YS", 4)], writes=[("YS", 4)])
            cs3 = CS[:].rearrange("p (c t) -> p c t", t=CH)
            P.op("dve", lambda e: e.tensor_tensor(out=E4[:].rearrange("p (c t) -> p c t", t=CH),
                                                  in0=cs3[:, :, CH - 1:CH].to_broadcast([128, NCH, CH]), in1=cs3, op=ALU.subtract),
                 reads=[("YS", 2)], writes=[("YS", 6)])
            P.op("act", lambda e: e.activation(out=E4[:], in_=E4[:], func=AF.Exp, scale=-C0), reads=[("YS", 6)], writes=[("YS", 6)])
            P.op("act", lambda e: e.activation(out=GAM[:].unsqueeze(2), in_=cs3[:, :, CH - 1:CH], func=AF.Exp, scale=-C0),
                 reads=[("YS", 2)], writes=["GAM"])
            KKr, KK_, T1 = YSk[7], TT[4], TT[5]
            P.op("pool", lambda e, hp=hp: e.tensor_scalar(out=KKr[:], in0=XK[:], scalar1=VEC[:, V_KK + hp:V_KK + hp + 1], scalar2=None,
                                                         op0=ALU.mult), reads=[("TT", 1), "VEC"], writes=[("YS", 7)])
            P.op("pool", lambda e: e.tensor_tensor(out=T1[:], in0=KKr[:], in1=KKr[:], op=ALU.mult),
                 reads=[("YS", 7)], writes=[("TT", 5)])
            bn = mbank()
            P.op("pe", lambda e, bn=bn: e.matmul(PS[bn][:, 0:TB], lhsT=bones, rhs=T1[:], start=True, stop=True),
                 reads=["CST", ("TT", 5)], writes=[("ps", bn)])
            P.op("act", lambda e, bn=bn: e.activation(out=T1[:], in_=PS[bn][:, 0:TB], func=AF.Ln, bias=EPS2[:]),
                 reads=[("ps", bn), "EPS2"], writes=[("TT", 5)])
            P.op("act", lambda e: e.activation(out=T1[:], in_=T1[:], func=AF.Exp, scale=-0.5), reads=[("TT", 5)], writes=[("TT", 5)])
            P.op("dve", lambda e: e.tensor_tensor(out=KK_[:], in0=KKr[:], in1=T1[:], op=ALU.mult),
                 reads=[("YS", 7), ("TT", 5)], writes=[("TT", 4)])
            BETA, KMOD = TT[6], TT[7]
            P.op("dve", lambda e: e.tensor_tensor(out=BETA[:], in0=KK_[:], in1=AA[:], op=ALU.mult),
                 reads=[("TT", 4), ("YS", 1)], writes=[("TT", 6)])
            P.op("dve", lambda e, hp=hp: e.tensor_scalar(out=T1[:], in0=AA[:], scalar1=VEC[:, V_KA + hp:V_KA + hp + 1],
                                                        scalar2=DER[:, 56 + hp:57 + hp], op0=ALU.mult, op1=ALU.add),
                 reads=[("YS", 1), "VEC", "DER", ("TT", 5)], writes=[("TT", 5)])
            P.op("pool", lambda e: e.tensor_tensor(out=KMOD[:], in0=XK[:], in1=T1[:], op=ALU.mult),
                 reads=[("TT", 1), ("TT", 5)], writes=[("TT", 7)])
            P.op("dve", lambda e, hp=hp: e.scalar_tensor_tensor(out=T1[:], in0=XR[:], scalar=VEC[:, V_RK + hp:V_RK + hp + 1],
                                                               in1=KMOD[:], op0=ALU.mult, op1=ALU.mult),
                 reads=[("TT", 0), ("TT", 7), "VEC", ("TT", 5)], writes=[("TT", 5)])
            bb = mbank()
            P.op("pe", lambda e, bb=bb: e.matmul(PS[bb][:, 0:TB], lhsT=bones, rhs=T1[:], start=True, stop=True),
                 reads=["CST", ("TT", 5)], writes=[("ps", bb)])
            BON = KKr
            P.op("dve", lambda e, bb=bb: e.tensor_tensor(out=BON[:], in0=PS[bb][:, 0:TB], in1=XV[:], op=ALU.mult),
                 reads=[("ps", bb), ("TT", 2), ("YS", 7)], writes=[("YS", 7)])
            P.op("pool", lambda e, hp=hp: e.tensor_scalar(out=BON[:], in0=BON[:], scalar1=VEC[:, V_LNB + hp:V_LNB + hp + 1], scalar2=None,
                                                         op0=ALU.add), reads=[("YS", 7), "VEC"], writes=[("YS", 7)])
            P.dma("sp", lambda e, hp=hp: e.dma_start(out=bo_scr[hp * 128:(hp + 1) * 128, tok], in_=BON[:]),
                  reads=[("YS", 7)], writes=[("bs", hp, t)], dsem="bos")
            v3 = lambda ap: ap.rearrange("p (c t) -> p c t", t=CH)
            BHb, KHb, XVb = G[:, 4, :], G[:, 5, :], G[:, 6, :]
            gk = lambda j: ("G", j)
            P.op("dve", lambda e: e.scalar_tensor_tensor(out=AR[:, :, 0:64], in0=v3(KK_[:]), scalar=-1.0, in1=v3(E2[:]), op0=ALU.mult, op1=ALU.mult),
                 reads=[("TT", 4), ("YS", 4)], writes=[gk(0)])
            P.op("pool", lambda e: e.tensor_tensor(out=AR[:, :, 64:128], in0=v3(XR[:]), in1=v3(E1[:]), op=ALU.mult),
                 reads=[("TT", 0), ("YS", 3)], writes=[gk(1)])
            for q in range(2):
                rows = slice(64 * q, 64 * q + 64)
                ks, bs = (0, 1) if q == 0 else (1, 0)
                P.op("dve", lambda e, rows=rows, ks=ks: e.tensor_tensor(out=KBt[rows, :, ks * 64:ks * 64 + 64], in0=v3(KMOD[rows, :]), in1=v3(E3[rows, :]), op=ALU.mult),
                     reads=[("TT", 7), ("YS", 5)], writes=[gk(2 + ks)])
                P.op("pool", lambda e, rows=rows, bs=bs: e.tensor_tensor(out=KBt[rows, :, bs * 64:bs * 64 + 64], in0=v3(BETA[rows, :]), in1=v3(E3[rows, :]), op=ALU.mult),
                     reads=[("TT", 6), ("YS", 5)], writes=[gk(2 + bs)])
            P.op("dve", lambda e: e.tensor_tensor(out=BHb, in0=BETA[:], in1=E4[:], op=ALU.mult),
                 reads=[("TT", 6), ("YS", 6)], writes=[gk(4)])
            P.op("pool", lambda e: e.tensor_tensor(out=KHb, in0=KMOD[:], in1=E4[:], op=ALU.mult),
                 reads=[("TT", 7), ("YS", 6)], writes=[gk(5)])
            P.op("pool", lambda e: e.tensor_copy(out=XVb, in_=XV[:]), reads=[("TT", 2)], writes=[gk(6)])
            if t == 0 and hp == 0:
                dbg.update(XR=(XR, ("TT", 0)), XK=(XK, ("TT", 1)), XV=(XV, ("TT", 2)), AA=(AA, ("YS", 1)),
                           SGW=(SGW, ("YS", 0)), KK=(KK_, ("TT", 4)), CS=(CS, ("YS", 2)))
            def head_setup(q):
                PK, PB = 64 * q, 64 * (1 - q)
                rk = slice(PK, PK + 64)
                rb = slice(PB, PB + 64)
                vcol = slice(0, 64) if q == 0 else slice(64, 128)
                sbeta = 1 if q == 0 else 0
                cc = lambda c: slice(c * CH, (c + 1) * CH)
                bv = mbank()
                for c in range(NCH):
                    P.op("pe", lambda e, c=c: e.matmul(PS[bv][rk, cc(c)], lhsT=XVb[rk, cc(c)], rhs=IDB[rk, rk], start=True, stop=True),
                         reads=[gk(6), "IDB"], writes=[("ps", bv)], signal=(c == NCH - 1))
                P.op("act", lambda e: e.copy(out=VTq[q][rk, :, vcol], in_=PS[bv][rk, 0:TB].rearrange("p (c t) -> p c t", t=CH)),
                     reads=[("ps", bv)], writes=[("VT", q, q)])
                bk = mbank()
                for c in range(NCH):
                    P.op("pe", lambda e, c=c: e.matmul(PS[bk][rb, cc(c)], lhsT=BHb[rk, cc(c)], rhs=IDB[rk, rk], start=True, stop=True),
                         reads=[gk(4), "IDB"], writes=[("ps", bk)], signal=False)
                    P.op("pe", lambda e, c=c: e.matmul(PS[bk][rk, cc(c)], lhsT=KHb[rk, cc(c)], rhs=IDB[rk, rk], start=True, stop=True),
                         reads=[gk(5), "IDB"], writes=[("ps", bk)], signal=(c == NCH - 1))
                P.op("dve", lambda e: e.tensor_copy(out=BKq[q][:], in_=PS[bk][:, 0:TB].rearrange("p (c t) -> p c t", t=CH)),
                     reads=[("ps", bk)], writes=[("BK", q)])
                for c4 in range(2):
                    b4 = mbank()
                    for ci in range(4):
                        c = c4 * 4 + ci
                        P.op("pe", lambda e, c=c, ci=ci, b4=b4: e.matmul(PS[b4][:, ci * 128:(ci + 1) * 128], lhsT=KBt[rk, c, :],
                                                                        rhs=AR[rk, c, :], start=True, stop=True),
                             reads=[gk(0), gk(1), gk(2), gk(3)], writes=[("ps", b4)], signal=(ci == 3))
                    P.op("dve", lambda e, c4=c4, b4=b4: e.tensor_tensor(
                        out=A4[q][:, c4 * 4:(c4 + 1) * 4, :], in0=PS[b4][:, 0:512].rearrange("p (c t) -> p c t", t=128),
                        in1=CST[:, C_MASK4:C_MASK4 + 128].unsqueeze(1).to_broadcast([128, 4, 128]), op=ALU.mult),
                        reads=[("ps", b4), "CST"], writes=[("A4", q)])
                bl = mbank()
                for c in range(NCH):
                    P.op("pe", lambda e, c=c: e.matmul(PS[bl][rb, cc(c)], lhsT=AR[rk, c, 0:64], rhs=KBt[rk, c, sbeta * 64:sbeta * 64 + 64],
                                                      start=True, stop=True),
                         reads=[gk(0), gk(2 + sbeta)], writes=[("ps", bl)], signal=(c == NCH - 1))
                P.op("dve", lambda e: e.tensor_tensor(out=XNN[0][rb, :, :], in0=PS[bl][rb, 0:TB].rearrange("p (c t) -> p c t", t=CH),
                                                      in1=CST[rb, C_MASKL:C_MASKL + 64].unsqueeze(1).to_broadcast([64, NCH, CH]), op=ALU.mult),
                     reads=[("ps", bl), "CST"], writes=[("XNN", 0, q)])
                ba0, bb0 = mbank(), mbank()
                for c in range(NCH):
                    P.op("pe", lambda e, c=c: e.matmul(PS[ba0][rb, cc(c)], lhsT=XNN[0][rb, c, :], rhs=A4[q][rb, c, 0:64], start=True, stop=True),
                         reads=[("XNN", 0, q), ("A4", q)], writes=[("ps", ba0)], signal=(c == NCH - 1))
                for c in range(NCH):
                    P.op("pe", lambda e, c=c: e.matmul(PS[bb0][rb, cc(c)], lhsT=A4[q][rb, c, 0:64], rhs=XNN[0][rb, c, :], start=True, stop=True),
                         reads=[("XNN", 0, q), ("A4", q)], writes=[("ps", bb0)], signal=(c == NCH - 1))
                P.op("act", lambda e: e.copy(out=RR[1][rb, :, 64:128], in_=PS[ba0][rb, 0:TB].rearrange("p (c t) -> p c t", t=CH)),
                     reads=[("ps", ba0)], writes=[("RR", 1, q)])
                P.op("dve", lambda e: e.tensor_copy(out=XNN[1][rb, :, :], in_=PS[bb0][rb, 0:TB].rearrange("p (c t) -> p c t", t=CH)),
                     reads=[("ps", bb0)], writes=[("XNN", 1, q)])
                P.op("pool", lambda e: e.tensor_tensor(out=RR[1][rb, :, 0:64], in0=A4[q][rb, :, 0:64],
                                                       in1=IDB[rb, rb].unsqueeze(1).to_broadcast([64, NCH, CH]), op=ALU.add),
                     reads=[("A4", q), "IDB"], writes=[("RR", 1, q)])
                def level(lvl):
                    cur, nxt = lvl % 2, (lvl + 1) % 2
                    bas = [mbank(), mbank()]
                    bbn = mbank()
                    for c in range(NCH):
                        P.op("pe", lambda e, c=c: e.matmul(PS[bas[c // 4]][rb, (c % 4) * 128:(c % 4 + 1) * 128], lhsT=XNN[cur][rb, c, :],
                                                          rhs=RR[cur][rb, c, :], start=True, stop=True),
                             reads=[("XNN", cur, q), ("RR", cur, q)], writes=[("ps", bas[c // 4])], signal=(c % 4 == 3))
                    for c in range(NCH):
                        P.op("pe", lambda e, c=c: e.matmul(PS[bbn][rb, cc(c)], lhsT=RR[cur][rb, c, 64:128], rhs=XNN[cur][rb, c, :],
                                                          start=True, stop=True),
                             reads=[("XNN", cur, q), ("RR", cur, q)], writes=[("ps", bbn)], signal=(c == NCH - 1))
                    for hf in range(2):
                        pv = PS[bas[hf]][rb, 0:512].rearrange("p (c t) -> p c t", t=128)
                        P.op("dve", lambda e, hf=hf, pv=pv: e.tensor_tensor(out=RR[nxt][rb, hf * 4:(hf + 1) * 4, 0:64], in0=pv[:, :, 0:64],
                                                                           in1=RR[cur][rb, hf * 4:(hf + 1) * 4, 0:64], op=ALU.add),
                             reads=[("ps", bas[hf]), ("RR", cur, q)], writes=[("RR", nxt, q)])
                        P.op("act", lambda e, hf=hf, pv=pv: e.copy(out=RR[nxt][rb, hf * 4:(hf + 1) * 4, 64:128], in_=pv[:, :, 64:128]),
                             reads=[("ps", bas[hf])], writes=[("RR", nxt, q)])
                    P.op("act", lambda e: e.copy(out=XNN[nxt][rb, :, :], in_=PS[bbn][rb, 0:TB].rearrange("p (c t) -> p c t", t=CH)),
                         reads=[("ps", bbn)], writes=[("XNN", nxt, q)])
                for lvl in range(1, 5):
                    level(lvl)
                bf = mbank()
                for c in range(NCH):
                    P.op("pe", lambda e, c=c: e.matmul(PS[bf][rb, cc(c)], lhsT=XNN[1][rb, c, :], rhs=RR[1][rb, c, 0:64], start=True, stop=True),
                         reads=[("XNN", 1, q), ("RR", 1, q)], writes=[("ps", bf)], signal=(c == NCH - 1))
                P.op("dve", lambda e: e.tensor_tensor(out=RR[0][rb, :, 0:64], in0=PS[bf][rb, 0:TB].rearrange("p (c t) -> p c t", t=CH),
                                                      in1=RR[1][rb, :, 0:64], op=ALU.add),
                     reads=[("ps", bf), ("RR", 1, q)], writes=[("RR", 0, q)])
            for q in range(2):
                head_setup(q)

            def chain(c, q):
                if True:
                    PK, PB = 64 * q, 64 * (1 - q)
                    rk = slice(PK, PK + 64)
                    rb = slice(PB, PB + 64)
                    cs_ = slice(c * CH, (c + 1) * CH)
                    bc = mbank()
                    by = 6 + q
                    hxk, hxbk = ("HX", hp, q), ("HXB", hp, q)
                    P.op("pe", lambda e: e.matmul(PS[bc][rb, 0:128], lhsT=AR[rk, c, 0:64], rhs=HXB[hp][rk, :], start=True, stop=False),
                         reads=[gk(0), hxbk], writes=[("ps", bc)], signal=False)
                    P.op("pe", lambda e: e.matmul(PS[bc][rb, 0:128], lhsT=A4[q][rk, c, 0:64], rhs=VTq[q][rk, c, :], start=False, stop=True),
                         reads=[("A4", q), ("VT", q, q)], writes=[("ps", bc)])
                    P.op("act", lambda e: e.copy(out=BSB[q][rb, :], in_=PS[bc][rb, 0:128]), reads=[("ps", bc)], writes=[("BSB", q)])
                    P.op("pe", lambda e: e.matmul(PS[bc][rb, 128:256], lhsT=RR[0][rb, c, 0:64], rhs=BSB[q][rb, :], start=True, stop=True),
                         reads=[("RR", 0, q), ("BSB", q)], writes=[("ps", bc)])
                    P.op("act", lambda e: e.copy(out=VTq[q][rb, c, :], in_=PS[bc][rb, 128:256]), reads=[("ps", bc)], writes=[("VT", q, 1 - q)])
                    P.op("pe", lambda e: e.matmul(PS[by][:, cs_], lhsT=HXB[hp][rk, :], rhs=AR[rk, c, 64:128], start=True, stop=False),
                         reads=[gk(1), hxbk], writes=[("ps", by)], signal=False)
                    P.op("pe", lambda e: e.matmul(PS[by][:, cs_], lhsT=VTq[q][:, c, :], rhs=A4[q][:, c, 64:128], start=False, stop=True),
                         reads=[("A4", q), ("VT", q, 0), ("VT", q, 1)], writes=[("ps", by)])
                    P.op("pe", lambda e: e.matmul(PS[bc][rk, 256:384], lhsT=BKq[q][:, c, :], rhs=VTq[q][:, c, :], start=True, stop=True),
                         reads=[("BK", q), ("VT", q, 0), ("VT", q, 1)], writes=[("ps", bc)])
                    P.op("dve", lambda e: e.scalar_tensor_tensor(out=HX[hp][rk, :], in0=HX[hp][rk, :], scalar=GAM[rk, c:c + 1],
                                                                 in1=PS[bc][rk, 256:384], op0=ALU.mult, op1=ALU.add),
                         reads=[hxk, "GAM", ("ps", bc)], writes=[hxk])
                    P.op("pool", lambda e: e.tensor_copy(out=HXB[hp][rk, :], in_=HX[hp][rk, :]), reads=[hxk], writes=[hxbk])
            for c in range(NCH):
                for q in range(2):
                    chain(c, q)
            for q in range(2):
                h = 2 * hp + q
                by = 6 + q
                ye = YES[q]
                P.op("act", lambda e, by=by, ye=ye: e.copy(out=ye[:], in_=PS[by][:, 0:TB]), reads=[("ps", by)], writes=[("YES", q)])
                P.dma("sp", lambda e, h=h, ye=ye: e.dma_start(out=ye_scr[h * 128:(h + 1) * 128, tok], in_=ye[:]),
                      reads=[("YES", q)], writes=[("yes", h, t)], dsem=f"yes{q}")
                if t == 0 and hp == 0:
                    dbg[f"YE{q}"] = (ye, ("YES", q))

        for hp in range(4):
            hp_body(hp)

    if STAGE >= 2:
        mix_halo()
    for t in range(NT):
        xt = XT[0]
        xkey = ("XT", 0)
        P.dma("sp", lambda e, t=t, xt=xt: e.dma_start(out=xt[:], in_=xTv[:, :, 2 + t * TB:2 + (t + 1) * TB]),
              writes=[xkey], dsem="xt0")
        ffn("ffn1", xt, xkey, TB, 0, 0, 8)
        if STAGE == 1:
            P.dma("sp", lambda e, t=t, xt=xt: e.dma_start(out=outTv[:, :, t * TB:(t + 1) * TB], in_=xt[:]),
                  reads=[xkey], dsem="xo0")
            continue
        P.dma("sp", lambda e, t=t, xt=xt: e.dma_start(out=h_scrv[:, :, t * TB:(t + 1) * TB], in_=xt[:]),
              reads=[xkey], writes=[("hs", t)], dsem="hs")
        mixer_tile(t, xt, xkey)
        if DEBUG and t == 0:
            for name, (tile_, key) in dbg.items():
                dd = dt("dbg_" + name, [128, TB], F32, kind="ExternalOutput").ap()
                P.dma("sp", lambda e, dd=dd, tile_=tile_: e.dma_start(out=dd[:, :], in_=tile_[:]), reads=[key], dsem="dbg_" + name)
            dd = dt("dbg_HX", [128, 128], F32, kind="ExternalOutput").ap()
            P.dma("sp", lambda e, dd=dd: e.dma_start(out=dd[:, :], in_=HX[0][:]), reads=[("HX", 0, 0), ("HX", 0, 1)], dsem="dbg_HX")
        if DEBUG and t == 0 and STAGE == 2:
            break


    FX = sb("FX", [128, 8, 128])
    P.op("pool", lambda e: e.memset(FX[:], 0.0), writes=["FX"])
    for q in range(2):
        rk = slice(64 * q, 64 * q + 64)
        P.op("pool", lambda e, q=q, rk=rk: e.tensor_copy(
            out=FX[rk, q::2, 64 * q:64 * q + 64], in_=ident[rk, rk].unsqueeze(1).to_broadcast([64, 4, 64])),
            reads=["CST", "FX"], writes=["FX"])
    if STAGE >= 3 and (NRUN == NCORES or FAKE_CC):
        for hp in range(4):
            P.dma("sp", lambda e, hp=hp: e.dma_start(out=hx_loc[:, hp * 128:(hp + 1) * 128], in_=HX[hp][:]),
                  reads=[("HX", hp, 0), ("HX", hp, 1)], writes=["hx_loc"], dsem="hxl")
        ccsem = st.enter_context(nc.semaphore("ccsem"))
        ccd = sb("ccdummy", [128, 2])

        def cc(e):
            ins = e.collective_compute("AllGather", ALU.bypass, replica_groups=[list(range(NCORES))],
                                       ins=[hx_loc.tensor.ap().opt()], outs=[hx_all.tensor.ap().opt()])
            ins.then_inc(ccsem)
            e.wait_ge(ccsem, 1)
            return e.memset(ccd[:], 0.0)
        if FAKE_CC:
            P.dma("sp", lambda e: [e.dma_start(out=hx_all[j * 128:(j + 1) * 128, :], in_=hx_loc[:, :]) for j in range(NCORES)],
                  reads=["hx_loc"], writes=["hx_all"], dsem="fcc", nd=NCORES)
        else:
            P.op("pool", cc, reads=["hx_loc"], writes=["hx_all"])

        def ld_hxa(e):
            r = []
            for j in range(NCORES):
                r.append(e.dma_start(out=YSk[j][0:64, :], in_=hx_all[j * 128 + 64:j * 128 + 128, :]))
                r.append(e.dma_start(out=YSk[j][64:128, :], in_=hx_all[j * 128:j * 128 + 64, :]))
            return r
        P.dma("sp", ld_hxa, reads=["hx_all"], writes=[("YS", j) for j in range(KC)], dsem="hxa", nd=16)
        HC, SS = TT[0], TT[1]
        HCB = G[:, 20, 0:256]
        P.op("pool", lambda e: e.memset(SS[:, 0:256], 0.0), writes=[("TT", 1)])
        for j in range(NCORES - 1):
            P.op("pool", lambda e, j=j: e.tensor_copy(out=G[:, 8 + j, :], in_=YSk[j][:]), reads=[("YS", j)], writes=[("G", 8 + j)])
            if j > 0:
                pt = G[:, 16 + j // 2, (j % 2) * 256:(j % 2) * 256 + 256]
                for q in range(2):
                    rb = slice(64 * (1 - q), 64 * (1 - q) + 64)
                    kcol = 64 * (1 - q)
                    bq = newbank()
                    for hp in range(4):
                        P.op("pe", lambda e, j=j, hp=hp, rb=rb, kcol=kcol, bq=bq: e.matmul(
                            PS[bq][rb, hp * 64:(hp + 1) * 64], lhsT=G[rb, 8 + j, hp * 128 + kcol:hp * 128 + kcol + 64],
                            rhs=IDB[rb, rb], start=True, stop=True),
                            reads=[("G", 8 + j), "IDB"], writes=[("ps", bq)], signal=(hp == 3))
                    P.op("act", lambda e, rb=rb, bq=bq, pt=pt: e.copy(out=pt[rb, :], in_=PS[bq][rb, 0:256]),
                         reads=[("ps", bq)], writes=[("G", 16 + j // 2, 1 - q)])
            for q in range(2):
                rb = slice(64 * (1 - q), 64 * (1 - q) + 64)
                hl = YSk[j][rb, :].rearrange("p (h c) -> p h c", c=128)[:, :, 64 * q:64 * q + 64]
                hc3 = HC[rb, 0:256].rearrange("p (h c) -> p h c", c=64)
                if j == 0:
                    P.op("dve", lambda e, hl=hl, hc3=hc3: e.tensor_copy(out=hc3, in_=hl), reads=[("YS", j)], writes=[("TT", 0, 1 - q)])
                else:
                    pt = G[:, 16 + j // 2, (j % 2) * 256:(j % 2) * 256 + 256]
                    bq = newbank()
                    for hp in range(4):
                        P.op("pe", lambda e, hp=hp, rb=rb, bq=bq, pt=pt: e.matmul(
                            PS[bq][rb, hp * 64:(hp + 1) * 64], lhsT=pt[rb, hp * 64:(hp + 1) * 64],
                            rhs=HCB[rb, hp * 64:(hp + 1) * 64], start=True, stop=True),
                            reads=[("G", 16 + j // 2, 1 - q), ("G", 20, 1 - q)], writes=[("ps", bq)], signal=(hp == 3))
                    P.op("dve", lambda e, rb=rb, bq=bq, hl=hl, hc3=hc3: e.tensor_tensor(
                        out=hc3, in0=PS[bq][rb, 0:256].rearrange("p (h c) -> p h c", c=64), in1=hl, op=ALU.add),
                        reads=[("ps", bq), ("YS", j)], writes=[("TT", 0, 1 - q)])
                P.op("pool", lambda e, rb=rb: e.tensor_copy(out=HCB[rb, :], in_=HC[rb, 0:256]),
                     reads=[("TT", 0, 1 - q)], writes=[("G", 20, 1 - q)])
                P.op("dve", lambda e, rb=rb, j=j: e.scalar_tensor_tensor(
                    out=SS[rb, 0:256], in0=HC[rb, 0:256], scalar=VEC[rb, V_ONEHOT + j + 1:V_ONEHOT + j + 2],
                    in1=SS[rb, 0:256], op0=ALU.mult, op1=ALU.add),
                    reads=[("TT", 0, 1 - q), "VEC", ("TT", 1)], writes=[("TT", 1)])
        for q in range(2):
            rb = slice(64 * (1 - q), 64 * (1 - q) + 64)
            P.op("dve", lambda e, q=q, rb=rb: e.tensor_copy(
                out=FX[rb, q::2, 64 * q:64 * q + 64], in_=SS[rb, 0:256].rearrange("p (h c) -> p h c", c=64)),
                reads=[("TT", 1), "FX"], writes=["FX"])

    w_outv = w_out.rearrange("(kc p) n -> p kc n", p=128)
    yc_v = yc_scr.rearrange("(c p) n -> p c n", p=128)

    def phaseB_tile(t):
        tok = slice(t * TB, (t + 1) * TB)
        xt, xkey = XT[0], ("XT", 0)
        P.dma("sp", lambda e: e.dma_start(out=xt[:], in_=h_scrv[:, :, tok]), reads=[("hs", t)], writes=[xkey], dsem="xt0")
        P.dma("sp", lambda e: e.dma_start(out=YC[:], in_=yc_v[:, :, tok]), reads=[("ycs", t)], writes=[("YC", c) for c in range(4)], dsem="ycl")

        def hpB(hp):
            Y0, Y1, GT, BON = TT[4], TT[5], TT[6], TT[7]

            def ld(e):
                return [e.dma_start(out=Y0[:], in_=ye_scr[(2 * hp) * 128:(2 * hp + 1) * 128, tok]),
                        e.dma_start(out=Y1[:], in_=ye_scr[(2 * hp + 1) * 128:(2 * hp + 2) * 128, tok]),
                        e.dma_start(out=GT[:], in_=g_scr[hp * 128:(hp + 1) * 128, tok]),
                        e.dma_start(out=BON[:], in_=bo_scr[hp * 128:(hp + 1) * 128, tok])]
            P.dma("sp", ld, reads=[("yes", 2 * hp, t), ("yes", 2 * hp + 1, t), ("gs", hp, t), ("bs", hp, t)],
                  writes=[("TT", 4), ("TT", 5), ("TT", 6), ("TT", 7)], dsem="yel", nd=4)
            bo = newbank()
            P.op("pe", lambda e: e.matmul(PS[bo][:, 0:TB], lhsT=FX[:, 2 * hp, :], rhs=Y0[:], start=True, stop=False),
                 reads=["FX", ("TT", 4)], writes=[("ps", bo)], signal=False)
            P.op("pe", lambda e: e.matmul(PS[bo][:, 0:TB], lhsT=FX[:, 2 * hp + 1, :], rhs=Y1[:], start=False, stop=True),
                 reads=["FX", ("TT", 5)], writes=[("ps", bo)])
            OS, DD, SQ, RS = YSk[0], YSk[1], YSk[2], YSk[3]
            P.op("act", lambda e: e.copy(out=OS[:], in_=PS[bo][:, 0:TB]), reads=[("ps", bo)], writes=[("YS", 0)])
            bm = newbank()
            P.op("pe", lambda e: e.matmul(PS[bm][:, 0:TB], lhsT=bones, rhs=OS[:], start=True, stop=True),
                 reads=["CST", ("YS", 0)], writes=[("ps", bm)])
            P.op("dve", lambda e: e.scalar_tensor_tensor(out=DD[:], in0=PS[bm][:, 0:TB], scalar=-1.0 / 64, in1=OS[:],
                                                         op0=ALU.mult, op1=ALU.add),
                 reads=[("ps", bm), ("YS", 0)], writes=[("YS", 1)])
            P.op("pool", lambda e: e.tensor_tensor(out=SQ[:], in0=DD[:], in1=DD[:], op=ALU.mult), reads=[("YS", 1)], writes=[("YS", 2)])
            bv = newbank()
            P.op("pe", lambda e: e.matmul(PS[bv][:, 0:TB], lhsT=bones, rhs=SQ[:], start=True, stop=True),
                 reads=["CST", ("YS", 2)], writes=[("ps", bv)])
            rstd_from_sumsq(bv, TB, 64, 1, RS[:], ("YS", 3))
            P.op("dve", lambda e: e.tensor_tensor(out=DD[:], in0=DD[:], in1=RS[:], op=ALU.mult),
                 reads=[("YS", 1), ("YS", 3)], writes=[("YS", 1)])
            P.op("dve", lambda e: e.scalar_tensor_tensor(out=SQ[:], in0=DD[:], scalar=VEC[:, V_LNW + hp:V_LNW + hp + 1], in1=BON[:],
                                                         op0=ALU.mult, op1=ALU.add),
                 reads=[("YS", 1), "VEC", ("TT", 7), ("YS", 2)], writes=[("YS", 2)])
            P.op("pool", lambda e: e.tensor_tensor(out=G[:, hp, :], in0=SQ[:], in1=GT[:], op=ALU.mult),
                 reads=[("YS", 2), ("TT", 6)], writes=[("G", hp)])
        for hp in range(4):
            hpB(hp)

        def wo(n):
            s = load_w([(n * 256, 256)], w_outv)
            for mm_ in range(2):
                m = n * 2 + mm_
                b = newbank()
                for kc in range(KC):
                    rhs = YC[:, kc, :] if kc < 4 else G[:, kc - 4, :]
                    rkey = ("YC", kc) if kc < 4 else ("G", kc - 4)
                    P.op("pe", lambda e, b=b, kc=kc, rhs=rhs, mm_=mm_: e.matmul(
                        PS[b][:, 0:TB], lhsT=W13[s][:, kc, mm_ * 128:(mm_ + 1) * 128], rhs=rhs, start=(kc == 0), stop=(kc == KC - 1)),
                        reads=[("W13", s), rkey], writes=[("ps", b)], signal=(kc == KC - 1))
                P.op("act", lambda e, b=b, m=m: e.copy(out=YSk[m][:], in_=PS[b][:, 0:TB]), reads=[("ps", b)], writes=[("YS", m)])
        for n in range(4):
            wo(n)
        post_residual(xt, xkey, TB, 24)
        ffn("ffn2", xt, xkey, TB, 32, 48, 40)
        P.dma("sp", lambda e: e.dma_start(out=outTv[:, :, tok], in_=xt[:]), reads=[xkey], dsem="xo0")

    if STAGE >= 3:
        for t in range(NT):
            phaseB_tile(t)

    P.finalize_and_emit()
    st.close()
    return nc


_NC_CACHE = {}


def kernel(**inputs):
    x = np.asarray(inputs["x"], np.float32)[0]
    g = lambda k: np.asarray(inputs[k], np.float32)[0]
    cst = _consts()
    xTfull = np.ascontiguousarray(x.T)
    xpad = np.concatenate([np.zeros((D, 2), np.float32), xTfull], axis=1)
    lora_wa = np.ascontiguousarray(np.concatenate([g("w_up"), g("a_up")], axis=0))
    in_maps = []
    for c in range(NCORES):
        vec = np.zeros((128, NVEC), np.float32)
        vec[:, V_C:V_C + 8] = _col(g("c"), 8)
        vec[:, V_GPRE1:V_GPRE1 + 8] = _col(g("ffn1_g_pre"), 8)
        vec[:, V_GPOST1:V_GPOST1 + 8] = _col(g("ffn1_g_post"), 8)
        vec[:, V_GPRE2:V_GPRE2 + 8] = _col(g("mix_g_pre"), 8)
        vec[:, V_GPOST2:V_GPOST2 + 8] = _col(g("mix_g_post"), 8)
        vec[:, V_GPRE3:V_GPRE3 + 8] = _col(g("ffn2_g_pre"), 8)
        vec[:, V_GPOST3:V_GPOST3 + 8] = _col(g("ffn2_g_post"), 8)
        mu = g("mu_shift")
        vec[:, V_MU:V_MU + 12] = _col(mu[:1536], 12)
        vec[0:64, V_MU + 12] = mu[1536:1600]
        vec[0:96, V_MU + 13] = mu[1600:1696]
        cw = g("conv_w")
        for ch in range(4):
            for j in range(3):
                vec[:, V_CONV + ch * 3 + j] = cw[j, ch * 128:(ch + 1) * 128]
        for (off, k) in ((V_W0, "w0"), (V_A0, "a0"), (V_KK, "k_k"), (V_KA, "k_a"), (V_LNW, "ln_x_w"), (V_LNB, "ln_x_b")):
            vec[:, off:off + 4] = _col(g(k), 4)
        vec[:, V_RK:V_RK + 4] = _col(g("r_k").reshape(-1), 4)
        vec[:, V_FLAG] = 0.0 if c == 0 else 1.0
        vec[:, V_ONEHOT + c] = 1.0
        m = {
            "xT": np.ascontiguousarray(xpad[:, c * NTOK:c * NTOK + NTOK + 2]),
            "vec": vec, "cst": cst,
            "b_ada": g("b_ada").reshape(1, -1), "w_ada": g("w_ada"),
            "ffn1_w1": g("ffn1_w1"), "ffn1_w3": g("ffn1_w3"), "ffn1_w2": g("ffn1_w2"),
            "ffn2_w1": g("ffn2_w1"), "ffn2_w3": g("ffn2_w3"), "ffn2_w2": g("ffn2_w2"),
            "w_in": g("w_in"), "w_out": g("w_out"), "lora_wa": lora_wa, "lora_g": g("g_up"),
        }
        in_maps.append(m)
    if "nc" not in _NC_CACHE:
        _NC_CACHE["nc"] = build()
    nc = _NC_CACHE["nc"]
    res = run_bass_kernel_spmd(nc, in_maps[:NRUN], core_ids=list(range(NRUN)))
    _NC_CACHE["res"] = res
    outs = [np.asarray(r["outT"], np.float32) for r in res.results]
    outs = outs + [np.zeros_like(outs[0])] * (NCORES - NRUN)
    full = np.concatenate(outs, axis=1)
    return np.ascontiguousarray(full.T)[None].astype(np.float32)
```

```python
from contextlib import ExitStack
import numpy as np
import concourse.bass as bass
import concourse.mybir as mybir
from concourse.bass_utils import run_bass_kernel_spmd

F32 = mybir.dt.float32
BF16 = mybir.dt.bfloat16
ALU = mybir.AluOpType
AF = mybir.ActivationFunctionType

NCORES = 8
D = 1024
KC = 8
DFF = 2816
JC = 22
SEQ = 16384
NTOK = SEQ // NCORES
TB = 512
NT = NTOK // TB
CH = 64
NCH = TB // CH
C0 = float(np.exp(-0.5))
NORM_EPS = 1e-6
GN_EPS = 64e-5
STAGE = 3
DEBUG = False
NRUN = 8
FAKE_CC = False
POOL_TO_DVE = True

ENGS = ("pe", "act", "dve", "pool", "sp")


class Op:
    __slots__ = ("eng", "fn", "reads", "writes", "signal", "dsem", "idx", "waits", "sigval", "nd", "deps", "inc")

    def __init__(self, eng, fn, reads, writes, signal, dsem, nd):
        self.eng, self.fn, self.reads, self.writes = eng, fn, reads, writes
        self.signal, self.dsem, self.nd = signal, dsem, nd
        self.waits = []
        self.sigval = None


class Prog:
    def __init__(self, nc):
        self.nc = nc
        self.ops = {e: [] for e in ENGS}
        self.order = []
        self.last_writer = {}
        self.readers = {}
        self.dsem_count = {}

    def op(self, eng, fn, reads=(), writes=(), signal=True):
        if eng == "pool" and POOL_TO_DVE:
            eng = "dve"
        if eng == "POOL!":
            eng = "pool"
        o = Op(eng, fn, tuple(reads), tuple(writes), signal, None, 0)
        self._add(o)
        return o

    def dma(self, eng, fn, reads=(), writes=(), dsem=None, nd=1, inc=16):
        o = Op(eng, fn, tuple(reads), tuple(writes), True, dsem, nd)
        o.inc = inc
        self._add(o)
        return o

    def _add(self, o):
        o.idx = len(self.ops[o.eng])
        self.ops[o.eng].append(o)
        self.order.append(o)
        deps = []
        for k in o.reads:
            w = self.last_writer.get(k)
            if w is not None:
                deps.append((w, "raw"))
        for k in o.writes:
            w = self.last_writer.get(k)
            if w is not None:
                deps.append((w, "waw"))
            for r in self.readers.get(k, ()):
                deps.append((r, "war"))
        o.deps = deps
        for k in o.writes:
            self.last_writer[k] = o
            self.readers[k] = []
        for k in o.reads:
            self.readers.setdefault(k, []).append(o)

    def finalize_and_emit(self):
        nc = self.nc
        for e in ENGS:
            cnt = 0
            for o in self.ops[e]:
                if o.dsem is not None:
                    c = self.dsem_count.get(o.dsem, 0) + o.inc * o.nd
                    self.dsem_count[o.dsem] = c
                    o.sigval = ("d_" + o.dsem, c)
                elif o.signal:
                    cnt += 1
                    o.sigval = ("e_" + e, cnt)
            nxt = None
            for o in reversed(self.ops[e]):
                if o.dsem is None:
                    if o.signal:
                        nxt = o.sigval
                    else:
                        assert nxt is not None
                        o.sigval = nxt
        for e in ENGS:
            known = {}
            for o in self.ops[e]:
                need = {}
                for (d, kind) in o.deps:
                    if d is o:
                        continue
                    if d.eng == e and d.dsem is None:
                        if kind != "raw" or e == "pe":
                            continue
                    s, v = d.sigval
                    if known.get(s, 0) >= v:
                        continue
                    if need.get(s, 0) < v:
                        need[s] = v
                for s, v in need.items():
                    known[s] = v
                o.waits = list(need.items())
        sem_names = sorted({o.sigval[0] for o in self.order})
        with ExitStack() as st:
            sems = {n: st.enter_context(nc.semaphore(n)) for n in sem_names}
            block = st.enter_context(nc.Block())
            reg = {"pe": block.tensor, "act": block.scalar, "dve": block.vector,
                   "pool": block.gpsimd, "sp": block.sync}

            def mk(e):
                def body(eng):
                    for o in self.ops[e]:
                        for s, v in o.waits:
                            eng.wait_ge(sems[s], v)
                        r = o.fn(eng)
                        if o.dsem is not None:
                            rs = r if isinstance(r, (list, tuple)) else [r]
                            assert len(rs) == o.nd, (len(rs), o.nd)
                            for ins in rs:
                                ins.then_inc(sems[o.sigval[0]], o.inc)
                        elif o.signal:
                            r.then_inc(sems[o.sigval[0]], 1)
                return body

            final = {}
            for o in self.order:
                s, v = o.sigval
                final[s] = max(final.get(s, 0), v)
            for e in ENGS:
                if e != "sp" and self.ops[e]:
                    reg[e](mk(e))

            def sp_body(eng):
                mk("sp")(eng)
                for s, v in final.items():
                    eng.wait_ge(sems[s], v)

            reg["sp"](sp_body)
        return len(sem_names)


C_IDENT = 0
C_BONES = 128
C_ONES = 256
C_MASK4 = 384
C_MASKL = 512
C_RESET = 576
C_HINIT = 1088
C_IDB = 1216
NCONST = 1216

V_C = 0
V_GPRE1 = 8
V_GPOST1 = 16
V_GPRE2 = 24
V_GPOST2 = 32
V_GPRE3 = 40
V_GPOST3 = 48
V_MU = 56
V_CONV = 70
V_W0 = 82
V_A0 = 86
V_KK = 90
V_KA = 94
V_RK = 98
V_LNW = 102
V_LNB = 106
V_FLAG = 110
V_ONEHOT = 111
NVEC = 119


def _consts():
    c = np.zeros((128, NCONST), np.float32)
    p = np.arange(128)
    c[p, C_IDENT + p] = 1.0
    c[:, C_BONES:C_BONES + 128] = (p[:, None] // 64 == p[None, :] // 64).astype(np.float32)
    c[:, C_ONES:C_ONES + 128] = 1.0
    s = (p % 64)[:, None]
    t = np.arange(64)[None, :]
    c[:, C_MASK4:C_MASK4 + 64] = (s < t)
    c[:, C_MASK4 + 64:C_MASK4 + 128] = (s <= t)
    c[:, C_MASKL:C_MASKL + 64] = (t < s)
    r = np.ones((512,), np.float32)
    r[::64] = 0.0
    c[:, C_RESET:C_RESET + 512] = r[None, :]
    hi = np.zeros((128, 128), np.float32)
    hi[np.arange(64), 64 + np.arange(64)] = 1.0
    hi[64 + np.arange(64), np.arange(64)] = 1.0
    c[:, C_HINIT:C_HINIT + 128] = hi
    return c


def _col(v, n):
    return np.ascontiguousarray(np.asarray(v, np.float32).reshape(n, 128).T)


def build():
    nc = bass.Bass("TRN2", target_bir_lowering=False)
    dt = nc.dram_tensor
    xT = dt("xT", [D, NTOK + 2], F32, kind="ExternalInput").ap()
    vec = dt("vec", [128, NVEC], F32, kind="ExternalInput").ap()
    cst = dt("cst", [128, NCONST], F32, kind="ExternalInput").ap()
    b_ada = dt("b_ada", [1, 9 * D], F32, kind="ExternalInput").ap()
    w_ada = dt("w_ada", [D, 9 * D], F32, kind="ExternalInput").ap()
    fw = {}
    for n in ("ffn1", "ffn2"):
        fw[n] = (dt(n + "_w1", [D, DFF], F32, kind="ExternalInput").ap(),
                 dt(n + "_w3", [D, DFF], F32, kind="ExternalInput").ap(),
                 dt(n + "_w2", [DFF, D], F32, kind="ExternalInput").ap())
    w_in = dt("w_in", [D, 3232], F32, kind="ExternalInput").ap()
    w_out = dt("w_out", [D, D], F32, kind="ExternalInput").ap()
    lora_wa = dt("lora_wa", [64, 512], F32, kind="ExternalInput").ap()
    lora_g = dt("lora_g", [96, 512], F32, kind="ExternalInput").ap()
    outT = dt("outT", [D, NTOK], F32, kind="ExternalOutput").ap()
    h_scr = dt("h_scr", [D, NTOK], F32).ap()
    ye_scr = dt("ye_scr", [8 * 128, NTOK], F32).ap()
    g_scr = dt("g_scr", [512, NTOK], F32).ap()
    bo_scr = dt("bo_scr", [512, NTOK], F32).ap()
    yc_scr = dt("yc_scr", [512, NTOK], BF16).ap()
    hx_loc = dt("hx_loc", [128, 512], F32).ap()
    hx_all = dt("hx_all", [NCORES * 128, 512], F32).ap()

    xTv = xT.rearrange("(kc p) n -> p kc n", p=128)
    outTv = outT.rearrange("(kc p) n -> p kc n", p=128)
    h_scrv = h_scr.rearrange("(kc p) n -> p kc n", p=128)

    st = ExitStack()
    sb = lambda name, shape, dtype=F32: st.enter_context(nc.sbuf_tensor(name, shape, dtype))
    P = Prog(nc)

    CST = sb("CST", [128, NCONST])
    VEC = sb("VEC", [128, NVEC])
    ADA = sb("ADA", [128, 72])
    DER = sb("DER", [128, 64])
    LWA = sb("LWA", [64, 512], BF16)
    LG = sb("LG", [96, 512], BF16)
    XT = [sb(f"XT{i}", [128, KC, TB]) for i in range(1)]
    U = sb("U", [128, KC, TB], BF16)
    SCR = [sb(f"SCR{i}", [128, TB]) for i in range(4)]
    SCB = [sb(f"SCB{i}", [128, TB], BF16) for i in range(2)]
    G = sb("G", [128, JC, TB], BF16)
    YSk = [sb(f"YS{m}", [128, TB]) for m in range(KC)]
    NW13 = 4
    W13 = [sb(f"W13_{i}", [128, KC, 256], BF16) for i in range(NW13)]
    NW2 = 3
    W2 = [sb(f"W2_{i}", [128, JC, 128], BF16) for i in range(NW2)]
    RSTD = sb("RSTD", [128, TB])
    PS = [st.enter_context(nc.psum_tensor(f"ps{b}", [128, 512], F32)) for b in range(8)]
    bank_ctr = [0]

    def newbank():
        b = bank_ctr[0] % 8
        bank_ctr[0] += 1
        return b

    ident = CST[:, C_IDENT:C_IDENT + 128]
    ones = CST[:, C_ONES:C_ONES + 128]
    bones = CST[:, C_BONES:C_BONES + 128]

    def vcol(off, i=0):
        return VEC[:, off + i:off + i + 1]

    P.dma("sp", lambda e: e.dma_start(out=CST[:], in_=cst[:, :]), writes=["CST"], dsem="cst")
    P.dma("sp", lambda e: e.dma_start(out=VEC[:], in_=vec[:, :]), writes=["VEC"], dsem="vec")
    P.dma("pool", lambda e: e.dma_start(out=LWA[:], in_=lora_wa[:, :]), writes=["LWA"], dsem="lwa")
    P.dma("pool", lambda e: e.dma_start(out=LG[:], in_=lora_g[:, :]), writes=["LG"], dsem="lg")

    CONDB = sb("CONDB", [128, 8], BF16)
    BROW = sb("BROW", [1, 256])
    AROW = sb("AROW", [1, 256])
    P.op("act", lambda e: e.activation(out=CONDB[:], in_=VEC[:, V_C:V_C + 8], func=AF.Silu),
         reads=["VEC"], writes=["CONDB"])
    w_adav = w_ada.rearrange("(kc p) n -> p kc n", p=128)
    adab = newbank()
    for n in range(36):
        s = n % NW13
        P.dma("pool", lambda e, n=n, s=s: e.dma_start(out=W13[s][:], in_=w_adav[:, :, n * 256:(n + 1) * 256]),
              writes=[("W13", s)], dsem=f"w13_{s}")
        P.dma("sp", lambda e, n=n: e.dma_start(out=BROW[:], in_=b_ada[:, n * 256:(n + 1) * 256]),
              writes=["BROW"], dsem="brow")
        b = newbank()
        if b == adab:
            b = newbank()
        for kc in range(KC):
            P.op("pe", lambda e, b=b, kc=kc, s=s: e.matmul(PS[b][0:1, 0:256], lhsT=CONDB[:, kc:kc + 1],
                                                        rhs=W13[s][:, kc, :], start=(kc == 0), stop=(kc == KC - 1)),
                 reads=["CONDB", ("W13", s)], writes=[("ps", b)], signal=(kc == KC - 1))
        P.op("dve", lambda e, b=b: e.tensor_tensor(out=AROW[:], in0=PS[b][0:1, 0:256], in1=BROW[:], op=ALU.add),
             reads=[("ps", b), "BROW"], writes=["AROW"])
        for hh in range(2):
            col = n * 2 + hh
            P.op("pe", lambda e, col=col, hh=hh: e.matmul(PS[adab][:, col:col + 1],
                                                         lhsT=AROW[0:1, hh * 128:(hh + 1) * 128],
                                                         rhs=CST[0:1, C_ONES:C_ONES + 1], start=True, stop=True),
                 reads=["AROW", "CST"], writes=[("ps", adab)])
    P.op("dve", lambda e: e.tensor_copy(out=ADA[:], in_=PS[adab][:, 0:72]), reads=[("ps", adab)], writes=["ADA"])
    for (o, gpre, sc) in ((0, V_GPRE1, 8), (16, V_GPRE2, 32), (32, V_GPRE3, 56)):
        P.op("dve", lambda e, o=o, gpre=gpre, sc=sc: e.scalar_tensor_tensor(
            out=DER[:, o:o + 8], in0=ADA[:, sc:sc + 8], scalar=1.0, in1=VEC[:, gpre:gpre + 8],
            op0=ALU.add, op1=ALU.mult), reads=["ADA", "VEC"], writes=["DER"])
    for (o, gpost, gt, f) in ((8, V_GPOST1, 16, 0.5), (24, V_GPOST2, 40, 1.0), (40, V_GPOST3, 64, 0.5)):
        P.op("dve", lambda e, o=o, gpost=gpost, gt=gt, f=f: e.scalar_tensor_tensor(
            out=DER[:, o:o + 8], in0=ADA[:, gt:gt + 8], scalar=f, in1=VEC[:, gpost:gpost + 8],
            op0=ALU.mult, op1=ALU.mult), reads=["ADA", "VEC"], writes=["DER"])
    P.op("dve", lambda e: e.tensor_scalar(out=DER[:, 48:56], in0=VEC[:, V_W0:V_W0 + 8], scalar1=0.5, scalar2=None,
                                          op0=ALU.mult), reads=["VEC"], writes=["DER"])
    P.op("dve", lambda e: e.tensor_scalar(out=DER[:, 56:60], in0=VEC[:, V_KA:V_KA + 4], scalar1=-1.0, scalar2=1.0,
                                          op0=ALU.mult, op1=ALU.add), reads=["VEC"], writes=["DER"])
    OMMU = sb("OMMU", [128, 14])
    P.op("dve", lambda e: e.tensor_scalar(out=OMMU[:], in0=VEC[:, V_MU:V_MU + 14], scalar1=-1.0, scalar2=1.0,
                                          op0=ALU.mult, op1=ALU.add), reads=["VEC"], writes=["OMMU"])

    EPS = sb("EPS", [128, 2])
    P.op("pool", lambda e: e.memset(EPS[:, 0:1], NORM_EPS), writes=["EPS"])
    P.op("pool", lambda e: e.memset(EPS[:, 1:2], GN_EPS), writes=["EPS"])

    def rstd_from_sumsq(b, width, nfeat, eps_col, out_ap, out_key):
        P.op("act", lambda e: e.activation(out=SCR[2][:, 0:width], in_=PS[b][:, 0:width], func=AF.Ln,
                                           scale=1.0 / nfeat, bias=EPS[:, eps_col:eps_col + 1]),
             reads=[("ps", b), "EPS"], writes=[("SCR", 2)])
        P.op("act", lambda e: e.activation(out=out_ap, in_=SCR[2][:, 0:width], func=AF.Exp, scale=-0.5),
             reads=[("SCR", 2)], writes=[out_key])

    def sumsq(src, src_key, n, width, keyf=None):
        b = newbank()
        for kc in range(n):
            s = SCR[kc % 2]
            sk = ("SCR", kc % 2)
            rk = keyf(kc) if keyf is not None else src_key
            P.op("pool", lambda e, s=s, kc=kc: e.tensor_tensor(out=s[:, 0:width], in0=src(kc), in1=src(kc), op=ALU.mult),
                 reads=[rk], writes=[sk])
            P.op("pe", lambda e, s=s, kc=kc, b=b: e.matmul(PS[b][:, 0:width], lhsT=ones, rhs=s[:, 0:width],
                                                         start=(kc == 0), stop=(kc == n - 1)),
                 reads=[sk, "CST"], writes=[("ps", b)], signal=True)
        return b

    def modulate(xt, xkey, width, gs_off, sh_off):
        b = sumsq(lambda kc: xt[:, kc, 0:width], xkey, KC, width)
        rstd_from_sumsq(b, width, D, 0, RSTD[:, 0:width], "RSTD")
        for kc in range(KC):
            s = SCR[kc % 2]
            sk = ("SCR", kc % 2)
            P.op("dve", lambda e, s=s, kc=kc: e.scalar_tensor_tensor(
                out=s[:, 0:width], in0=xt[:, kc, 0:width], scalar=DER[:, gs_off + kc:gs_off + kc + 1],
                in1=RSTD[:, 0:width], op0=ALU.mult, op1=ALU.mult),
                reads=[xkey, "DER", "RSTD"], writes=[sk])
            P.op("pool", lambda e, s=s, kc=kc: e.tensor_scalar(
                out=U[:, kc, 0:width], in0=s[:, 0:width], scalar1=ADA[:, sh_off + kc:sh_off + kc + 1], scalar2=None,
                op0=ALU.add), reads=[sk, "ADA"], writes=["U"])

    def post_residual(xt, xkey, width, gg_off):
        b = sumsq(lambda kc: YSk[kc][:, 0:width], None, KC, width, keyf=lambda kc: ("YS", kc))
        rstd_from_sumsq(b, width, D, 0, RSTD[:, 0:width], "RSTD")
        for kc in range(KC):
            s = SCR[kc % 2]
            sk = ("SCR", kc % 2)
            P.op("dve", lambda e, s=s, kc=kc: e.scalar_tensor_tensor(
                out=s[:, 0:width], in0=YSk[kc][:, 0:width], scalar=DER[:, gg_off + kc:gg_off + kc + 1],
                in1=RSTD[:, 0:width], op0=ALU.mult, op1=ALU.mult),
                reads=[("YS", kc), "DER", "RSTD"], writes=[sk])
            P.op("pool", lambda e, s=s, kc=kc: e.tensor_tensor(
                out=xt[:, kc, 0:width], in0=xt[:, kc, 0:width], in1=s[:, 0:width], op=ALU.add),
                reads=[sk, xkey], writes=[xkey])

    w13_ctr = [36]
    w2_ctr = [0]

    def ffn(name, xt, xkey, width, gs_off, sh_off, gg_off):
        w1, w3, w2 = fw[name]
        w1v = w1.rearrange("(kc p) n -> p kc n", p=128)
        w3v = w3.rearrange("(kc p) n -> p kc n", p=128)
        w2v = w2.rearrange("(j p) n -> p j n", p=128)
        modulate(xt, xkey, width, gs_off, sh_off)
        for j in range(JC):
            s = w13_ctr[0] % NW13
            w13_ctr[0] += 1

            def ld(e, j=j, s=s):
                return [e.dma_start(out=W13[s][:, :, 0:128], in_=w1v[:, :, j * 128:(j + 1) * 128]),
                        e.dma_start(out=W13[s][:, :, 128:256], in_=w3v[:, :, j * 128:(j + 1) * 128])]
            P.dma("pool", ld, writes=[("W13", s)], dsem=f"w13_{s}", nd=2)
            b1 = newbank()
            b3 = newbank()
            for (bb, off) in ((b1, 0), (b3, 128)):
                for kc in range(KC):
                    P.op("pe", lambda e, bb=bb, off=off, kc=kc, s=s: e.matmul(
                        PS[bb][:, 0:width], lhsT=W13[s][:, kc, off:off + 128], rhs=U[:, kc, 0:width],
                        start=(kc == 0), stop=(kc == KC - 1)),
                        reads=[("W13", s), "U"], writes=[("ps", bb)], signal=(kc == KC - 1))
            sc = SCB[j % 2]
            sck = ("SCB", j % 2)
            P.op("act", lambda e, b1=b1, sc=sc: e.activation(out=sc[:, 0:width], in_=PS[b1][:, 0:width], func=AF.Silu),
                 reads=[("ps", b1)], writes=[sck])
            P.op("dve", lambda e, b3=b3, sc=sc, j=j: e.tensor_tensor(out=G[:, j, 0:width], in0=PS[b3][:, 0:width],
                                                                    in1=sc[:, 0:width], op=ALU.mult),
                 reads=[("ps", b3), sck], writes=[("G", j)])
        for m in range(KC):
            s = w2_ctr[0] % NW2
            w2_ctr[0] += 1
            P.dma("pool", lambda e, m=m, s=s: e.dma_start(out=W2[s][:], in_=w2v[:, :, m * 128:(m + 1) * 128]),
                  writes=[("W2", s)], dsem=f"w2_{s}")
            b = newbank()
            for j in range(JC):
                P.op("pe", lambda e, b=b, j=j, s=s: e.matmul(PS[b][:, 0:width], lhsT=W2[s][:, j, :], rhs=G[:, j, 0:width],
                                                           start=(j == 0), stop=(j == JC - 1)),
                     reads=[("W2", s), ("G", j)], writes=[("ps", b)], signal=(j == JC - 1))
            P.op("act", lambda e, b=b, m=m: e.copy(out=YSk[m][:, 0:width], in_=PS[b][:, 0:width]),
                 reads=[("ps", b)], writes=[("YS", m)])
        post_residual(xt, xkey, width, gg_off)


    TT = [sb(f"TT{i}", [128, TB]) for i in range(10)]
    CVAL = sb("CVAL", [128, TB])
    CV = sb("CV", [128, 4, TB + 2])
    YC = sb("YC", [128, 4, TB], BF16)
    PRM = [sb(f"PRM{i}", [128, TB + 1]) for i in range(2)]
    CAR = sb("CAR", [128, 14])
    XH = sb("XH", [128, KC, 2])
    VTq = [sb(f"VT{q}", [128, NCH, 128], BF16) for q in range(2)]
    BKq = [sb(f"BK{q}", [128, NCH, 64], BF16) for q in range(2)]
    A4 = [sb(f"A4{q}", [128, NCH, 128], BF16) for q in range(2)]
    RR = [sb(f"RR{i}", [128, NCH, 128], BF16) for i in range(2)]
    XNN = [sb(f"XNN{i}", [128, NCH, 64], BF16) for i in range(2)]
    WT = sb("WT", [128, NCH, 64], BF16)
    U0 = sb("U0", [128, NCH, 128], BF16)
    HX = [sb(f"HX{hp}", [128, 128]) for hp in range(4)]
    HXB = [sb(f"HXB{hp}", [128, 128], BF16) for hp in range(4)]
    YES = [sb(f"YES{i}", [128, TB]) for i in range(2)]
    IDB = sb("IDB", [128, 128], BF16)
    ARs = [sb(f"AR{i}", [128, NCH, 128], BF16) for i in range(2)]
    KBts = [sb(f"KBt{i}", [128, NCH, 128], BF16) for i in range(2)]
    GAMs = [sb(f"GAM{i}", [128, NCH]) for i in range(2)]
    TWA = sb("TWA", [64, TB], BF16)
    SGB = sb("SGB", [96, TB], BF16)
    EPS2 = sb("EPS2", [128, 1])
    P.op("pool", lambda e: e.memset(EPS2[:], 1e-24), writes=["EPS2"])
    P.op("pool", lambda e: e.tensor_copy(out=IDB[:], in_=ident), reads=["CST"], writes=["IDB"])
    for q in range(2):
        P.op("pool", lambda e, q=q: e.memset(VTq[q][:], 0.0), writes=[("VT", q, 0), ("VT", q, 1)])
    for hp in range(4):
        P.op("pool", lambda e, hp=hp: e.tensor_copy(out=HX[hp][:], in_=CST[:, C_HINIT:C_HINIT + 128]),
             reads=["CST"], writes=[("HX", hp, 0), ("HX", hp, 1)])
        P.op("pool", lambda e, hp=hp: e.tensor_copy(out=HXB[hp][:], in_=CST[:, C_HINIT:C_HINIT + 128]),
             reads=["CST"], writes=[("HXB", hp, 0), ("HXB", hp, 1)])
    mix_banks = [0]

    def mbank():
        b = mix_banks[0] % 3
        mix_banks[0] += 1
        return b
    s_banks = [0]

    def sbank():
        b = 3 + s_banks[0] % 3
        s_banks[0] += 1
        return b

    w_inv = w_in.rearrange("(kc p) n -> p kc n", p=128)
    flagc = VEC[:, V_FLAG:V_FLAG + 1]

    def load_w(cols, srcv=None):
        srcv = w_inv if srcv is None else srcv
        s = w13_ctr[0] % NW13
        w13_ctr[0] += 1
        offs = []
        o = 0
        for (c0, n) in cols:
            offs.append((o, c0, n))
            o += n

        def ld(e):
            return [e.dma_start(out=W13[s][:, :, o:o + n], in_=srcv[:, :, c0:c0 + n]) for (o, c0, n) in offs]
        P.dma("pool", ld, writes=[("W13", s)], dsem=f"w13_{s}", nd=len(offs))
        return s

    def proj(s, off, m, width):
        b = mbank()
        for kc in range(KC):
            P.op("pe", lambda e, b=b, kc=kc: e.matmul(PS[b][0:m, 0:width], lhsT=W13[s][:, kc, off:off + m],
                                                    rhs=U[:, kc, 0:width], start=(kc == 0), stop=(kc == KC - 1)),
                 reads=[("W13", s), "U"], writes=[("ps", b)], signal=(kc == KC - 1))
        return b

    def mix_halo():
        P.dma("sp", lambda e: e.dma_start(out=XH[:], in_=xTv[:, :, 0:2]), writes=["XH"], dsem="xh")
        ffn("ffn1", XH, "XH", 2, 0, 0, 8)
        modulate(XH, "XH", 2, 16, 24)
        for ch in range(4):
            s = load_w([(1024 + ch * 128, 128), (ch * 128, 128)])
            b = proj(s, 0, 128, 2)
            P.op("act", lambda e, b=b: e.copy(out=CVAL[:, 0:2], in_=PS[b][:, 0:2]), reads=[("ps", b)], writes=["CVAL"])
            b = proj(s, 128, 128, 2)
            P.op("dve", lambda e, b=b, ch=ch: e.scalar_tensor_tensor(out=CV[:, ch, TB:TB + 2], in0=PS[b][:, 0:2], scalar=flagc,
                                                                    in1=CVAL[:, 0:2], op0=ALU.mult, op1=ALU.mult),
                 reads=[("ps", b), "CVAL", "VEC"], writes=[("CV", ch)])
        for i in range(14):
            if i < 12:
                s = load_w([(1536 + i * 128, 128)])
                m = 128
            elif i == 12:
                s = load_w([(3072, 64)])
                m = 64
            else:
                s = load_w([(3136, 96)])
                m = 96
            b = proj(s, 0, m, 2)
            P.op("dve", lambda e, b=b, i=i, m=m: e.tensor_scalar(out=CAR[0:m, i:i + 1], in0=PS[b][0:m, 1:2],
                                                                scalar1=VEC[0:m, V_MU + i:V_MU + i + 1], scalar2=flagc[0:m],
                                                                op0=ALU.mult, op1=ALU.mult),
                 reads=[("ps", b), "VEC"], writes=[("CAR", i)])

    prm_ctr = [0]

    def lerp_evac(b, i, m, out_ap, out_key):
        pi = prm_ctr[0] % 2
        prm_ctr[0] += 1
        pr = PRM[pi]
        pk = ("PRM", pi)
        P.op("pool", lambda e: e.tensor_copy(out=pr[0:m, 0:1], in_=CAR[0:m, i:i + 1]), reads=[("CAR", i)], writes=[pk])
        P.op("act", lambda e: e.activation(out=pr[0:m, 1:TB + 1], in_=PS[b][0:m, 0:TB], func=AF.Identity,
                                           scale=VEC[0:m, V_MU + i:V_MU + i + 1]),
             reads=[("ps", b), "VEC"], writes=[pk])
        P.op("dve", lambda e: e.scalar_tensor_tensor(out=out_ap, in0=PS[b][0:m, 0:TB], scalar=OMMU[0:m, i:i + 1],
                                                     in1=pr[0:m, 0:TB], op0=ALU.mult, op1=ALU.add),
             reads=[("ps", b), "OMMU", pk], writes=[out_key])
        P.op("pool", lambda e: e.tensor_copy(out=CAR[0:m, i:i + 1], in_=pr[0:m, TB:TB + 1]), reads=[pk], writes=[("CAR", i)])

    dbg = {}

    def mixer_tile(t, xt, xkey):
        tok = slice(t * TB, (t + 1) * TB)
        modulate(xt, xkey, TB, 16, 24)
        for ch in range(4):
            s = load_w([(1024 + ch * 128, 128), (ch * 128, 128)])
            s2 = load_w([(512 + ch * 128, 128)])
            b = proj(s, 0, 128, TB)
            P.op("act", lambda e, b=b: e.copy(out=CVAL[:], in_=PS[b][:, 0:TB]), reads=[("ps", b)], writes=["CVAL"])
            b = proj(s, 128, 128, TB)
            P.op("pool", lambda e, ch=ch: e.tensor_copy(out=CV[:, ch, 0:2], in_=CV[:, ch, TB:TB + 2]),
                 reads=[("CV", ch)], writes=[("CV", ch)])
            P.op("dve", lambda e, b=b, ch=ch: e.tensor_tensor(out=CV[:, ch, 2:TB + 2], in0=PS[b][:, 0:TB], in1=CVAL[:],
                                                             op=ALU.mult),
                 reads=[("ps", b), "CVAL"], writes=[("CV", ch)])
            acc = SCR[3]
            P.op("pool", lambda e, ch=ch: e.tensor_scalar(out=acc[:], in0=CV[:, ch, 2:TB + 2],
                                                         scalar1=VEC[:, V_CONV + ch * 3 + 2:V_CONV + ch * 3 + 3], scalar2=None,
                                                         op0=ALU.mult),
                 reads=[("CV", ch), "VEC"], writes=[("SCR", 3)])
            for (j, o) in ((1, 1), (0, 0)):
                P.op("dve", lambda e, ch=ch, j=j, o=o: e.scalar_tensor_tensor(
                    out=acc[:], in0=CV[:, ch, o:o + TB], scalar=VEC[:, V_CONV + ch * 3 + j:V_CONV + ch * 3 + j + 1],
                    in1=acc[:], op0=ALU.mult, op1=ALU.add),
                    reads=[("CV", ch), "VEC", ("SCR", 3)], writes=[("SCR", 3)])
            b = proj(s2, 0, 128, TB)
            P.op("dve", lambda e, b=b, ch=ch: e.tensor_tensor(out=YC[:, ch, :], in0=PS[b][:, 0:TB], in1=acc[:], op=ALU.mult),
                 reads=[("ps", b), ("SCR", 3)], writes=[("YC", ch)])
        P.dma("sp", lambda e: e.dma_start(out=yc_scr.rearrange("(c p) n -> p c n", p=128)[:, :, tok], in_=YC[:]),
              reads=[("YC", c) for c in range(4)], writes=[("ycs", t)], dsem="ycs")
        XWA, XG = TT[8], TT[9]
        s = load_w([(3072, 64), (3136, 96)])
        b = proj(s, 0, 64, TB)
        lerp_evac(b, 12, 64, XWA[0:64, :], ("TT", 8))
        b = proj(s, 64, 96, TB)
        lerp_evac(b, 13, 96, XG[0:96, :], ("TT", 9))
        P.op("act", lambda e: e.activation(out=TWA[0:32, :], in_=XWA[0:32, :], func=AF.Tanh), reads=[("TT", 8)], writes=["TWA0"])
        P.op("pool", lambda e: e.tensor_copy(out=TWA[32:64, :], in_=XWA[32:64, :]), reads=[("TT", 8)], writes=["TWA1"])
        P.op("act", lambda e: e.activation(out=XG[0:96, :], in_=XG[0:96, :], func=AF.Tanh, scale=0.5),
             reads=[("TT", 9)], writes=[("TT", 9)])
        P.op("pool", lambda e: e.tensor_scalar(out=SGB[:], in0=XG[0:96, :], scalar1=0.5, scalar2=0.5, op0=ALU.mult, op1=ALU.add),
             reads=[("TT", 9)], writes=["SGB"])
        def hp_bufs(hp):
            pb = hp % 2
            return (ARs[pb], KBts[pb], GAMs[pb], G[:, 4 + 8 * pb, :], G[:, 5 + 8 * pb, :], G[:, 6 + 8 * pb, :],
                    (lambda j: ("Gk", j, pb)))

        def hp_prep(hp):
            AR, KBt, GAM, BHb, KHb, XVb, gk = hp_bufs(hp)
            hcol = slice(hp * 128, (hp + 1) * 128)
            XR, XK, XV = TT[0], TT[1], TT[2]
            s = load_w([(1536 + hp * 128, 128), (2048 + hp * 128, 128)])
            s2 = load_w([(2560 + hp * 128, 128)])
            b = proj(s, 0, 128, TB)
            lerp_evac(b, hp, 128, XR[:], ("TT", 0))
            b = proj(s, 128, 128, TB)
            lerp_evac(b, 4 + hp, 128, XK[:], ("TT", 1))
            b = proj(s2, 0, 128, TB)
            lerp_evac(b, 8 + hp, 128, XV[:], ("TT", 2))
            SGW, AA, CS = YSk[0], YSk[1], YSk[2]
            bw = mbank()
            P.op("pe", lambda e, bw=bw: e.matmul(PS[bw][:, 0:TB], lhsT=LWA[0:32, hcol], rhs=TWA[0:32, :], start=True, stop=True),
                 reads=["LWA", "TWA0"], writes=[("ps", bw)])
            P.op("act", lambda e, bw=bw, hp=hp: e.activation(out=SGW[:], in_=PS[bw][:, 0:TB], func=AF.Tanh, scale=0.5,
                                                             bias=DER[:, 48 + hp:49 + hp]),
                 reads=[("ps", bw), "DER"], writes=[("YS", 0)])
            P.op("pool", lambda e: e.tensor_scalar(out=SGW[:], in0=SGW[:], scalar1=0.5, scalar2=0.5, op0=ALU.mult, op1=ALU.add),
                 reads=[("YS", 0)], writes=[("YS", 0)])
            ba = mbank()
            P.op("pe", lambda e, ba=ba: e.matmul(PS[ba][:, 0:TB], lhsT=LWA[32:64, hcol], rhs=TWA[32:64, :], start=True, stop=True),
                 reads=["LWA", "TWA1"], writes=[("ps", ba)])
            P.op("act", lambda e, ba=ba, hp=hp: e.activation(out=AA[:], in_=PS[ba][:, 0:TB], func=AF.Tanh, scale=0.5,
                                                             bias=DER[:, 52 + hp:53 + hp]),
                 reads=[("ps", ba), "DER"], writes=[("YS", 1)])
            P.op("pool", lambda e: e.tensor_scalar(out=AA[:], in0=AA[:], scalar1=0.5, scalar2=0.5, op0=ALU.mult, op1=ALU.add),
                 reads=[("YS", 1)], writes=[("YS", 1)])
            bg = mbank()
            P.op("pe", lambda e, bg=bg: e.matmul(PS[bg][:, 0:TB], lhsT=LG[0:96, hcol], rhs=SGB[0:96, :], start=True, stop=True),
                 reads=["LG", "SGB"], writes=[("ps", bg)])
            GT = TT[3]
            P.op("act", lambda e, bg=bg: e.copy(out=GT[:], in_=PS[bg][:, 0:TB]), reads=[("ps", bg)], writes=[("TT", 3)])
            P.dma("sp", lambda e, hp=hp: e.dma_start(out=g_scr[hp * 128:(hp + 1) * 128, tok], in_=GT[:]),
                  reads=[("TT", 3)], writes=[("gs", hp, t)], dsem="gts")
            P.op("dve", lambda e: e.tensor_tensor_scan(out=CS[:], data0=CST[:, C_RESET:C_RESET + TB], data1=SGW[:], initial=0.0,
                                                       op0=ALU.mult, op1=ALU.add),
                 reads=["CST", ("YS", 0)], writes=[("YS", 2)])
            E1, E2, E3, E4 = YSk[3], YSk[4], YSk[5], YSk[6]
            P.op("act", lambda e: e.activation(out=E1[:], in_=CS[:], func=AF.Exp, scale=-C0), reads=[("YS", 2)], writes=[("YS", 3)])
            P.op("act", lambda e: e.activation(out=E3[:], in_=CS[:], func=AF.Exp, scale=C0), reads=[("YS", 2)], writes=[("YS", 5)])
            P.op("pool", lambda e: e.tensor_tensor(out=E2[:], in0=CS[:], in1=SGW[:], op=ALU.subtract),
                 reads=[("YS", 2), ("YS", 0)], writes=[("YS", 4)])
            P.op("act", lambda e: e.activation(out=E2[:], in_=E2[:], func=AF.Exp, scale=-C0), reads=[("YS", 4)], writes=[("YS", 4)])
            cs3 = CS[:].rearrange("p (c t) -> p c t", t=CH)
            P.op("dve", lambda e: e.tensor_tensor(out=E4[:].rearrange("p (c t) -> p c t", t=CH),
                                                  in0=cs3[:, :, CH - 1:CH].to_broadcast([128, NCH, CH]), in1=cs3, op=ALU.subtract),
                 reads=[("YS", 2)], writes=[("YS", 6)])
            P.op("act", lambda e: e.activation(out=E4[:], in_=E4[:], func=AF.Exp, scale=-C0), reads=[("YS", 6)], writes=[("YS", 6)])
            P.op("act", lambda e: e.activation(out=GAM[:].unsqueeze(2), in_=cs3[:, :, CH - 1:CH], func=AF.Exp, scale=-C0),
                 reads=[("YS", 2)], writes=[gk(9)])
            KKr, KK_, T1 = YSk[7], TT[4], TT[5]
            P.op("pool", lambda e, hp=hp: e.tensor_scalar(out=KKr[:], in0=XK[:], scalar1=VEC[:, V_KK + hp:V_KK + hp + 1], scalar2=None,
                                                         op0=ALU.mult), reads=[("TT", 1), "VEC"], writes=[("YS", 7)])
            P.op("pool", lambda e: e.tensor_tensor(out=T1[:], in0=KKr[:], in1=KKr[:], op=ALU.mult),
                 reads=[("YS", 7)], writes=[("TT", 5)])
            bn = mbank()
            P.op("pe", lambda e, bn=bn: e.matmul(PS[bn][:, 0:TB], lhsT=bones, rhs=T1[:], start=True, stop=True),
                 reads=["CST", ("TT", 5)], writes=[("ps", bn)])
            P.op("act", lambda e, bn=bn: e.activation(out=T1[:], in_=PS[bn][:, 0:TB], func=AF.Ln, bias=EPS2[:]),
                 reads=[("ps", bn), "EPS2"], writes=[("TT", 5)])
            P.op("act", lambda e: e.activation(out=T1[:], in_=T1[:], func=AF.Exp, scale=-0.5), reads=[("TT", 5)], writes=[("TT", 5)])
            P.op("dve", lambda e: e.tensor_tensor(out=KK_[:], in0=KKr[:], in1=T1[:], op=ALU.mult),
                 reads=[("YS", 7), ("TT", 5)], writes=[("TT", 4)])
            BETA, KMOD = TT[6], TT[7]
            P.op("dve", lambda e: e.tensor_tensor(out=BETA[:], in0=KK_[:], in1=AA[:], op=ALU.mult),
                 reads=[("TT", 4), ("YS", 1)], writes=[("TT", 6)])
            P.op("dve", lambda e, hp=hp: e.tensor_scalar(out=T1[:], in0=AA[:], scalar1=VEC[:, V_KA + hp:V_KA + hp + 1],
                                                        scalar2=DER[:, 56 + hp:57 + hp], op0=ALU.mult, op1=ALU.add),
                 reads=[("YS", 1), "VEC", "DER", ("TT", 5)], writes=[("TT", 5)])
            P.op("pool", lambda e: e.tensor_tensor(out=KMOD[:], in0=XK[:], in1=T1[:], op=ALU.mult),
                 reads=[("TT", 1), ("TT", 5)], writes=[("TT", 7)])
            P.op("dve", lambda e, hp=hp: e.scalar_tensor_tensor(out=T1[:], in0=XR[:], scalar=VEC[:, V_RK + hp:V_RK + hp + 1],
                                                               in1=KMOD[:], op0=ALU.mult, op1=ALU.mult),
                 reads=[("TT", 0), ("TT", 7), "VEC", ("TT", 5)], writes=[("TT", 5)])
            bb = mbank()
            P.op("pe", lambda e, bb=bb: e.matmul(PS[bb][:, 0:TB], lhsT=bones, rhs=T1[:], start=True, stop=True),
                 reads=["CST", ("TT", 5)], writes=[("ps", bb)])
            BON = KKr
            P.op("dve", lambda e, bb=bb: e.tensor_tensor(out=BON[:], in0=PS[bb][:, 0:TB], in1=XV[:], op=ALU.mult),
                 reads=[("ps", bb), ("TT", 2), ("YS", 7)], writes=[("YS", 7)])
            P.op("pool", lambda e, hp=hp: e.tensor_scalar(out=BON[:], in0=BON[:], scalar1=VEC[:, V_LNB + hp:V_LNB + hp + 1], scalar2=None,
                                                         op0=ALU.add), reads=[("YS", 7), "VEC"], writes=[("YS", 7)])
            P.dma("sp", lambda e, hp=hp: e.dma_start(out=bo_scr[hp * 128:(hp + 1) * 128, tok], in_=BON[:]),
                  reads=[("YS", 7)], writes=[("bs", hp, t)], dsem="bos")
            v3 = lambda ap: ap.rearrange("p (c t) -> p c t", t=CH)
            P.op("dve", lambda e: e.scalar_tensor_tensor(out=AR[:, :, 0:64], in0=v3(KK_[:]), scalar=-1.0, in1=v3(E2[:]), op0=ALU.mult, op1=ALU.mult),
                 reads=[("TT", 4), ("YS", 4)], writes=[gk(0)])
            P.op("pool", lambda e: e.tensor_tensor(out=AR[:, :, 64:128], in0=v3(XR[:]), in1=v3(E1[:]), op=ALU.mult),
                 reads=[("TT", 0), ("YS", 3)], writes=[gk(1)])
            for q in range(2):
                rows = slice(64 * q, 64 * q + 64)
                ks, bs = (0, 1) if q == 0 else (1, 0)
                P.op("dve", lambda e, rows=rows, ks=ks: e.tensor_tensor(out=KBt[rows, :, ks * 64:ks * 64 + 64], in0=v3(KMOD[rows, :]), in1=v3(E3[rows, :]), op=ALU.mult),
                     reads=[("TT", 7), ("YS", 5)], writes=[gk(2 + ks)])
                P.op("pool", lambda e, rows=rows, bs=bs: e.tensor_tensor(out=KBt[rows, :, bs * 64:bs * 64 + 64], in0=v3(BETA[rows, :]), in1=v3(E3[rows, :]), op=ALU.mult),
                     reads=[("TT", 6), ("YS", 5)], writes=[gk(2 + bs)])
            P.op("dve", lambda e: e.tensor_tensor(out=BHb, in0=BETA[:], in1=E4[:], op=ALU.mult),
                 reads=[("TT", 6), ("YS", 6)], writes=[gk(4)])
            P.op("pool", lambda e: e.tensor_tensor(out=KHb, in0=KMOD[:], in1=E4[:], op=ALU.mult),
                 reads=[("TT", 7), ("YS", 6)], writes=[gk(5)])
            P.op("pool", lambda e: e.tensor_copy(out=XVb, in_=XV[:]), reads=[("TT", 2)], writes=[gk(6)])
            if t == 0 and hp == 0:
                dbg.update(XR=(XR, ("TT", 0)), XK=(XK, ("TT", 1)), XV=(XV, ("TT", 2)), AA=(AA, ("YS", 1)),
                           SGW=(SGW, ("YS", 0)), KK=(KK_, ("TT", 4)), CS=(CS, ("YS", 2)))

        def hp_rest(hp, D):
            AR, KBt, GAM, BHb, KHb, XVb, gk = hp_bufs(hp)
            nflush = (len(D) + 15) // 16

            def head_setup(q):
                PK, PB = 64 * q, 64 * (1 - q)
                rk = slice(PK, PK + 64)
                rb = slice(PB, PB + 64)
                vcol = slice(0, 64) if q == 0 else slice(64, 128)
                sbeta = 1 if q == 0 else 0
                cc = lambda c: slice(c * CH, (c + 1) * CH)
                bv = sbank()
                for c in range(NCH):
                    P.op("pe", lambda e, c=c: e.matmul(PS[bv][rk, cc(c)], lhsT=XVb[rk, cc(c)], rhs=IDB[rk, rk], start=True, stop=True),
                         reads=[gk(6), "IDB"], writes=[("ps", bv)], signal=(c == NCH - 1))
                P.op("act", lambda e: e.copy(out=VTq[q][rk, :, vcol], in_=PS[bv][rk, 0:TB].rearrange("p (c t) -> p c t", t=CH)),
                     reads=[("ps", bv)], writes=[("VT", q, q)])
                yield
                bk = sbank()
                for c in range(NCH):
                    P.op("pe", lambda e, c=c: e.matmul(PS[bk][rb, cc(c)], lhsT=BHb[rk, cc(c)], rhs=IDB[rk, rk], start=True, stop=True),
                         reads=[gk(4), "IDB"], writes=[("ps", bk)], signal=False)
                    P.op("pe", lambda e, c=c: e.matmul(PS[bk][rk, cc(c)], lhsT=KHb[rk, cc(c)], rhs=IDB[rk, rk], start=True, stop=True),
                         reads=[gk(5), "IDB"], writes=[("ps", bk)], signal=(c == NCH - 1))
                P.op("dve", lambda e: e.tensor_copy(out=BKq[q][:], in_=PS[bk][:, 0:TB].rearrange("p (c t) -> p c t", t=CH)),
                     reads=[("ps", bk)], writes=[("BK", q)])
                yield
                for c4 in range(2):
                    b4 = sbank()
                    for ci in range(4):
                        c = c4 * 4 + ci
                        P.op("pe", lambda e, c=c, ci=ci, b4=b4: e.matmul(PS[b4][:, ci * 128:(ci + 1) * 128], lhsT=KBt[rk, c, :],
                                                                        rhs=AR[rk, c, :], start=True, stop=True),
                             reads=[gk(0), gk(1), gk(2), gk(3)], writes=[("ps", b4)], signal=(ci == 3))
                    P.op("dve", lambda e, c4=c4, b4=b4: e.tensor_tensor(
                        out=A4[q][:, c4 * 4:(c4 + 1) * 4, :], in0=PS[b4][:, 0:512].rearrange("p (c t) -> p c t", t=128),
                        in1=CST[:, C_MASK4:C_MASK4 + 128].unsqueeze(1).to_broadcast([128, 4, 128]), op=ALU.mult),
                        reads=[("ps", b4), "CST"], writes=[("A4", q)])
                yield
                bl = sbank()
                for c in range(NCH):
                    P.op("pe", lambda e, c=c: e.matmul(PS[bl][rb, cc(c)], lhsT=AR[rk, c, 0:64], rhs=KBt[rk, c, sbeta * 64:sbeta * 64 + 64],
                                                      start=True, stop=True),
                         reads=[gk(0), gk(2 + sbeta)], writes=[("ps", bl)], signal=(c == NCH - 1))
                P.op("dve", lambda e: e.tensor_tensor(out=XNN[0][rb, :, :], in0=PS[bl][rb, 0:TB].rearrange("p (c t) -> p c t", t=CH),
                                                      in1=CST[rb, C_MASKL:C_MASKL + 64].unsqueeze(1).to_broadcast([64, NCH, CH]), op=ALU.mult),
                     reads=[("ps", bl), "CST"], writes=[("XNN", 0, q)])
                yield
                ba0, bb0 = sbank(), sbank()
                for c in range(NCH):
                    P.op("pe", lambda e, c=c: e.matmul(PS[ba0][rb, cc(c)], lhsT=XNN[0][rb, c, :], rhs=A4[q][rb, c, 0:64], start=True, stop=True),
                         reads=[("XNN", 0, q), ("A4", q)], writes=[("ps", ba0)], signal=(c == NCH - 1))
                for c in range(NCH):
                    P.op("pe", lambda e, c=c: e.matmul(PS[bb0][rb, cc(c)], lhsT=A4[q][rb, c, 0:64], rhs=XNN[0][rb, c, :], start=True, stop=True),
                         reads=[("XNN", 0, q), ("A4", q)], writes=[("ps", bb0)], signal=(c == NCH - 1))
                P.op("act", lambda e: e.copy(out=RR[1][rb, :, 64:128], in_=PS[ba0][rb, 0:TB].rearrange("p (c t) -> p c t", t=CH)),
                     reads=[("ps", ba0)], writes=[("RR", 1, q)])
                P.op("dve", lambda e: e.tensor_copy(out=XNN[1][rb, :, :], in_=PS[bb0][rb, 0:TB].rearrange("p (c t) -> p c t", t=CH)),
                     reads=[("ps", bb0)], writes=[("XNN", 1, q)])
                P.op("pool", lambda e: e.tensor_tensor(out=RR[1][rb, :, 0:64], in0=A4[q][rb, :, 0:64],
                                                       in1=IDB[rb, rb].unsqueeze(1).to_broadcast([64, NCH, CH]), op=ALU.add),
                     reads=[("A4", q), "IDB"], writes=[("RR", 1, q)])
                yield
                def level(lvl):
                    cur, nxt = lvl % 2, (lvl + 1) % 2
                    bas = [sbank(), sbank()]
                    bbn = sbank()
                    for c in range(NCH):
                        P.op("pe", lambda e, c=c: e.matmul(PS[bas[c // 4]][rb, (c % 4) * 128:(c % 4 + 1) * 128], lhsT=XNN[cur][rb, c, :],
                                                          rhs=RR[cur][rb, c, :], start=True, stop=True),
                             reads=[("XNN", cur, q), ("RR", cur, q)], writes=[("ps", bas[c // 4])], signal=(c % 4 == 3))
                    for c in range(NCH):
                        P.op("pe", lambda e, c=c: e.matmul(PS[bbn][rb, cc(c)], lhsT=RR[cur][rb, c, 64:128], rhs=XNN[cur][rb, c, :],
                                                          start=True, stop=True),
                             reads=[("XNN", cur, q), ("RR", cur, q)], writes=[("ps", bbn)], signal=(c == NCH - 1))
                    for hf in range(2):
                        pv = PS[bas[hf]][rb, 0:512].rearrange("p (c t) -> p c t", t=128)
                        P.op("dve", lambda e, hf=hf, pv=pv: e.tensor_tensor(out=RR[nxt][rb, hf * 4:(hf + 1) * 4, 0:64], in0=pv[:, :, 0:64],
                                                                           in1=RR[cur][rb, hf * 4:(hf + 1) * 4, 0:64], op=ALU.add),
                             reads=[("ps", bas[hf]), ("RR", cur, q)], writes=[("RR", nxt, q)])
                        P.op("act", lambda e, hf=hf, pv=pv: e.copy(out=RR[nxt][rb, hf * 4:(hf + 1) * 4, 64:128], in_=pv[:, :, 64:128]),
                             reads=[("ps", bas[hf])], writes=[("RR", nxt, q)])
                    P.op("act", lambda e: e.copy(out=XNN[nxt][rb, :, :], in_=PS[bbn][rb, 0:TB].rearrange("p (c t) -> p c t", t=CH)),
                         reads=[("ps", bbn)], writes=[("XNN", nxt, q)])
                for lvl in range(1, 5):
                    level(lvl)
                    yield
                yield
                bf = sbank()
                for c in range(NCH):
                    P.op("pe", lambda e, c=c: e.matmul(PS[bf][rb, cc(c)], lhsT=XNN[1][rb, c, :], rhs=RR[1][rb, c, 0:64], start=True, stop=True),
                         reads=[("XNN", 1, q), ("RR", 1, q)], writes=[("ps", bf)], signal=(c == NCH - 1))
                P.op("dve", lambda e: e.tensor_tensor(out=RR[0][rb, :, 0:64], in0=PS[bf][rb, 0:TB].rearrange("p (c t) -> p c t", t=CH),
                                                      in1=RR[1][rb, :, 0:64], op=ALU.add),
                     reads=[("ps", bf), ("RR", 1, q)], writes=[("RR", 0, q)])
                yield
                bt = sbank()
                for c in range(NCH):
                    P.op("pe", lambda e, c=c: e.matmul(PS[bt][rb, cc(c)], lhsT=AR[rk, c, 0:64], rhs=IDB[rk, rk], start=True, stop=True),
                         reads=[gk(0), "IDB"], writes=[("ps", bt)], signal=(c == NCH - 1))
                P.op("act", lambda e: e.copy(out=XNN[0][rb, :, :], in_=PS[bt][rb, 0:TB].rearrange("p (c t) -> p c t", t=CH)),
                     reads=[("ps", bt)], writes=[("XNN", 0, q)])
                yield
                bw_ = sbank()
                for c in range(NCH):
                    P.op("pe", lambda e, c=c: e.matmul(PS[bw_][rk, cc(c)], lhsT=XNN[0][rb, c, :], rhs=RR[0][rb, c, 0:64], start=True, stop=True),
                         reads=[("XNN", 0, q), ("RR", 0, q)], writes=[("ps", bw_)], signal=(c == NCH - 1))
                P.op("dve", lambda e: e.tensor_copy(out=WT[rk, :, :], in_=PS[bw_][rk, 0:TB].rearrange("p (c t) -> p c t", t=CH)),
                     reads=[("ps", bw_)], writes=[("WT", q)])
                yield
                b0s = [sbank(), sbank()]
                for c in range(NCH):
                    P.op("pe", lambda e, c=c: e.matmul(PS[b0s[c // 4]][rb, (c % 4) * 128:(c % 4 + 1) * 128], lhsT=A4[q][rk, c, 0:64],
                                                      rhs=VTq[q][rk, c, :], start=True, stop=True),
                         reads=[("A4", q), ("VT", q, q)], writes=[("ps", b0s[c // 4])], signal=(c % 4 == 3))
                for hf in range(2):
                    P.op("act", lambda e, hf=hf: e.copy(out=RR[1][rb, hf * 4:(hf + 1) * 4, :],
                                                        in_=PS[b0s[hf]][rb, 0:512].rearrange("p (c t) -> p c t", t=128)),
                         reads=[("ps", b0s[hf])], writes=[("RR", 1, q)])
                yield
                u0s = [sbank(), sbank()]
                for c in range(NCH):
                    P.op("pe", lambda e, c=c: e.matmul(PS[u0s[c // 4]][rb, (c % 4) * 128:(c % 4 + 1) * 128], lhsT=RR[0][rb, c, 0:64],
                                                      rhs=RR[1][rb, c, :], start=True, stop=True),
                         reads=[("RR", 0, q), ("RR", 1, q)], writes=[("ps", u0s[c // 4])], signal=(c % 4 == 3))
                for hf in range(2):
                    P.op("dve", lambda e, hf=hf: e.tensor_copy(out=U0[rb, hf * 4:(hf + 1) * 4, :],
                                                              in_=PS[u0s[hf]][rb, 0:512].rearrange("p (c t) -> p c t", t=128)),
                         reads=[("ps", u0s[hf])], writes=[("U0", q)])
            gens = [head_setup(0), head_setup(1)]
            while gens:
                for g_ in list(gens):
                    try:
                        next(g_)
                    except StopIteration:
                        gens.remove(g_)

            def chain(c, q):
                if True:
                    PK, PB = 64 * q, 64 * (1 - q)
                    rk = slice(PK, PK + 64)
                    rb = slice(PB, PB + 64)
                    cs_ = slice(c * CH, (c + 1) * CH)
                    bc = sbank()
                    by = 6 + q
                    hxk, hxbk = ("HX", hp, q), ("HXB", hp, q)
                    P.op("pe", lambda e: e.matmul(PS[bc][rb, 0:128], lhsT=WT[rk, c, :], rhs=HXB[hp][rk, :], start=True, stop=True),
                         reads=[("WT", q), hxbk], writes=[("ps", bc)])
                    P.op("dve", lambda e: e.tensor_tensor(out=VTq[q][rb, c, :], in0=PS[bc][rb, 0:128], in1=U0[rb, c, :], op=ALU.add),
                         reads=[("ps", bc), ("U0", q)], writes=[("VT", q, 1 - q)])
                    P.op("pe", lambda e: e.matmul(PS[by][:, cs_], lhsT=HXB[hp][rk, :], rhs=AR[rk, c, 64:128], start=True, stop=False),
                         reads=[gk(1), hxbk], writes=[("ps", by)], signal=False)
                    P.op("pe", lambda e: e.matmul(PS[by][:, cs_], lhsT=VTq[q][:, c, :], rhs=A4[q][:, c, 64:128], start=False, stop=True),
                         reads=[("A4", q), ("VT", q, 0), ("VT", q, 1)], writes=[("ps", by)])
                    P.op("pe", lambda e: e.matmul(PS[bc][rk, 256:384], lhsT=BKq[q][:, c, :], rhs=VTq[q][:, c, :], start=True, stop=True),
                         reads=[("BK", q), ("VT", q, 0), ("VT", q, 1)], writes=[("ps", bc)])
                    P.op("dve", lambda e: e.scalar_tensor_tensor(out=HXB[hp][rk, :], in0=HX[hp][rk, :], scalar=GAM[rk, c:c + 1],
                                                                 in1=PS[bc][rk, 256:384], op0=ALU.mult, op1=ALU.add),
                         reads=[hxk, gk(9), ("ps", bc)], writes=[hxbk])
                    P.op("dve", lambda e: e.scalar_tensor_tensor(out=HX[hp][rk, :], in0=HX[hp][rk, :], scalar=GAM[rk, c:c + 1],
                                                                 in1=PS[bc][rk, 256:384], op0=ALU.mult, op1=ALU.add),
                         reads=[hxk, gk(9), ("ps", bc)], writes=[hxk])
            for c in range(NCH):
                for q in range(2):
                    chain(c, q)
                    for _ in range(nflush):
                        if D:
                            kind, aa, kk_ = D.pop(0)
                            (Prog.op if kind == "op" else Prog.dma)(P, *aa, **kk_)
            while D:
                kind, aa, kk_ = D.pop(0)
                (Prog.op if kind == "op" else Prog.dma)(P, *aa, **kk_)
            for q in range(2):
                h = 2 * hp + q
                by = 6 + q
                ye = YES[q]
                P.op("act", lambda e, by=by, ye=ye: e.copy(out=ye[:], in_=PS[by][:, 0:TB]), reads=[("ps", by)], writes=[("YES", q)])
                P.dma("sp", lambda e, h=h, ye=ye: e.dma_start(out=ye_scr[h * 128:(h + 1) * 128, tok], in_=ye[:]),
                      reads=[("YES", q)], writes=[("yes", h, t)], dsem=f"yes{q}")
                if t == 0 and hp == 0:
                    dbg[f"YE{q}"] = (ye, ("YES", q))

        hp_prep(0)
        for hp in range(4):
            D = []
            if hp < 3:
                P.op = lambda *aa, **kk_: D.append(("op", aa, kk_))
                P.dma = lambda *aa, **kk_: D.append(("dma", aa, kk_))
                try:
                    hp_prep(hp + 1)
                finally:
                    del P.op
                    del P.dma
            hp_rest(hp, D)

    if STAGE >= 2:
        mix_halo()
    for t in range(NT):
        xt = XT[0]
        xkey = ("XT", 0)
        P.dma("sp", lambda e, t=t, xt=xt: e.dma_start(out=xt[:], in_=xTv[:, :, 2 + t * TB:2 + (t + 1) * TB]),
              writes=[xkey], dsem="xt0")
        ffn("ffn1", xt, xkey, TB, 0, 0, 8)
        if STAGE == 1:
            P.dma("sp", lambda e, t=t, xt=xt: e.dma_start(out=outTv[:, :, t * TB:(t + 1) * TB], in_=xt[:]),
                  reads=[xkey], dsem="xo0")
            continue
        P.dma("sp", lambda e, t=t, xt=xt: e.dma_start(out=h_scrv[:, :, t * TB:(t + 1) * TB], in_=xt[:]),
              reads=[xkey], writes=[("hs", t)], dsem="hs")
        mixer_tile(t, xt, xkey)
        if DEBUG and t == 0:
            for name, (tile_, key) in dbg.items():
                dd = dt("dbg_" + name, [128, TB], F32, kind="ExternalOutput").ap()
                P.dma("sp", lambda e, dd=dd, tile_=tile_: e.dma_start(out=dd[:, :], in_=tile_[:]), reads=[key], dsem="dbg_" + name)
            dd = dt("dbg_HX", [128, 128], F32, kind="ExternalOutput").ap()
            P.dma("sp", lambda e, dd=dd: e.dma_start(out=dd[:, :], in_=HX[0][:]), reads=[("HX", 0, 0), ("HX", 0, 1)], dsem="dbg_HX")
        if DEBUG and t == 0 and STAGE == 2:
            break


    FX = sb("FX", [128, 8, 128])
    P.op("pool", lambda e: e.memset(FX[:], 0.0), writes=["FX"])
    for q in range(2):
        rk = slice(64 * q, 64 * q + 64)
        P.op("pool", lambda e, q=q, rk=rk: e.tensor_copy(
            out=FX[rk, q::2, 64 * q:64 * q + 64], in_=ident[rk, rk].unsqueeze(1).to_broadcast([64, 4, 64])),
            reads=["CST", "FX"], writes=["FX"])
    if STAGE >= 3 and (NRUN == NCORES or FAKE_CC):
        for hp in range(4):
            P.dma("sp", lambda e, hp=hp: e.dma_start(out=hx_loc[:, hp * 128:(hp + 1) * 128], in_=HX[hp][:]),
                  reads=[("HX", hp, 0), ("HX", hp, 1)], writes=["hx_loc"], dsem="hxl")
        ccsem = st.enter_context(nc.semaphore("ccsem"))
        ccd = sb("ccdummy", [128, 2])

        def cc(e):
            ins = e.collective_compute("AllGather", ALU.bypass, replica_groups=[list(range(NCORES))],
                                       ins=[hx_loc.tensor.ap().opt()], outs=[hx_all.tensor.ap().opt()])
            ins.then_inc(ccsem)
            e.wait_ge(ccsem, 1)
            return e.memset(ccd[:], 0.0)
        if FAKE_CC:
            P.dma("sp", lambda e: [e.dma_start(out=hx_all[j * 128:(j + 1) * 128, :], in_=hx_loc[:, :]) for j in range(NCORES)],
                  reads=["hx_loc"], writes=["hx_all"], dsem="fcc", nd=NCORES)
        else:
            P.op("POOL!", cc, reads=["hx_loc"], writes=["hx_all"])

        def ld_hxa(e):
            r = []
            for j in range(NCORES):
                r.append(e.dma_start(out=YSk[j][0:64, :], in_=hx_all[j * 128 + 64:j * 128 + 128, :]))
                r.append(e.dma_start(out=YSk[j][64:128, :], in_=hx_all[j * 128:j * 128 + 64, :]))
            return r
        P.dma("sp", ld_hxa, reads=["hx_all"], writes=[("YS", j) for j in range(KC)], dsem="hxa", nd=16)
        HC, SS = TT[0], TT[1]
        HCB = G[:, 20, 0:256]
        P.op("pool", lambda e: e.memset(SS[:, 0:256], 0.0), writes=[("TT", 1)])
        for j in range(NCORES - 1):
            P.op("pool", lambda e, j=j: e.tensor_copy(out=G[:, 8 + j, :], in_=YSk[j][:]), reads=[("YS", j)], writes=[("G", 8 + j)])
            if j > 0:
                pt = G[:, 16 + j // 2, (j % 2) * 256:(j % 2) * 256 + 256]
                for q in range(2):
                    rb = slice(64 * (1 - q), 64 * (1 - q) + 64)
                    kcol = 64 * (1 - q)
                    bq = newbank()
                    for hp in range(4):
                        P.op("pe", lambda e, j=j, hp=hp, rb=rb, kcol=kcol, bq=bq: e.matmul(
                            PS[bq][rb, hp * 64:(hp + 1) * 64], lhsT=G[rb, 8 + j, hp * 128 + kcol:hp * 128 + kcol + 64],
                            rhs=IDB[rb, rb], start=True, stop=True),
                            reads=[("G", 8 + j), "IDB"], writes=[("ps", bq)], signal=(hp == 3))
                    P.op("act", lambda e, rb=rb, bq=bq, pt=pt: e.copy(out=pt[rb, :], in_=PS[bq][rb, 0:256]),
                         reads=[("ps", bq)], writes=[("G", 16 + j // 2, 1 - q)])
            for q in range(2):
                rb = slice(64 * (1 - q), 64 * (1 - q) + 64)
                hl = YSk[j][rb, :].rearrange("p (h c) -> p h c", c=128)[:, :, 64 * q:64 * q + 64]
                hc3 = HC[rb, 0:256].rearrange("p (h c) -> p h c", c=64)
                if j == 0:
                    P.op("dve", lambda e, hl=hl, hc3=hc3: e.tensor_copy(out=hc3, in_=hl), reads=[("YS", j)], writes=[("TT", 0, 1 - q)])
                else:
                    pt = G[:, 16 + j // 2, (j % 2) * 256:(j % 2) * 256 + 256]
                    bq = newbank()
                    for hp in range(4):
                        P.op("pe", lambda e, hp=hp, rb=rb, bq=bq, pt=pt: e.matmul(
                            PS[bq][rb, hp * 64:(hp + 1) * 64], lhsT=pt[rb, hp * 64:(hp + 1) * 64],
                            rhs=HCB[rb, hp * 64:(hp + 1) * 64], start=True, stop=True),
                            reads=[("G", 16 + j // 2, 1 - q), ("G", 20, 1 - q)], writes=[("ps", bq)], signal=(hp == 3))
                    P.op("dve", lambda e, rb=rb, bq=bq, hl=hl, hc3=hc3: e.tensor_tensor(
                        out=hc3, in0=PS[bq][rb, 0:256].rearrange("p (h c) -> p h c", c=64), in1=hl, op=ALU.add),
                        reads=[("ps", bq), ("YS", j)], writes=[("TT", 0, 1 - q)])
                P.op("pool", lambda e, rb=rb: e.tensor_copy(out=HCB[rb, :], in_=HC[rb, 0:256]),
                     reads=[("TT", 0, 1 - q)], writes=[("G", 20, 1 - q)])
                P.op("dve", lambda e, rb=rb, j=j: e.scalar_tensor_tensor(
                    out=SS[rb, 0:256], in0=HC[rb, 0:256], scalar=VEC[rb, V_ONEHOT + j + 1:V_ONEHOT + j + 2],
                    in1=SS[rb, 0:256], op0=ALU.mult, op1=ALU.add),
                    reads=[("TT", 0, 1 - q), "VEC", ("TT", 1)], writes=[("TT", 1)])
        for q in range(2):
            rb = slice(64 * (1 - q), 64 * (1 - q) + 64)
            P.op("dve", lambda e, q=q, rb=rb: e.tensor_copy(
                out=FX[rb, q::2, 64 * q:64 * q + 64], in_=SS[rb, 0:256].rearrange("p (h c) -> p h c", c=64)),
                reads=[("TT", 1), "FX"], writes=["FX"])

    w_outv = w_out.rearrange("(kc p) n -> p kc n", p=128)
    yc_v = yc_scr.rearrange("(c p) n -> p c n", p=128)

    def phaseB_tile(t):
        tok = slice(t * TB, (t + 1) * TB)
        xt, xkey = XT[0], ("XT", 0)
        P.dma("sp", lambda e: e.dma_start(out=xt[:], in_=h_scrv[:, :, tok]), reads=[("hs", t)], writes=[xkey], dsem="xt0")
        P.dma("sp", lambda e: e.dma_start(out=YC[:], in_=yc_v[:, :, tok]), reads=[("ycs", t)], writes=[("YC", c) for c in range(4)], dsem="ycl")

        def hpB(hp):
            Y0, Y1, GT, BON = TT[4], TT[5], TT[6], TT[7]

            def ld(e):
                return [e.dma_start(out=Y0[:], in_=ye_scr[(2 * hp) * 128:(2 * hp + 1) * 128, tok]),
                        e.dma_start(out=Y1[:], in_=ye_scr[(2 * hp + 1) * 128:(2 * hp + 2) * 128, tok]),
                        e.dma_start(out=GT[:], in_=g_scr[hp * 128:(hp + 1) * 128, tok]),
                        e.dma_start(out=BON[:], in_=bo_scr[hp * 128:(hp + 1) * 128, tok])]
            P.dma("sp", ld, reads=[("yes", 2 * hp, t), ("yes", 2 * hp + 1, t), ("gs", hp, t), ("bs", hp, t)],
                  writes=[("TT", 4), ("TT", 5), ("TT", 6), ("TT", 7)], dsem="yel", nd=4)
            bo = newbank()
            P.op("pe", lambda e: e.matmul(PS[bo][:, 0:TB], lhsT=FX[:, 2 * hp, :], rhs=Y0[:], start=True, stop=False),
                 reads=["FX", ("TT", 4)], writes=[("ps", bo)], signal=False)
            P.op("pe", lambda e: e.matmul(PS[bo][:, 0:TB], lhsT=FX[:, 2 * hp + 1, :], rhs=Y1[:], start=False, stop=True),
                 reads=["FX", ("TT", 5)], writes=[("ps", bo)])
            OS, DD, SQ, RS = YSk[0], YSk[1], YSk[2], YSk[3]
            P.op("act", lambda e: e.copy(out=OS[:], in_=PS[bo][:, 0:TB]), reads=[("ps", bo)], writes=[("YS", 0)])
            bm = newbank()
            P.op("pe", lambda e: e.matmul(PS[bm][:, 0:TB], lhsT=bones, rhs=OS[:], start=True, stop=True),
                 reads=["CST", ("YS", 0)], writes=[("ps", bm)])
            P.op("dve", lambda e: e.scalar_tensor_tensor(out=DD[:], in0=PS[bm][:, 0:TB], scalar=-1.0 / 64, in1=OS[:],
                                                         op0=ALU.mult, op1=ALU.add),
                 reads=[("ps", bm), ("YS", 0)], writes=[("YS", 1)])
            P.op("pool", lambda e: e.tensor_tensor(out=SQ[:], in0=DD[:], in1=DD[:], op=ALU.mult), reads=[("YS", 1)], writes=[("YS", 2)])
            bv = newbank()
            P.op("pe", lambda e: e.matmul(PS[bv][:, 0:TB], lhsT=bones, rhs=SQ[:], start=True, stop=True),
                 reads=["CST", ("YS", 2)], writes=[("ps", bv)])
            rstd_from_sumsq(bv, TB, 64, 1, RS[:], ("YS", 3))
            P.op("dve", lambda e: e.tensor_tensor(out=DD[:], in0=DD[:], in1=RS[:], op=ALU.mult),
                 reads=[("YS", 1), ("YS", 3)], writes=[("YS", 1)])
            P.op("dve", lambda e: e.scalar_tensor_tensor(out=SQ[:], in0=DD[:], scalar=VEC[:, V_LNW + hp:V_LNW + hp + 1], in1=BON[:],
                                                         op0=ALU.mult, op1=ALU.add),
                 reads=[("YS", 1), "VEC", ("TT", 7), ("YS", 2)], writes=[("YS", 2)])
            P.op("pool", lambda e: e.tensor_tensor(out=G[:, hp, :], in0=SQ[:], in1=GT[:], op=ALU.mult),
                 reads=[("YS", 2), ("TT", 6)], writes=[("G", hp)])
        for hp in range(4):
            hpB(hp)

        def wo(n):
            s = load_w([(n * 256, 256)], w_outv)
            for mm_ in range(2):
                m = n * 2 + mm_
                b = newbank()
                for kc in range(KC):
                    rhs = YC[:, kc, :] if kc < 4 else G[:, kc - 4, :]
                    rkey = ("YC", kc) if kc < 4 else ("G", kc - 4)
                    P.op("pe", lambda e, b=b, kc=kc, rhs=rhs, mm_=mm_: e.matmul(
                        PS[b][:, 0:TB], lhsT=W13[s][:, kc, mm_ * 128:(mm_ + 1) * 128], rhs=rhs, start=(kc == 0), stop=(kc == KC - 1)),
                        reads=[("W13", s), rkey], writes=[("ps", b)], signal=(kc == KC - 1))
                P.op("act", lambda e, b=b, m=m: e.copy(out=YSk[m][:], in_=PS[b][:, 0:TB]), reads=[("ps", b)], writes=[("YS", m)])
        for n in range(4):
            wo(n)
        post_residual(xt, xkey, TB, 24)
        ffn("ffn2", xt, xkey, TB, 32, 48, 40)
        P.dma("sp", lambda e: e.dma_start(out=outTv[:, :, tok], in_=xt[:]), reads=[xkey], dsem="xo0")

    if STAGE >= 3:
        for t in range(NT):
            phaseB_tile(t)

    P.finalize_and_emit()
    st.close()
    return nc


_NC_CACHE = {}


def kernel(**inputs):
    x = np.asarray(inputs["x"], np.float32)[0]
    g = lambda k: np.asarray(inputs[k], np.float32)[0]
    cst = _consts()
    xTfull = np.ascontiguousarray(x.T)
    xpad = np.concatenate([np.zeros((D, 2), np.float32), xTfull], axis=1)
    lora_wa = np.ascontiguousarray(np.concatenate([g("w_up"), g("a_up")], axis=0))
    in_maps = []
    for c in range(NCORES):
        vec = np.zeros((128, NVEC), np.float32)
        vec[:, V_C:V_C + 8] = _col(g("c"), 8)
        vec[:, V_GPRE1:V_GPRE1 + 8] = _col(g("ffn1_g_pre"), 8)
        vec[:, V_GPOST1:V_GPOST1 + 8] = _col(g("ffn1_g_post"), 8)
        vec[:, V_GPRE2:V_GPRE2 + 8] = _col(g("mix_g_pre"), 8)
        vec[:, V_GPOST2:V_GPOST2 + 8] = _col(g("mix_g_post"), 8)
        vec[:, V_GPRE3:V_GPRE3 + 8] = _col(g("ffn2_g_pre"), 8)
        vec[:, V_GPOST3:V_GPOST3 + 8] = _col(g("ffn2_g_post"), 8)
        mu = g("mu_shift")
        vec[:, V_MU:V_MU + 12] = _col(mu[:1536], 12)
        vec[0:64, V_MU + 12] = mu[1536:1600]
        vec[0:96, V_MU + 13] = mu[1600:1696]
        cw = g("conv_w")
        for ch in range(4):
            for j in range(3):
                vec[:, V_CONV + ch * 3 + j] = cw[j, ch * 128:(ch + 1) * 128]
        for (off, k) in ((V_W0, "w0"), (V_A0, "a0"), (V_KK, "k_k"), (V_KA, "k_a"), (V_LNW, "ln_x_w"), (V_LNB, "ln_x_b")):
            vec[:, off:off + 4] = _col(g(k), 4)
        vec[:, V_RK:V_RK + 4] = _col(g("r_k").reshape(-1), 4)
        vec[:, V_FLAG] = 0.0 if c == 0 else 1.0
        vec[:, V_ONEHOT + c] = 1.0
        m = {
            "xT": np.ascontiguousarray(xpad[:, c * NTOK:c * NTOK + NTOK + 2]),
            "vec": vec, "cst": cst,
            "b_ada": g("b_ada").reshape(1, -1), "w_ada": g("w_ada"),
            "ffn1_w1": g("ffn1_w1"), "ffn1_w3": g("ffn1_w3"), "ffn1_w2": g("ffn1_w2"),
            "ffn2_w1": g("ffn2_w1"), "ffn2_w3": g("ffn2_w3"), "ffn2_w2": g("ffn2_w2"),
            "w_in": g("w_in"), "w_out": g("w_out"), "lora_wa": lora_wa, "lora_g": g("g_up"),
        }
        in_maps.append(m)
    if "nc" not in _NC_CACHE:
        _NC_CACHE["nc"] = build()
    nc = _NC_CACHE["nc"]
    res = run_bass_kernel_spmd(nc, in_maps[:NRUN], core_ids=list(range(NRUN)))
    _NC_CACHE["res"] = res
    outs = [np.asarray(r["outT"], np.float32) for r in res.results]
    outs = outs + [np.zeros_like(outs[0])] * (NCORES - NRUN)
    full = np.concatenate(outs, axis=1)
    return np.ascontiguousarray(full.T)[None].astype(np.float32)
```

```python
from contextlib import ExitStack
import numpy as np
import concourse.bass as bass
import concourse.mybir as mybir
from concourse.bass_utils import run_bass_kernel_spmd

F32 = mybir.dt.float32
BF16 = mybir.dt.bfloat16
ALU = mybir.AluOpType
AF = mybir.ActivationFunctionType

NCORES = 8
D = 1024
KC = 8
DFF = 2816
JC = 22
SEQ = 16384
NTOK = SEQ // NCORES
TB = 512
NT = NTOK // TB
CH = 64
NCH = TB // CH
C0 = float(np.exp(-0.5))
NORM_EPS = 1e-6
GN_EPS = 64e-5
STAGE = 3
DEBUG = False
NRUN = 8
FAKE_CC = False
POOL_TO_DVE = True

ENGS = ("pe", "act", "dve", "pool", "sp")


class Op:
    __slots__ = ("eng", "fn", "reads", "writes", "signal", "dsem", "idx", "waits", "sigval", "nd", "deps", "inc")

    def __init__(self, eng, fn, reads, writes, signal, dsem, nd):
        self.eng, self.fn, self.reads, self.writes = eng, fn, reads, writes
        self.signal, self.dsem, self.nd = signal, dsem, nd
        self.waits = []
        self.sigval = None


class Prog:
    def __init__(self, nc):
        self.nc = nc
        self.ops = {e: [] for e in ENGS}
        self.order = []
        self.last_writer = {}
        self.readers = {}
        self.dsem_count = {}

    def op(self, eng, fn, reads=(), writes=(), signal=True):
        if eng == "pool" and POOL_TO_DVE:
            eng = "dve"
        if eng == "POOL!":
            eng = "pool"
        o = Op(eng, fn, tuple(reads), tuple(writes), signal, None, 0)
        self._add(o)
        return o

    def dma(self, eng, fn, reads=(), writes=(), dsem=None, nd=1, inc=16):
        o = Op(eng, fn, tuple(reads), tuple(writes), True, dsem, nd)
        o.inc = inc
        self._add(o)
        return o

    def _add(self, o):
        o.idx = len(self.ops[o.eng])
        self.ops[o.eng].append(o)
        self.order.append(o)
        deps = []
        for k in o.reads:
            w = self.last_writer.get(k)
            if w is not None:
                deps.append((w, "raw"))
        for k in o.writes:
            w = self.last_writer.get(k)
            if w is not None:
                deps.append((w, "waw"))
            for r in self.readers.get(k, ()):
                deps.append((r, "war"))
        o.deps = deps
        for k in o.writes:
            self.last_writer[k] = o
            self.readers[k] = []
        for k in o.reads:
            self.readers.setdefault(k, []).append(o)

    def finalize_and_emit(self):
        nc = self.nc
        for e in ENGS:
            cnt = 0
            for o in self.ops[e]:
                if o.dsem is not None:
                    c = self.dsem_count.get(o.dsem, 0) + o.inc * o.nd
                    self.dsem_count[o.dsem] = c
                    o.sigval = ("d_" + o.dsem, c)
                elif o.signal:
                    cnt += 1
                    o.sigval = ("e_" + e, cnt)
            nxt = None
            for o in reversed(self.ops[e]):
                if o.dsem is None:
                    if o.signal:
                        nxt = o.sigval
                    else:
                        assert nxt is not None
                        o.sigval = nxt
        for e in ENGS:
            known = {}
            for o in self.ops[e]:
                need = {}
                for (d, kind) in o.deps:
                    if d is o:
                        continue
                    if d.eng == e and d.dsem is None:
                        if kind != "raw" or e == "pe":
                            continue
                    s, v = d.sigval
                    if known.get(s, 0) >= v:
                        continue
                    if need.get(s, 0) < v:
                        need[s] = v
                for s, v in need.items():
                    known[s] = v
                o.waits = list(need.items())
        sem_names = sorted({o.sigval[0] for o in self.order})
        with ExitStack() as st:
            sems = {n: st.enter_context(nc.semaphore(n)) for n in sem_names}
            block = st.enter_context(nc.Block())
            reg = {"pe": block.tensor, "act": block.scalar, "dve": block.vector,
                   "pool": block.gpsimd, "sp": block.sync}

            def mk(e):
                def body(eng):
                    for o in self.ops[e]:
                        for s, v in o.waits:
                            eng.wait_ge(sems[s], v)
                        r = o.fn(eng)
                        if o.dsem is not None:
                            rs = r if isinstance(r, (list, tuple)) else [r]
                            assert len(rs) == o.nd, (len(rs), o.nd)
                            for ins in rs:
                                ins.then_inc(sems[o.sigval[0]], o.inc)
                        elif o.signal:
                            r.then_inc(sems[o.sigval[0]], 1)
                return body

            final = {}
            for o in self.order:
                s, v = o.sigval
                final[s] = max(final.get(s, 0), v)
            for e in ENGS:
                if e != "sp" and self.ops[e]:
                    reg[e](mk(e))

            def sp_body(eng):
                mk("sp")(eng)
                for s, v in final.items():
                    eng.wait_ge(sems[s], v)

            reg["sp"](sp_body)
        return len(sem_names)


C_IDENT = 0
C_BONES = 128
C_ONES = 256
C_MASK4 = 384
C_MASKL = 512
C_RESET = 576
C_HINIT = 1088
C_IDB = 1216
NCONST = 1216

V_C = 0
V_GPRE1 = 8
V_GPOST1 = 16
V_GPRE2 = 24
V_GPOST2 = 32
V_GPRE3 = 40
V_GPOST3 = 48
V_MU = 56
V_CONV = 70
V_W0 = 82
V_A0 = 86
V_KK = 90
V_KA = 94
V_RK = 98
V_LNW = 102
V_LNB = 106
V_FLAG = 110
V_ONEHOT = 111
NVEC = 119


def _consts():
    c = np.zeros((128, NCONST), np.float32)
    p = np.arange(128)
    c[p, C_IDENT + p] = 1.0
    c[:, C_BONES:C_BONES + 128] = (p[:, None] // 64 == p[None, :] // 64).astype(np.float32)
    c[:, C_ONES:C_ONES + 128] = 1.0
    s = (p % 64)[:, None]
    t = np.arange(64)[None, :]
    c[:, C_MASK4:C_MASK4 + 64] = (s < t)
    c[:, C_MASK4 + 64:C_MASK4 + 128] = (s <= t)
    c[:, C_MASKL:C_MASKL + 64] = (t < s)
    r = np.ones((512,), np.float32)
    r[::64] = 0.0
    c[:, C_RESET:C_RESET + 512] = r[None, :]
    hi = np.zeros((128, 128), np.float32)
    hi[np.arange(64), 64 + np.arange(64)] = 1.0
    hi[64 + np.arange(64), np.arange(64)] = 1.0
    c[:, C_HINIT:C_HINIT + 128] = hi
    return c


def _col(v, n):
    return np.ascontiguousarray(np.asarray(v, np.float32).reshape(n, 128).T)


def build():
    nc = bass.Bass("TRN2", target_bir_lowering=False)
    dt = nc.dram_tensor
    xT = dt("xT", [D, NTOK + 2], F32, kind="ExternalInput").ap()
    vec = dt("vec", [128, NVEC], F32, kind="ExternalInput").ap()
    cst = dt("cst", [128, NCONST], F32, kind="ExternalInput").ap()
    b_ada = dt("b_ada", [1, 9 * D], F32, kind="ExternalInput").ap()
    w_ada = dt("w_ada", [36 * 128, KC * 256], F32, kind="ExternalInput").ap()
    fw = {}
    for n in ("ffn1", "ffn2"):
        fw[n] = (dt(n + "_w13", [JC * 128, KC * 256], F32, kind="ExternalInput").ap(),
                 dt(n + "_w2", [KC * 128, JC * 128], F32, kind="ExternalInput").ap())
    w_in = dt("w_in", [D, 3232], F32, kind="ExternalInput").ap()
    w_out = dt("w_out", [D, D], F32, kind="ExternalInput").ap()
    lora_wa = dt("lora_wa", [64, 512], F32, kind="ExternalInput").ap()
    lora_g = dt("lora_g", [96, 512], F32, kind="ExternalInput").ap()
    outT = dt("outT", [D, NTOK], F32, kind="ExternalOutput").ap()
    h_scr = dt("h_scr", [D, NTOK], F32).ap()
    ye_scr = dt("ye_scr", [8 * 128, NTOK], F32).ap()
    g_scr = dt("g_scr", [512, NTOK], F32).ap()
    bo_scr = dt("bo_scr", [512, NTOK], F32).ap()
    yc_scr = dt("yc_scr", [512, NTOK], BF16).ap()
    hx_loc = dt("hx_loc", [128, 512], F32).ap()
    hx_all = dt("hx_all", [NCORES * 128, 512], F32).ap()

    xTv = xT.rearrange("(kc p) n -> p kc n", p=128)
    outTv = outT.rearrange("(kc p) n -> p kc n", p=128)
    h_scrv = h_scr.rearrange("(kc p) n -> p kc n", p=128)

    st = ExitStack()
    sb = lambda name, shape, dtype=F32: st.enter_context(nc.sbuf_tensor(name, shape, dtype))
    P = Prog(nc)

    CST = sb("CST", [128, NCONST])
    VEC = sb("VEC", [128, NVEC])
    ADA = sb("ADA", [128, 72])
    DER = sb("DER", [128, 64])
    LWA = sb("LWA", [64, 512], BF16)
    LG = sb("LG", [96, 512], BF16)
    XT = [sb(f"XT{i}", [128, KC, TB]) for i in range(1)]
    U = sb("U", [128, KC, TB], BF16)
    SCR = [sb(f"SCR{i}", [128, TB]) for i in range(4)]
    SCB = [sb(f"SCB{i}", [128, TB], BF16) for i in range(2)]
    G = sb("G", [128, JC, TB], BF16)
    YSk = [sb(f"YS{m}", [128, TB]) for m in range(KC)]
    NW13 = 4
    W13 = [sb(f"W13_{i}", [128, KC, 256], BF16) for i in range(NW13)]
    NW2 = 3
    W2 = [sb(f"W2_{i}", [128, JC, 128], BF16) for i in range(NW2)]
    RSTD = sb("RSTD", [128, TB])
    PS = [st.enter_context(nc.psum_tensor(f"ps{b}", [128, 512], F32)) for b in range(8)]
    bank_ctr = [0]

    def newbank():
        b = bank_ctr[0] % 8
        bank_ctr[0] += 1
        return b

    ident = CST[:, C_IDENT:C_IDENT + 128]
    ones = CST[:, C_ONES:C_ONES + 128]
    bones = CST[:, C_BONES:C_BONES + 128]

    def vcol(off, i=0):
        return VEC[:, off + i:off + i + 1]

    P.dma("sp", lambda e: e.dma_start(out=CST[:], in_=cst[:, :]), writes=["CST"], dsem="cst")
    P.dma("sp", lambda e: e.dma_start(out=VEC[:], in_=vec[:, :]), writes=["VEC"], dsem="vec")
    P.dma("pool", lambda e: e.dma_start(out=LWA[:], in_=lora_wa[:, :]), writes=["LWA"], dsem="lwa")
    P.dma("pool", lambda e: e.dma_start(out=LG[:], in_=lora_g[:, :]), writes=["LG"], dsem="lg")

    CONDB = sb("CONDB", [128, 8], BF16)
    BROW = sb("BROW", [1, 256])
    AROW = sb("AROW", [1, 256])
    P.op("act", lambda e: e.activation(out=CONDB[:], in_=VEC[:, V_C:V_C + 8], func=AF.Silu),
         reads=["VEC"], writes=["CONDB"])
    w_adav = w_ada.rearrange("(n p) (kc c) -> n p kc c", p=128, c=256)
    w13_ctr = [0]
    w2_ctr = [0]

    def ada_chunk(n):
        s = w13_ctr[0] % NW13
        w13_ctr[0] += 1
        P.dma("pool", lambda e: [e.dma_start(out=W13[s][:, 0:4, :], in_=w_adav[n][:, 0:4, :]),
                                 e.dma_start(out=W13[s][:, 4:8, :], in_=w_adav[n][:, 4:8, :])],
              writes=[("W13", s)], dsem=f"w13_{s}", nd=2)
        P.dma("sp", lambda e: e.dma_start(out=BROW[:], in_=b_ada[:, n * 256:(n + 1) * 256]),
              writes=["BROW"], dsem="brow")
        b = newbank()
        for kc in range(KC):
            P.op("pe", lambda e, kc=kc: e.matmul(PS[b][0:1, 0:256], lhsT=CONDB[:, kc:kc + 1],
                                                 rhs=W13[s][:, kc, :], start=(kc == 0), stop=(kc == KC - 1)),
                 reads=["CONDB", ("W13", s)], writes=[("ps", b)], signal=(kc == KC - 1))
        P.op("dve", lambda e: e.tensor_tensor(out=AROW[:], in0=PS[b][0:1, 0:256], in1=BROW[:], op=ALU.add),
             reads=[("ps", b), "BROW"], writes=["AROW"])
        b2 = newbank()
        for hh in range(2):
            P.op("pe", lambda e, hh=hh: e.matmul(PS[b2][:, hh:hh + 1], lhsT=AROW[0:1, hh * 128:(hh + 1) * 128],
                                                 rhs=CST[0:1, C_ONES:C_ONES + 1], start=True, stop=True),
                 reads=["AROW", "CST"], writes=[("ps", b2)])
        P.op("dve", lambda e: e.tensor_copy(out=ADA[:, 2 * n:2 * n + 2], in_=PS[b2][:, 0:2]), reads=[("ps", b2)], writes=["ADA"])

    for n in range(12):
        ada_chunk(n)
    pending_ada = list(range(12, 36))
    def der_ops(idx):
        for (o, gpre, sc) in [((0, V_GPRE1, 8), (16, V_GPRE2, 32), (32, V_GPRE3, 56))[i] for i in idx]:
            P.op("dve", lambda e, o=o, gpre=gpre, sc=sc: e.scalar_tensor_tensor(
                out=DER[:, o:o + 8], in0=ADA[:, sc:sc + 8], scalar=1.0, in1=VEC[:, gpre:gpre + 8],
                op0=ALU.add, op1=ALU.mult), reads=["ADA", "VEC"], writes=[("DER", o)])
        for (o, gpost, gt, f) in [((8, V_GPOST1, 16, 0.5), (24, V_GPOST2, 40, 1.0), (40, V_GPOST3, 64, 0.5))[i] for i in idx]:
            P.op("dve", lambda e, o=o, gpost=gpost, gt=gt, f=f: e.scalar_tensor_tensor(
                out=DER[:, o:o + 8], in0=ADA[:, gt:gt + 8], scalar=f, in1=VEC[:, gpost:gpost + 8],
                op0=ALU.mult, op1=ALU.mult), reads=["ADA", "VEC"], writes=[("DER", o)])
    der_ops([0])
    der_done = [False]
    P.op("dve", lambda e: e.tensor_scalar(out=DER[:, 48:56], in0=VEC[:, V_W0:V_W0 + 8], scalar1=0.5, scalar2=None,
                                          op0=ALU.mult), reads=["VEC"], writes=[("DER", 48)])
    P.op("dve", lambda e: e.tensor_scalar(out=DER[:, 56:60], in0=VEC[:, V_KA:V_KA + 4], scalar1=-1.0, scalar2=1.0,
                                          op0=ALU.mult, op1=ALU.add), reads=["VEC"], writes=[("DER", 56)])
    OMMU = sb("OMMU", [128, 14])
    P.op("dve", lambda e: e.tensor_scalar(out=OMMU[:], in0=VEC[:, V_MU:V_MU + 14], scalar1=-1.0, scalar2=1.0,
                                          op0=ALU.mult, op1=ALU.add), reads=["VEC"], writes=["OMMU"])

    EPS = sb("EPS", [128, 2])
    P.op("pool", lambda e: e.memset(EPS[:, 0:1], NORM_EPS), writes=["EPS"])
    P.op("pool", lambda e: e.memset(EPS[:, 1:2], GN_EPS), writes=["EPS"])

    def rstd_from_sumsq(b, width, nfeat, eps_col, out_ap, out_key):
        P.op("act", lambda e: e.activation(out=SCR[2][:, 0:width], in_=PS[b][:, 0:width], func=AF.Ln,
                                           scale=1.0 / nfeat, bias=EPS[:, eps_col:eps_col + 1]),
             reads=[("ps", b), "EPS"], writes=[("SCR", 2)])
        P.op("act", lambda e: e.activation(out=out_ap, in_=SCR[2][:, 0:width], func=AF.Exp, scale=-0.5),
             reads=[("SCR", 2)], writes=[out_key])

    def sumsq(src, src_key, n, width, keyf=None):
        b = newbank()
        for kc in range(n):
            s = SCR[kc % 2]
            sk = ("SCR", kc % 2)
            rk = keyf(kc) if keyf is not None else src_key
            P.op("pool", lambda e, s=s, kc=kc: e.tensor_tensor(out=s[:, 0:width], in0=src(kc), in1=src(kc), op=ALU.mult),
                 reads=[rk], writes=[sk])
            P.op("pe", lambda e, s=s, kc=kc, b=b: e.matmul(PS[b][:, 0:width], lhsT=ones, rhs=s[:, 0:width],
                                                         start=(kc == 0), stop=(kc == n - 1)),
                 reads=[sk, "CST"], writes=[("ps", b)], signal=True)
        return b

    UH = sb("UH", [128, KC, 2], BF16)
    GH = sb("GH", [128, JC, 2], BF16)
    YSH = [sb(f"YSH{m}", [128, 2]) for m in range(KC)]
    RSTDH = sb("RSTDH", [128, 2])
    SCBH = [sb(f"SCBH{i}", [128, 2], BF16) for i in range(2)]
    MAIN = dict(U=U, ukey="U", G=G, gkey=lambda j: ("G", j), YS=YSk, yskey=lambda m: ("YS", m), RSTD=RSTD, rkey="RSTD",
                SCB=SCB, sckey=lambda i: ("SCB", i))
    HALO = dict(U=UH, ukey="UH", G=GH, gkey=lambda j: ("GH", j), YS=YSH, yskey=lambda m: ("YSH", m), RSTD=RSTDH, rkey="RSTDH",
                SCB=SCBH, sckey=lambda i: ("SCBH", i))

    def modulate(xt, xkey, width, gs_off, sh_off, bs=None):
        bs = MAIN if bs is None else bs
        Ub, rs = bs["U"], bs["RSTD"]
        b = sumsq(lambda kc: xt[:, kc, 0:width], xkey, KC, width)
        rstd_from_sumsq(b, width, D, 0, rs[:, 0:width], bs["rkey"])
        for kc in range(KC):
            s = SCR[kc % 2]
            sk = ("SCR", kc % 2)
            P.op("dve", lambda e, s=s, kc=kc: e.scalar_tensor_tensor(
                out=s[:, 0:width], in0=xt[:, kc, 0:width], scalar=DER[:, gs_off + kc:gs_off + kc + 1],
                in1=rs[:, 0:width], op0=ALU.mult, op1=ALU.mult),
                reads=[xkey, ("DER", gs_off), bs["rkey"]], writes=[sk])
            P.op("pool", lambda e, s=s, kc=kc: e.tensor_scalar(
                out=Ub[:, kc, 0:width], in0=s[:, 0:width], scalar1=ADA[:, sh_off + kc:sh_off + kc + 1], scalar2=None,
                op0=ALU.add), reads=[sk, "ADA"], writes=[bs["ukey"]])

    def post_residual(xt, xkey, width, gg_off, bs=None):
        bs = MAIN if bs is None else bs
        YSb, rs = bs["YS"], bs["RSTD"]
        b = sumsq(lambda kc: YSb[kc][:, 0:width], None, KC, width, keyf=bs["yskey"])
        rstd_from_sumsq(b, width, D, 0, rs[:, 0:width], bs["rkey"])
        for kc in range(KC):
            s = SCR[kc % 2]
            sk = ("SCR", kc % 2)
            P.op("dve", lambda e, s=s, kc=kc: e.scalar_tensor_tensor(
                out=s[:, 0:width], in0=YSb[kc][:, 0:width], scalar=DER[:, gg_off + kc:gg_off + kc + 1],
                in1=rs[:, 0:width], op0=ALU.mult, op1=ALU.mult),
                reads=[bs["yskey"](kc), ("DER", gg_off), bs["rkey"]], writes=[sk])
            P.op("pool", lambda e, s=s, kc=kc: e.tensor_tensor(
                out=xt[:, kc, 0:width], in0=xt[:, kc, 0:width], in1=s[:, 0:width], op=ALU.add),
                reads=[sk, xkey], writes=[xkey])

    def ffn(name, blocks, gs_off, sh_off, gg_off):
        w13, w2 = fw[name]
        w13v = w13.rearrange("(j p) (kc c) -> j p kc c", p=128, c=256)
        w2v = w2.rearrange("(m p) (j c) -> m p j c", p=128, c=128)
        for (xt, xkey, width, bs) in blocks:
            modulate(xt, xkey, width, gs_off, sh_off, bs)
        for j in range(JC):
            for _ in range(2):
                if pending_ada:
                    ada_chunk(pending_ada.pop(0))
            s = w13_ctr[0] % NW13
            w13_ctr[0] += 1

            P.dma("pool", lambda e, j=j, s=s: [e.dma_start(out=W13[s][:, 0:4, :], in_=w13v[j][:, 0:4, :]),
                                               e.dma_start(out=W13[s][:, 4:8, :], in_=w13v[j][:, 4:8, :])],
                  writes=[("W13", s)], dsem=f"w13_{s}", nd=2)
            for (xt, xkey, width, bs) in blocks:
                b1 = newbank()
                b3 = newbank()
                Ub, Gb = bs["U"], bs["G"]
                for (bb, off) in ((b1, 0), (b3, 128)):
                    for kc in range(KC):
                        P.op("pe", lambda e, bb=bb, off=off, kc=kc, s=s, Ub=Ub, width=width: e.matmul(
                            PS[bb][:, 0:width], lhsT=W13[s][:, kc, off:off + 128], rhs=Ub[:, kc, 0:width],
                            start=(kc == 0), stop=(kc == KC - 1)),
                            reads=[("W13", s), bs["ukey"]], writes=[("ps", bb)], signal=(kc == KC - 1))
                sc = bs["SCB"][j % 2]
                sck = bs["sckey"](j % 2)
                P.op("act", lambda e, b1=b1, sc=sc, width=width: e.activation(out=sc[:, 0:width], in_=PS[b1][:, 0:width], func=AF.Silu),
                     reads=[("ps", b1)], writes=[sck])
                P.op("dve", lambda e, b3=b3, sc=sc, j=j, Gb=Gb, width=width: e.tensor_tensor(
                    out=Gb[:, j, 0:width], in0=PS[b3][:, 0:width], in1=sc[:, 0:width], op=ALU.mult),
                    reads=[("ps", b3), sck], writes=[bs["gkey"](j)])
        while pending_ada:
            ada_chunk(pending_ada.pop(0))
        if not der_done[0]:
            der_done[0] = True
            der_ops([1, 2])
        for m in range(KC):
            s = w2_ctr[0] % NW2
            w2_ctr[0] += 1
            P.dma("pool", lambda e, m=m, s=s: [e.dma_start(out=W2[s][:, 0:11, :], in_=w2v[m][:, 0:11, :]),
                                                           e.dma_start(out=W2[s][:, 11:22, :], in_=w2v[m][:, 11:22, :])],
                  writes=[("W2", s)], dsem=f"w2_{s}", nd=2)
            for (xt, xkey, width, bs) in blocks:
                b = newbank()
                Gb, YSb = bs["G"], bs["YS"]
                for j in range(JC):
                    P.op("pe", lambda e, b=b, j=j, s=s, Gb=Gb, width=width: e.matmul(
                        PS[b][:, 0:width], lhsT=W2[s][:, j, :], rhs=Gb[:, j, 0:width], start=(j == 0), stop=(j == JC - 1)),
                        reads=[("W2", s), bs["gkey"](j)], writes=[("ps", b)], signal=(j == JC - 1))
                P.op("act", lambda e, b=b, m=m, YSb=YSb, width=width: e.copy(out=YSb[m][:, 0:width], in_=PS[b][:, 0:width]),
                     reads=[("ps", b)], writes=[bs["yskey"](m)])
        for (xt, xkey, width, bs) in blocks:
            post_residual(xt, xkey, width, gg_off, bs)


    TT = [sb(f"TT{i}", [128, TB]) for i in range(10)]
    CVAL = sb("CVAL", [128, TB])
    CV = sb("CV", [128, 4, TB + 2])
    YC = sb("YC", [128, 4, TB], BF16)
    PRM = [sb(f"PRM{i}", [128, TB + 1]) for i in range(2)]
    CAR = sb("CAR", [128, 14])
    XH = sb("XH", [128, KC, 2])
    VTq = [sb(f"VT{q}", [128, NCH, 128], BF16) for q in range(2)]
    BKq = [sb(f"BK{q}", [128, NCH, 64], BF16) for q in range(2)]
    A4 = [sb(f"A4{q}", [128, NCH, 128], BF16) for q in range(2)]
    RR = [sb(f"RR{i}", [128, NCH, 128], BF16) for i in range(2)]
    XNN = [sb(f"XNN{i}", [128, NCH, 64], BF16) for i in range(2)]
    WT = sb("WT", [128, NCH, 64], BF16)
    U0 = sb("U0", [128, NCH, 128], BF16)
    HX = [sb(f"HX{hp}", [128, 128]) for hp in range(4)]
    HXB = [sb(f"HXB{hp}", [128, 128], BF16) for hp in range(4)]
    YES = [sb(f"YES{i}", [128, TB]) for i in range(2)]
    IDB = sb("IDB", [128, 128], BF16)
    ARs = [sb(f"AR{i}", [128, NCH, 128], BF16) for i in range(2)]
    KBts = [sb(f"KBt{i}", [128, NCH, 128], BF16) for i in range(2)]
    GAMs = [sb(f"GAM{i}", [128, NCH]) for i in range(2)]
    TWA = sb("TWA", [64, TB], BF16)
    SGB = sb("SGB", [96, TB], BF16)
    EPS2 = sb("EPS2", [128, 1])
    P.op("pool", lambda e: e.memset(EPS2[:], 1e-24), writes=["EPS2"])
    P.op("pool", lambda e: e.tensor_copy(out=IDB[:], in_=ident), reads=["CST"], writes=["IDB"])
    for q in range(2):
        P.op("pool", lambda e, q=q: e.memset(VTq[q][:], 0.0), writes=[("VT", q, 0), ("VT", q, 1)])
    for hp in range(4):
        P.op("pool", lambda e, hp=hp: e.tensor_copy(out=HX[hp][:], in_=CST[:, C_HINIT:C_HINIT + 128]),
             reads=["CST"], writes=[("HX", hp, 0), ("HX", hp, 1)])
        P.op("pool", lambda e, hp=hp: e.tensor_copy(out=HXB[hp][:], in_=CST[:, C_HINIT:C_HINIT + 128]),
             reads=["CST"], writes=[("HXB", hp, 0), ("HXB", hp, 1)])
    mix_banks = [0]

    def mbank():
        b = mix_banks[0] % 3
        mix_banks[0] += 1
        return b
    s_banks = [0]

    def sbank():
        b = 3 + s_banks[0] % 3
        s_banks[0] += 1
        return b

    w_inv = w_in.rearrange("(kc p) n -> p kc n", p=128)
    flagc = VEC[:, V_FLAG:V_FLAG + 1]

    def load_w(cols, srcv=None):
        srcv = w_inv if srcv is None else srcv
        s = w13_ctr[0] % NW13
        w13_ctr[0] += 1
        offs = []
        o = 0
        for (c0, n) in cols:
            offs.append((o, c0, n))
            o += n

        def ld(e):
            return [e.dma_start(out=W13[s][:, :, o:o + n], in_=srcv[:, :, c0:c0 + n]) for (o, c0, n) in offs]
        P.dma("pool", ld, writes=[("W13", s)], dsem=f"w13_{s}", nd=len(offs))
        return s

    def proj(s, off, m, width, Ub=None, ukey="U"):
        Ub = U if Ub is None else Ub
        b = mbank()
        for kc in range(KC):
            P.op("pe", lambda e, b=b, kc=kc: e.matmul(PS[b][0:m, 0:width], lhsT=W13[s][:, kc, off:off + m],
                                                    rhs=Ub[:, kc, 0:width], start=(kc == 0), stop=(kc == KC - 1)),
                 reads=[("W13", s), ukey], writes=[("ps", b)], signal=(kc == KC - 1))
        return b

    def mix_halo():
        modulate(XH, "XH", 2, 16, 24, HALO)
        for ch in range(4):
            s = load_w([(1024 + ch * 128, 128), (ch * 128, 128)])
            b = proj(s, 0, 128, 2, UH, "UH")
            P.op("act", lambda e, b=b: e.copy(out=CVAL[:, 0:2], in_=PS[b][:, 0:2]), reads=[("ps", b)], writes=["CVAL"])
            b = proj(s, 128, 128, 2, UH, "UH")
            P.op("dve", lambda e, b=b, ch=ch: e.scalar_tensor_tensor(out=CV[:, ch, TB:TB + 2], in0=PS[b][:, 0:2], scalar=flagc,
                                                                    in1=CVAL[:, 0:2], op0=ALU.mult, op1=ALU.mult),
                 reads=[("ps", b), "CVAL", "VEC"], writes=[("CV", ch)])
        for i in range(14):
            if i < 12:
                s = load_w([(1536 + i * 128, 128)])
                m = 128
            elif i == 12:
                s = load_w([(3072, 64)])
                m = 64
            else:
                s = load_w([(3136, 96)])
                m = 96
            b = proj(s, 0, m, 2, UH, "UH")
            P.op("dve", lambda e, b=b, i=i, m=m: e.tensor_scalar(out=CAR[0:m, i:i + 1], in0=PS[b][0:m, 1:2],
                                                                scalar1=VEC[0:m, V_MU + i:V_MU + i + 1], scalar2=flagc[0:m],
                                                                op0=ALU.mult, op1=ALU.mult),
                 reads=[("ps", b), "VEC"], writes=[("CAR", i)])

    prm_ctr = [0]

    def lerp_evac(b, i, m, out_ap, out_key):
        pi = prm_ctr[0] % 2
        prm_ctr[0] += 1
        pr = PRM[pi]
        pk = ("PRM", pi)
        P.op("pool", lambda e: e.tensor_copy(out=pr[0:m, 0:1], in_=CAR[0:m, i:i + 1]), reads=[("CAR", i)], writes=[pk])
        P.op("act", lambda e: e.activation(out=pr[0:m, 1:TB + 1], in_=PS[b][0:m, 0:TB], func=AF.Identity,
                                           scale=VEC[0:m, V_MU + i:V_MU + i + 1]),
             reads=[("ps", b), "VEC"], writes=[pk])
        P.op("dve", lambda e: e.scalar_tensor_tensor(out=out_ap, in0=PS[b][0:m, 0:TB], scalar=OMMU[0:m, i:i + 1],
                                                     in1=pr[0:m, 0:TB], op0=ALU.mult, op1=ALU.add),
             reads=[("ps", b), "OMMU", pk], writes=[out_key])
        P.op("pool", lambda e: e.tensor_copy(out=CAR[0:m, i:i + 1], in_=pr[0:m, TB:TB + 1]), reads=[pk], writes=[("CAR", i)])

    dbg = {}

    def mixer_tile(t, xt, xkey):
        tok = slice(t * TB, (t + 1) * TB)
        modulate(xt, xkey, TB, 16, 24)
        for ch in range(4):
            s = load_w([(1024 + ch * 128, 128), (ch * 128, 128)])
            s2 = load_w([(512 + ch * 128, 128)])
            b = proj(s, 0, 128, TB)
            P.op("act", lambda e, b=b: e.copy(out=CVAL[:], in_=PS[b][:, 0:TB]), reads=[("ps", b)], writes=["CVAL"])
            b = proj(s, 128, 128, TB)
            P.op("pool", lambda e, ch=ch: e.tensor_copy(out=CV[:, ch, 0:2], in_=CV[:, ch, TB:TB + 2]),
                 reads=[("CV", ch)], writes=[("CV", ch)])
            P.op("dve", lambda e, b=b, ch=ch: e.tensor_tensor(out=CV[:, ch, 2:TB + 2], in0=PS[b][:, 0:TB], in1=CVAL[:],
                                                             op=ALU.mult),
                 reads=[("ps", b), "CVAL"], writes=[("CV", ch)])
            acc = SCR[3]
            P.op("pool", lambda e, ch=ch: e.tensor_scalar(out=acc[:], in0=CV[:, ch, 2:TB + 2],
                                                         scalar1=VEC[:, V_CONV + ch * 3 + 2:V_CONV + ch * 3 + 3], scalar2=None,
                                                         op0=ALU.mult),
                 reads=[("CV", ch), "VEC"], writes=[("SCR", 3)])
            for (j, o) in ((1, 1), (0, 0)):
                P.op("dve", lambda e, ch=ch, j=j, o=o: e.scalar_tensor_tensor(
                    out=acc[:], in0=CV[:, ch, o:o + TB], scalar=VEC[:, V_CONV + ch * 3 + j:V_CONV + ch * 3 + j + 1],
                    in1=acc[:], op0=ALU.mult, op1=ALU.add),
                    reads=[("CV", ch), "VEC", ("SCR", 3)], writes=[("SCR", 3)])
            b = proj(s2, 0, 128, TB)
            P.op("dve", lambda e, b=b, ch=ch: e.tensor_tensor(out=YC[:, ch, :], in0=PS[b][:, 0:TB], in1=acc[:], op=ALU.mult),
                 reads=[("ps", b), ("SCR", 3)], writes=[("YC", ch)])
        P.dma("sp", lambda e: e.dma_start(out=yc_scr.rearrange("(c p) n -> p c n", p=128)[:, :, tok], in_=YC[:]),
              reads=[("YC", c) for c in range(4)], writes=[("ycs", t)], dsem="ycs")
        XWA, XG = TT[8], TT[9]
        s = load_w([(3072, 64), (3136, 96)])
        b = proj(s, 0, 64, TB)
        lerp_evac(b, 12, 64, XWA[0:64, :], ("TT", 8))
        b = proj(s, 64, 96, TB)
        lerp_evac(b, 13, 96, XG[0:96, :], ("TT", 9))
        P.op("act", lambda e: e.activation(out=TWA[0:32, :], in_=XWA[0:32, :], func=AF.Tanh), reads=[("TT", 8)], writes=["TWA0"])
        P.op("pool", lambda e: e.tensor_copy(out=TWA[32:64, :], in_=XWA[32:64, :]), reads=[("TT", 8)], writes=["TWA1"])
        P.op("act", lambda e: e.activation(out=XG[0:96, :], in_=XG[0:96, :], func=AF.Tanh, scale=0.5),
             reads=[("TT", 9)], writes=[("TT", 9)])
        P.op("pool", lambda e: e.tensor_scalar(out=SGB[:], in0=XG[0:96, :], scalar1=0.5, scalar2=0.5, op0=ALU.mult, op1=ALU.add),
             reads=[("TT", 9)], writes=["SGB"])
        def hp_bufs(hp):
            pb = hp % 2
            return (ARs[pb], KBts[pb], GAMs[pb], G[:, 4 + 8 * pb, :], G[:, 5 + 8 * pb, :], G[:, 6 + 8 * pb, :],
                    (lambda j: ("Gk", j, pb)))

        def hp_prep(hp):
            AR, KBt, GAM, BHb, KHb, XVb, gk = hp_bufs(hp)
            hcol = slice(hp * 128, (hp + 1) * 128)
            XR, XK, XV = TT[0], TT[1], TT[2]
            s = load_w([(1536 + hp * 128, 128), (2048 + hp * 128, 128)])
            s2 = load_w([(2560 + hp * 128, 128)])
            b = proj(s, 0, 128, TB)
            lerp_evac(b, hp, 128, XR[:], ("TT", 0))
            b = proj(s, 128, 128, TB)
            lerp_evac(b, 4 + hp, 128, XK[:], ("TT", 1))
            b = proj(s2, 0, 128, TB)
            lerp_evac(b, 8 + hp, 128, XV[:], ("TT", 2))
            SGW, AA, CS = YSk[0], YSk[1], YSk[2]
            bw = mbank()
            P.op("pe", lambda e, bw=bw: e.matmul(PS[bw][:, 0:TB], lhsT=LWA[0:32, hcol], rhs=TWA[0:32, :], start=True, stop=True),
                 reads=["LWA", "TWA0"], writes=[("ps", bw)])
            P.op("act", lambda e, bw=bw, hp=hp: e.activation(out=SGW[:], in_=PS[bw][:, 0:TB], func=AF.Tanh, scale=0.5,
                                                             bias=DER[:, 48 + hp:49 + hp]),
                 reads=[("ps", bw), ("DER", 48)], writes=[("YS", 0)])
            P.op("pool", lambda e: e.tensor_scalar(out=SGW[:], in0=SGW[:], scalar1=0.5, scalar2=0.5, op0=ALU.mult, op1=ALU.add),
                 reads=[("YS", 0)], writes=[("YS", 0)])
            ba = mbank()
            P.op("pe", lambda e, ba=ba: e.matmul(PS[ba][:, 0:TB], lhsT=LWA[32:64, hcol], rhs=TWA[32:64, :], start=True, stop=True),
                 reads=["LWA", "TWA1"], writes=[("ps", ba)])
            P.op("act", lambda e, ba=ba, hp=hp: e.activation(out=AA[:], in_=PS[ba][:, 0:TB], func=AF.Tanh, scale=0.5,
                                                             bias=DER[:, 52 + hp:53 + hp]),
                 reads=[("ps", ba), ("DER", 48)], writes=[("YS", 1)])
            P.op("pool", lambda e: e.tensor_scalar(out=AA[:], in0=AA[:], scalar1=0.5, scalar2=0.5, op0=ALU.mult, op1=ALU.add),
                 reads=[("YS", 1)], writes=[("YS", 1)])
            bg = mbank()
            P.op("pe", lambda e, bg=bg: e.matmul(PS[bg][:, 0:TB], lhsT=LG[0:96, hcol], rhs=SGB[0:96, :], start=True, stop=True),
                 reads=["LG", "SGB"], writes=[("ps", bg)])
            GT = TT[3]
            P.op("act", lambda e, bg=bg: e.copy(out=GT[:], in_=PS[bg][:, 0:TB]), reads=[("ps", bg)], writes=[("TT", 3)])
            P.dma("sp", lambda e, hp=hp: e.dma_start(out=g_scr[hp * 128:(hp + 1) * 128, tok], in_=GT[:]),
                  reads=[("TT", 3)], writes=[("gs", hp, t)], dsem="gts")
            P.op("dve", lambda e: e.tensor_tensor_scan(out=CS[:], data0=CST[:, C_RESET:C_RESET + TB], data1=SGW[:], initial=0.0,
                                                       op0=ALU.mult, op1=ALU.add),
                 reads=["CST", ("YS", 0)], writes=[("YS", 2)])
            E1, E2, E3, E4 = YSk[3], YSk[4], YSk[5], YSk[6]
            P.op("act", lambda e: e.activation(out=E1[:], in_=CS[:], func=AF.Exp, scale=-C0), reads=[("YS", 2)], writes=[("YS", 3)])
            P.op("act", lambda e: e.activation(out=E3[:], in_=CS[:], func=AF.Exp, scale=C0), reads=[("YS", 2)], writes=[("YS", 5)])
            P.op("pool", lambda e: e.tensor_tensor(out=E2[:], in0=CS[:], in1=SGW[:], op=ALU.subtract),
                 reads=[("YS", 2), ("YS", 0)], writes=[("YS", 4)])
            P.op("act", lambda e: e.activation(out=E2[:], in_=E2[:], func=AF.Exp, scale=-C0), reads=[("YS", 4)], writes=[("YS", 4)])
            cs3 = CS[:].rearrange("p (c t) -> p c t", t=CH)
            P.op("dve", lambda e: e.tensor_tensor(out=E4[:].rearrange("p (c t) -> p c t", t=CH),
                                                  in0=cs3[:, :, CH - 1:CH].to_broadcast([128, NCH, CH]), in1=cs3, op=ALU.subtract),
                 reads=[("YS", 2)], writes=[("YS", 6)])
            P.op("act", lambda e: e.activation(out=E4[:], in_=E4[:], func=AF.Exp, scale=-C0), reads=[("YS", 6)], writes=[("YS", 6)])
            P.op("act", lambda e: e.activation(out=GAM[:].unsqueeze(2), in_=cs3[:, :, CH - 1:CH], func=AF.Exp, scale=-C0),
                 reads=[("YS", 2)], writes=[gk(9)])
            KKr, KK_, T1 = YSk[7], TT[4], TT[5]
            P.op("pool", lambda e, hp=hp: e.tensor_scalar(out=KKr[:], in0=XK[:], scalar1=VEC[:, V_KK + hp:V_KK + hp + 1], scalar2=None,
                                                         op0=ALU.mult), reads=[("TT", 1), "VEC"], writes=[("YS", 7)])
            P.op("pool", lambda e: e.tensor_tensor(out=T1[:], in0=KKr[:], in1=KKr[:], op=ALU.mult),
                 reads=[("YS", 7)], writes=[("TT", 5)])
            bn = mbank()
            P.op("pe", lambda e, bn=bn: e.matmul(PS[bn][:, 0:TB], lhsT=bones, rhs=T1[:], start=True, stop=True),
                 reads=["CST", ("TT", 5)], writes=[("ps", bn)])
            P.op("act", lambda e, bn=bn: e.activation(out=T1[:], in_=PS[bn][:, 0:TB], func=AF.Ln, bias=EPS2[:]),
                 reads=[("ps", bn), "EPS2"], writes=[("TT", 5)])
            P.op("act", lambda e: e.activation(out=T1[:], in_=T1[:], func=AF.Exp, scale=-0.5), reads=[("TT", 5)], writes=[("TT", 5)])
            P.op("dve", lambda e: e.tensor_tensor(out=KK_[:], in0=KKr[:], in1=T1[:], op=ALU.mult),
                 reads=[("YS", 7), ("TT", 5)], writes=[("TT", 4)])
            BETA, KMOD = TT[6], TT[7]
            P.op("dve", lambda e: e.tensor_tensor(out=BETA[:], in0=KK_[:], in1=AA[:], op=ALU.mult),
                 reads=[("TT", 4), ("YS", 1)], writes=[("TT", 6)])
            P.op("dve", lambda e, hp=hp: e.tensor_scalar(out=T1[:], in0=AA[:], scalar1=VEC[:, V_KA + hp:V_KA + hp + 1],
                                                        scalar2=DER[:, 56 + hp:57 + hp], op0=ALU.mult, op1=ALU.add),
                 reads=[("YS", 1), "VEC", ("DER", 56), ("TT", 5)], writes=[("TT", 5)])
            P.op("pool", lambda e: e.tensor_tensor(out=KMOD[:], in0=XK[:], in1=T1[:], op=ALU.mult),
                 reads=[("TT", 1), ("TT", 5)], writes=[("TT", 7)])
            P.op("dve", lambda e, hp=hp: e.scalar_tensor_tensor(out=T1[:], in0=XR[:], scalar=VEC[:, V_RK + hp:V_RK + hp + 1],
                                                               in1=KMOD[:], op0=ALU.mult, op1=ALU.mult),
                 reads=[("TT", 0), ("TT", 7), "VEC", ("TT", 5)], writes=[("TT", 5)])
            bb = mbank()
            P.op("pe", lambda e, bb=bb: e.matmul(PS[bb][:, 0:TB], lhsT=bones, rhs=T1[:], start=True, stop=True),
                 reads=["CST", ("TT", 5)], writes=[("ps", bb)])
            BON = KKr
            P.op("dve", lambda e, bb=bb: e.tensor_tensor(out=BON[:], in0=PS[bb][:, 0:TB], in1=XV[:], op=ALU.mult),
                 reads=[("ps", bb), ("TT", 2), ("YS", 7)], writes=[("YS", 7)])
            P.op("pool", lambda e, hp=hp: e.tensor_scalar(out=BON[:], in0=BON[:], scalar1=VEC[:, V_LNB + hp:V_LNB + hp + 1], scalar2=None,
                                                         op0=ALU.add), reads=[("YS", 7), "VEC"], writes=[("YS", 7)])
            P.dma("sp", lambda e, hp=hp: e.dma_start(out=bo_scr[hp * 128:(hp + 1) * 128, tok], in_=BON[:]),
                  reads=[("YS", 7)], writes=[("bs", hp, t)], dsem="bos")
            v3 = lambda ap: ap.rearrange("p (c t) -> p c t", t=CH)
            P.op("dve", lambda e: e.scalar_tensor_tensor(out=AR[:, :, 0:64], in0=v3(KK_[:]), scalar=-1.0, in1=v3(E2[:]), op0=ALU.mult, op1=ALU.mult),
                 reads=[("TT", 4), ("YS", 4)], writes=[gk(0)])
            P.op("pool", lambda e: e.tensor_tensor(out=AR[:, :, 64:128], in0=v3(XR[:]), in1=v3(E1[:]), op=ALU.mult),
                 reads=[("TT", 0), ("YS", 3)], writes=[gk(1)])
            for q in range(2):
                rows = slice(64 * q, 64 * q + 64)
                ks, bs = (0, 1) if q == 0 else (1, 0)
                P.op("dve", lambda e, rows=rows, ks=ks: e.tensor_tensor(out=KBt[rows, :, ks * 64:ks * 64 + 64], in0=v3(KMOD[rows, :]), in1=v3(E3[rows, :]), op=ALU.mult),
                     reads=[("TT", 7), ("YS", 5)], writes=[gk(2 + ks)])
                P.op("pool", lambda e, rows=rows, bs=bs: e.tensor_tensor(out=KBt[rows, :, bs * 64:bs * 64 + 64], in0=v3(BETA[rows, :]), in1=v3(E3[rows, :]), op=ALU.mult),
                     reads=[("TT", 6), ("YS", 5)], writes=[gk(2 + bs)])
            P.op("dve", lambda e: e.tensor_tensor(out=BHb, in0=BETA[:], in1=E4[:], op=ALU.mult),
                 reads=[("TT", 6), ("YS", 6)], writes=[gk(4)])
            P.op("pool", lambda e: e.tensor_tensor(out=KHb, in0=KMOD[:], in1=E4[:], op=ALU.mult),
                 reads=[("TT", 7), ("YS", 6)], writes=[gk(5)])
            P.op("pool", lambda e: e.tensor_copy(out=XVb, in_=XV[:]), reads=[("TT", 2)], writes=[gk(6)])
            if t == 0 and hp == 0:
                dbg.update(XR=(XR, ("TT", 0)), XK=(XK, ("TT", 1)), XV=(XV, ("TT", 2)), AA=(AA, ("YS", 1)),
                           SGW=(SGW, ("YS", 0)), KK=(KK_, ("TT", 4)), CS=(CS, ("YS", 2)))

        def hp_rest(hp, D):
            AR, KBt, GAM, BHb, KHb, XVb, gk = hp_bufs(hp)
            nflush = (len(D) + 15) // 16

            def head_setup(q):
                PK, PB = 64 * q, 64 * (1 - q)
                rk = slice(PK, PK + 64)
                rb = slice(PB, PB + 64)
                vcol = slice(0, 64) if q == 0 else slice(64, 128)
                sbeta = 1 if q == 0 else 0
                cc = lambda c: slice(c * CH, (c + 1) * CH)
                bv = sbank()
                for c in range(NCH):
                    P.op("pe", lambda e, c=c: e.matmul(PS[bv][rk, cc(c)], lhsT=XVb[rk, cc(c)], rhs=IDB[rk, rk], start=True, stop=True),
                         reads=[gk(6), "IDB"], writes=[("ps", bv)], signal=(c == NCH - 1))
                P.op("act", lambda e: e.copy(out=VTq[q][rk, :, vcol], in_=PS[bv][rk, 0:TB].rearrange("p (c t) -> p c t", t=CH)),
                     reads=[("ps", bv)], writes=[("VT", q, q)])
                yield
                bk = sbank()
                for c in range(NCH):
                    P.op("pe", lambda e, c=c: e.matmul(PS[bk][rb, cc(c)], lhsT=BHb[rk, cc(c)], rhs=IDB[rk, rk], start=True, stop=True),
                         reads=[gk(4), "IDB"], writes=[("ps", bk)], signal=False)
                    P.op("pe", lambda e, c=c: e.matmul(PS[bk][rk, cc(c)], lhsT=KHb[rk, cc(c)], rhs=IDB[rk, rk], start=True, stop=True),
                         reads=[gk(5), "IDB"], writes=[("ps", bk)], signal=(c == NCH - 1))
                P.op("dve", lambda e: e.tensor_copy(out=BKq[q][:], in_=PS[bk][:, 0:TB].rearrange("p (c t) -> p c t", t=CH)),
                     reads=[("ps", bk)], writes=[("BK", q)])
                yield
                for c4 in range(2):
                    b4 = sbank()
                    for ci in range(4):
                        c = c4 * 4 + ci
                        P.op("pe", lambda e, c=c, ci=ci, b4=b4: e.matmul(PS[b4][:, ci * 128:(ci + 1) * 128], lhsT=KBt[rk, c, :],
                                                                        rhs=AR[rk, c, :], start=True, stop=True),
                             reads=[gk(0), gk(1), gk(2), gk(3)], writes=[("ps", b4)], signal=(ci == 3))
                    P.op("dve", lambda e, c4=c4, b4=b4: e.tensor_tensor(
                        out=A4[q][:, c4 * 4:(c4 + 1) * 4, :], in0=PS[b4][:, 0:512].rearrange("p (c t) -> p c t", t=128),
                        in1=CST[:, C_MASK4:C_MASK4 + 128].unsqueeze(1).to_broadcast([128, 4, 128]), op=ALU.mult),
                        reads=[("ps", b4), "CST"], writes=[("A4", q)])
                yield
                bl = sbank()
                for c in range(NCH):
                    P.op("pe", lambda e, c=c: e.matmul(PS[bl][rb, cc(c)], lhsT=AR[rk, c, 0:64], rhs=KBt[rk, c, sbeta * 64:sbeta * 64 + 64],
                                                      start=True, stop=True),
                         reads=[gk(0), gk(2 + sbeta)], writes=[("ps", bl)], signal=(c == NCH - 1))
                P.op("dve", lambda e: e.tensor_tensor(out=XNN[0][rb, :, :], in0=PS[bl][rb, 0:TB].rearrange("p (c t) -> p c t", t=CH),
                                                      in1=CST[rb, C_MASKL:C_MASKL + 64].unsqueeze(1).to_broadcast([64, NCH, CH]), op=ALU.mult),
                     reads=[("ps", bl), "CST"], writes=[("XNN", 0, q)])
                yield
                ba0, bb0 = sbank(), sbank()
                for c in range(NCH):
                    P.op("pe", lambda e, c=c: e.matmul(PS[ba0][rb, cc(c)], lhsT=XNN[0][rb, c, :], rhs=A4[q][rb, c, 0:64], start=True, stop=True),
                         reads=[("XNN", 0, q), ("A4", q)], writes=[("ps", ba0)], signal=(c == NCH - 1))
                for c in range(NCH):
                    P.op("pe", lambda e, c=c: e.matmul(PS[bb0][rb, cc(c)], lhsT=A4[q][rb, c, 0:64], rhs=XNN[0][rb, c, :], start=True, stop=True),
                         reads=[("XNN", 0, q), ("A4", q)], writes=[("ps", bb0)], signal=(c == NCH - 1))
                P.op("act", lambda e: e.copy(out=RR[1][rb, :, 64:128], in_=PS[ba0][rb, 0:TB].rearrange("p (c t) -> p c t", t=CH)),
                     reads=[("ps", ba0)], writes=[("RR", 1, q)])
                P.op("dve", lambda e: e.tensor_copy(out=XNN[1][rb, :, :], in_=PS[bb0][rb, 0:TB].rearrange("p (c t) -> p c t", t=CH)),
                     reads=[("ps", bb0)], writes=[("XNN", 1, q)])
                P.op("pool", lambda e: e.tensor_tensor(out=RR[1][rb, :, 0:64], in0=A4[q][rb, :, 0:64],
                                                       in1=IDB[rb, rb].unsqueeze(1).to_broadcast([64, NCH, CH]), op=ALU.add),
                     reads=[("A4", q), "IDB"], writes=[("RR", 1, q)])
                yield
                def level(lvl):
                    cur, nxt = lvl % 2, (lvl + 1) % 2
                    bas = [sbank(), sbank()]
                    bbn = sbank()
                    for c in range(NCH):
                        P.op("pe", lambda e, c=c: e.matmul(PS[bas[c // 4]][rb, (c % 4) * 128:(c % 4 + 1) * 128], lhsT=XNN[cur][rb, c, :],
                                                          rhs=RR[cur][rb, c, :], start=True, stop=True),
                             reads=[("XNN", cur, q), ("RR", cur, q)], writes=[("ps", bas[c // 4])], signal=(c % 4 == 3))
                    for c in range(NCH):
                        P.op("pe", lambda e, c=c: e.matmul(PS[bbn][rb, cc(c)], lhsT=RR[cur][rb, c, 64:128], rhs=XNN[cur][rb, c, :],
                                                          start=True, stop=True),
                             reads=[("XNN", cur, q), ("RR", cur, q)], writes=[("ps", bbn)], signal=(c == NCH - 1))
                    for hf in range(2):
                        pv = PS[bas[hf]][rb, 0:512].rearrange("p (c t) -> p c t", t=128)
                        P.op("dve", lambda e, hf=hf, pv=pv: e.tensor_tensor(out=RR[nxt][rb, hf * 4:(hf + 1) * 4, 0:64], in0=pv[:, :, 0:64],
                                                                           in1=RR[cur][rb, hf * 4:(hf + 1) * 4, 0:64], op=ALU.add),
                             reads=[("ps", bas[hf]), ("RR", cur, q)], writes=[("RR", nxt, q)])
                        P.op("act", lambda e, hf=hf, pv=pv: e.copy(out=RR[nxt][rb, hf * 4:(hf + 1) * 4, 64:128], in_=pv[:, :, 64:128]),
                             reads=[("ps", bas[hf])], writes=[("RR", nxt, q)])
                    P.op("act", lambda e: e.copy(out=XNN[nxt][rb, :, :], in_=PS[bbn][rb, 0:TB].rearrange("p (c t) -> p c t", t=CH)),
                         reads=[("ps", bbn)], writes=[("XNN", nxt, q)])
                for lvl in range(1, 5):
                    level(lvl)
                    yield
                yield
                bf = sbank()
                for c in range(NCH):
                    P.op("pe", lambda e, c=c: e.matmul(PS[bf][rb, cc(c)], lhsT=XNN[1][rb, c, :], rhs=RR[1][rb, c, 0:64], start=True, stop=True),
                         reads=[("XNN", 1, q), ("RR", 1, q)], writes=[("ps", bf)], signal=(c == NCH - 1))
                P.op("dve", lambda e: e.tensor_tensor(out=RR[0][rb, :, 0:64], in0=PS[bf][rb, 0:TB].rearrange("p (c t) -> p c t", t=CH),
                                                      in1=RR[1][rb, :, 0:64], op=ALU.add),
                     reads=[("ps", bf), ("RR", 1, q)], writes=[("RR", 0, q)])
                yield
                bt = sbank()
                for c in range(NCH):
                    P.op("pe", lambda e, c=c: e.matmul(PS[bt][rb, cc(c)], lhsT=AR[rk, c, 0:64], rhs=IDB[rk, rk], start=True, stop=True),
                         reads=[gk(0), "IDB"], writes=[("ps", bt)], signal=(c == NCH - 1))
                P.op("act", lambda e: e.copy(out=XNN[0][rb, :, :], in_=PS[bt][rb, 0:TB].rearrange("p (c t) -> p c t", t=CH)),
                     reads=[("ps", bt)], writes=[("XNN", 0, q)])
                yield
                bw_ = sbank()
                for c in range(NCH):
                    P.op("pe", lambda e, c=c: e.matmul(PS[bw_][rk, cc(c)], lhsT=XNN[0][rb, c, :], rhs=RR[0][rb, c, 0:64], start=True, stop=True),
                         reads=[("XNN", 0, q), ("RR", 0, q)], writes=[("ps", bw_)], signal=(c == NCH - 1))
                P.op("dve", lambda e: e.tensor_copy(out=WT[rk, :, :], in_=PS[bw_][rk, 0:TB].rearrange("p (c t) -> p c t", t=CH)),
                     reads=[("ps", bw_)], writes=[("WT", q)])
                yield
                b0s = [sbank(), sbank()]
                for c in range(NCH):
                    P.op("pe", lambda e, c=c: e.matmul(PS[b0s[c // 4]][rb, (c % 4) * 128:(c % 4 + 1) * 128], lhsT=A4[q][rk, c, 0:64],
                                                      rhs=VTq[q][rk, c, :], start=True, stop=True),
                         reads=[("A4", q), ("VT", q, q)], writes=[("ps", b0s[c // 4])], signal=(c % 4 == 3))
                for hf in range(2):
                    P.op("act", lambda e, hf=hf: e.copy(out=RR[1][rb, hf * 4:(hf + 1) * 4, :],
                                                        in_=PS[b0s[hf]][rb, 0:512].rearrange("p (c t) -> p c t", t=128)),
                         reads=[("ps", b0s[hf])], writes=[("RR", 1, q)])
                yield
                u0s = [sbank(), sbank()]
                for c in range(NCH):
                    P.op("pe", lambda e, c=c: e.matmul(PS[u0s[c // 4]][rb, (c % 4) * 128:(c % 4 + 1) * 128], lhsT=RR[0][rb, c, 0:64],
                                                      rhs=RR[1][rb, c, :], start=True, stop=True),
                         reads=[("RR", 0, q), ("RR", 1, q)], writes=[("ps", u0s[c // 4])], signal=(c % 4 == 3))
                for hf in range(2):
                    P.op("dve", lambda e, hf=hf: e.tensor_copy(out=U0[rb, hf * 4:(hf + 1) * 4, :],
                                                              in_=PS[u0s[hf]][rb, 0:512].rearrange("p (c t) -> p c t", t=128)),
                         reads=[("ps", u0s[hf])], writes=[("U0", q)])
            gens = [head_setup(0), head_setup(1)]
            while gens:
                for g_ in list(gens):
                    try:
                        next(g_)
                    except StopIteration:
                        gens.remove(g_)

            def chain(c, q):
                if True:
                    PK, PB = 64 * q, 64 * (1 - q)
                    rk = slice(PK, PK + 64)
                    rb = slice(PB, PB + 64)
                    cs_ = slice(c * CH, (c + 1) * CH)
                    bc = sbank()
                    by = 6 + q
                    hxk, hxbk = ("HX", hp, q), ("HXB", hp, q)
                    P.op("pe", lambda e: e.matmul(PS[bc][rb, 0:128], lhsT=WT[rk, c, :], rhs=HXB[hp][rk, :], start=True, stop=True),
                         reads=[("WT", q), hxbk], writes=[("ps", bc)])
                    P.op("dve", lambda e: e.tensor_tensor(out=VTq[q][rb, c, :], in0=PS[bc][rb, 0:128], in1=U0[rb, c, :], op=ALU.add),
                         reads=[("ps", bc), ("U0", q)], writes=[("VT", q, 1 - q)])
                    P.op("pe", lambda e: e.matmul(PS[by][:, cs_], lhsT=HXB[hp][rk, :], rhs=AR[rk, c, 64:128], start=True, stop=False),
                         reads=[gk(1), hxbk], writes=[("ps", by)], signal=False)
                    P.op("pe", lambda e: e.matmul(PS[by][:, cs_], lhsT=VTq[q][:, c, :], rhs=A4[q][:, c, 64:128], start=False, stop=True),
                         reads=[("A4", q), ("VT", q, 0), ("VT", q, 1)], writes=[("ps", by)])
                    P.op("pe", lambda e: e.matmul(PS[bc][rk, 256:384], lhsT=BKq[q][:, c, :], rhs=VTq[q][:, c, :], start=True, stop=True),
                         reads=[("BK", q), ("VT", q, 0), ("VT", q, 1)], writes=[("ps", bc)])
                    P.op("dve", lambda e: e.scalar_tensor_tensor(out=HXB[hp][rk, :], in0=HX[hp][rk, :], scalar=GAM[rk, c:c + 1],
                                                                 in1=PS[bc][rk, 256:384], op0=ALU.mult, op1=ALU.add),
                         reads=[hxk, gk(9), ("ps", bc)], writes=[hxbk])
                    P.op("dve", lambda e: e.scalar_tensor_tensor(out=HX[hp][rk, :], in0=HX[hp][rk, :], scalar=GAM[rk, c:c + 1],
                                                                 in1=PS[bc][rk, 256:384], op0=ALU.mult, op1=ALU.add),
                         reads=[hxk, gk(9), ("ps", bc)], writes=[hxk])
            for c in range(NCH):
                for q in range(2):
                    chain(c, q)
                    for _ in range(nflush):
                        if D:
                            kind, aa, kk_ = D.pop(0)
                            (Prog.op if kind == "op" else Prog.dma)(P, *aa, **kk_)
            while D:
                kind, aa, kk_ = D.pop(0)
                (Prog.op if kind == "op" else Prog.dma)(P, *aa, **kk_)
            for q in range(2):
                h = 2 * hp + q
                by = 6 + q
                ye = YES[q]
                P.op("act", lambda e, by=by, ye=ye: e.copy(out=ye[:], in_=PS[by][:, 0:TB]), reads=[("ps", by)], writes=[("YES", q)])
                P.dma("sp", lambda e, h=h, ye=ye: e.dma_start(out=ye_scr[h * 128:(h + 1) * 128, tok], in_=ye[:]),
                      reads=[("YES", q)], writes=[("yes", h, t)], dsem=f"yes{q}")
                if t == 0 and hp == 0:
                    dbg[f"YE{q}"] = (ye, ("YES", q))

        hp_prep(0)
        for hp in range(4):
            D = []
            if hp < 3:
                P.op = lambda *aa, **kk_: D.append(("op", aa, kk_))
                P.dma = lambda *aa, **kk_: D.append(("dma", aa, kk_))
                try:
                    hp_prep(hp + 1)
                finally:
                    del P.op
                    del P.dma
            hp_rest(hp, D)

    for t in range(NT):
        xt = XT[0]
        xkey = ("XT", 0)
        P.dma("sp", lambda e, t=t, xt=xt: e.dma_start(out=xt[:], in_=xTv[:, :, 2 + t * TB:2 + (t + 1) * TB]),
              writes=[xkey], dsem="xt0")
        if t == 0 and STAGE >= 2:
            P.dma("sp", lambda e: e.dma_start(out=XH[:], in_=xTv[:, :, 0:2]), writes=["XH"], dsem="xh")
            ffn("ffn1", [(xt, xkey, TB, MAIN), (XH, "XH", 2, HALO)], 0, 0, 8)
            mix_halo()
        else:
            ffn("ffn1", [(xt, xkey, TB, MAIN)], 0, 0, 8)
        if STAGE == 1:
            P.dma("sp", lambda e, t=t, xt=xt: e.dma_start(out=outTv[:, :, t * TB:(t + 1) * TB], in_=xt[:]),
                  reads=[xkey], dsem="xo0")
            continue
        P.dma("sp", lambda e, t=t, xt=xt: e.dma_start(out=h_scrv[:, :, t * TB:(t + 1) * TB], in_=xt[:]),
              reads=[xkey], writes=[("hs", t)], dsem="hs")
        mixer_tile(t, xt, xkey)
        if DEBUG and t == 0:
            for name, (tile_, key) in dbg.items():
                dd = dt("dbg_" + name, [128, TB], F32, kind="ExternalOutput").ap()
                P.dma("sp", lambda e, dd=dd, tile_=tile_: e.dma_start(out=dd[:, :], in_=tile_[:]), reads=[key], dsem="dbg_" + name)
            dd = dt("dbg_HX", [128, 128], F32, kind="ExternalOutput").ap()
            P.dma("sp", lambda e, dd=dd: e.dma_start(out=dd[:, :], in_=HX[0][:]), reads=[("HX", 0, 0), ("HX", 0, 1)], dsem="dbg_HX")
        if DEBUG and t == 0 and STAGE == 2:
            break


    FX = sb("FX", [128, 8, 128])
    P.op("pool", lambda e: e.memset(FX[:], 0.0), writes=["FX"])
    for q in range(2):
        rk = slice(64 * q, 64 * q + 64)
        P.op("pool", lambda e, q=q, rk=rk: e.tensor_copy(
            out=FX[rk, q::2, 64 * q:64 * q + 64], in_=ident[rk, rk].unsqueeze(1).to_broadcast([64, 4, 64])),
            reads=["CST", "FX"], writes=["FX"])
    if STAGE >= 3 and (NRUN == NCORES or FAKE_CC):
        for hp in range(4):
            P.dma("sp", lambda e, hp=hp: e.dma_start(out=hx_loc[:, hp * 128:(hp + 1) * 128], in_=HX[hp][:]),
                  reads=[("HX", hp, 0), ("HX", hp, 1)], writes=["hx_loc"], dsem="hxl")
        ccsem = st.enter_context(nc.semaphore("ccsem"))
        ccd = sb("ccdummy", [128, 2])

        def cc(e):
            ins = e.collective_compute("AllGather", ALU.bypass, replica_groups=[list(range(NCORES))],
                                       ins=[hx_loc.tensor.ap().opt()], outs=[hx_all.tensor.ap().opt()])
            ins.then_inc(ccsem)
            e.wait_ge(ccsem, 1)
            return e.memset(ccd[:], 0.0)
        if FAKE_CC:
            P.dma("sp", lambda e: [e.dma_start(out=hx_all[j * 128:(j + 1) * 128, :], in_=hx_loc[:, :]) for j in range(NCORES)],
                  reads=["hx_loc"], writes=["hx_all"], dsem="fcc", nd=NCORES)
        else:
            P.op("POOL!", cc, reads=["hx_loc"], writes=["hx_all"])

        def ld_hxa(e):
            r = []
            for j in range(NCORES):
                r.append(e.dma_start(out=YSk[j][0:64, :], in_=hx_all[j * 128 + 64:j * 128 + 128, :]))
                r.append(e.dma_start(out=YSk[j][64:128, :], in_=hx_all[j * 128:j * 128 + 64, :]))
            return r
        P.dma("sp", ld_hxa, reads=["hx_all"], writes=[("YS", j) for j in range(KC)], dsem="hxa", nd=16)
        HC, SS = TT[0], TT[1]
        HCB = G[:, 20, 0:256]
        P.op("pool", lambda e: e.memset(SS[:, 0:256], 0.0), writes=[("TT", 1)])
        for j in range(NCORES - 1):
            P.op("pool", lambda e, j=j: e.tensor_copy(out=G[:, 8 + j, :], in_=YSk[j][:]), reads=[("YS", j)], writes=[("G", 8 + j)])
            if j > 0:
                pt = G[:, 16 + j // 2, (j % 2) * 256:(j % 2) * 256 + 256]
                for q in range(2):
                    rb = slice(64 * (1 - q), 64 * (1 - q) + 64)
                    kcol = 64 * (1 - q)
                    bq = newbank()
                    for hp in range(4):
                        P.op("pe", lambda e, j=j, hp=hp, rb=rb, kcol=kcol, bq=bq: e.matmul(
                            PS[bq][rb, hp * 64:(hp + 1) * 64], lhsT=G[rb, 8 + j, hp * 128 + kcol:hp * 128 + kcol + 64],
                            rhs=IDB[rb, rb], start=True, stop=True),
                            reads=[("G", 8 + j), "IDB"], writes=[("ps", bq)], signal=(hp == 3))
                    P.op("act", lambda e, rb=rb, bq=bq, pt=pt: e.copy(out=pt[rb, :], in_=PS[bq][rb, 0:256]),
                         reads=[("ps", bq)], writes=[("G", 16 + j // 2, 1 - q)])
            for q in range(2):
                rb = slice(64 * (1 - q), 64 * (1 - q) + 64)
                hl = YSk[j][rb, :].rearrange("p (h c) -> p h c", c=128)[:, :, 64 * q:64 * q + 64]
                hc3 = HC[rb, 0:256].rearrange("p (h c) -> p h c", c=64)
                if j == 0:
                    P.op("dve", lambda e, hl=hl, hc3=hc3: e.tensor_copy(out=hc3, in_=hl), reads=[("YS", j)], writes=[("TT", 0, 1 - q)])
                else:
                    pt = G[:, 16 + j // 2, (j % 2) * 256:(j % 2) * 256 + 256]
                    bq = newbank()
                    for hp in range(4):
                        P.op("pe", lambda e, hp=hp, rb=rb, bq=bq, pt=pt: e.matmul(
                            PS[bq][rb, hp * 64:(hp + 1) * 64], lhsT=pt[rb, hp * 64:(hp + 1) * 64],
                            rhs=HCB[rb, hp * 64:(hp + 1) * 64], start=True, stop=True),
                            reads=[("G", 16 + j // 2, 1 - q), ("G", 20, 1 - q)], writes=[("ps", bq)], signal=(hp == 3))
                    P.op("dve", lambda e, rb=rb, bq=bq, hl=hl, hc3=hc3: e.tensor_tensor(
                        out=hc3, in0=PS[bq][rb, 0:256].rearrange("p (h c) -> p h c", c=64), in1=hl, op=ALU.add),
                        reads=[("ps", bq), ("YS", j)], writes=[("TT", 0, 1 - q)])
                P.op("pool", lambda e, rb=rb: e.tensor_copy(out=HCB[rb, :], in_=HC[rb, 0:256]),
                     reads=[("TT", 0, 1 - q)], writes=[("G", 20, 1 - q)])
                P.op("dve", lambda e, rb=rb, j=j: e.scalar_tensor_tensor(
                    out=SS[rb, 0:256], in0=HC[rb, 0:256], scalar=VEC[rb, V_ONEHOT + j + 1:V_ONEHOT + j + 2],
                    in1=SS[rb, 0:256], op0=ALU.mult, op1=ALU.add),
                    reads=[("TT", 0, 1 - q), "VEC", ("TT", 1)], writes=[("TT", 1)])
        for q in range(2):
            rb = slice(64 * (1 - q), 64 * (1 - q) + 64)
            P.op("dve", lambda e, q=q, rb=rb: e.tensor_copy(
                out=FX[rb, q::2, 64 * q:64 * q + 64], in_=SS[rb, 0:256].rearrange("p (h c) -> p h c", c=64)),
                reads=[("TT", 1), "FX"], writes=["FX"])

    w_outv = w_out.rearrange("(kc p) n -> p kc n", p=128)
    yc_v = yc_scr.rearrange("(c p) n -> p c n", p=128)

    def phaseB_tile(t):
        tok = slice(t * TB, (t + 1) * TB)
        xt, xkey = XT[0], ("XT", 0)
        P.dma("sp", lambda e: e.dma_start(out=xt[:], in_=h_scrv[:, :, tok]), reads=[("hs", t)], writes=[xkey], dsem="xt0")
        P.dma("sp", lambda e: e.dma_start(out=YC[:], in_=yc_v[:, :, tok]), reads=[("ycs", t)], writes=[("YC", c) for c in range(4)], dsem="ycl")

        def hpB(hp):
            tb_ = 4 if hp % 2 == 0 else 0
            yb_ = 0 if hp % 2 == 0 else 4
            Y0, Y1, GT, BON = TT[tb_], TT[tb_ + 1], TT[tb_ + 2], TT[tb_ + 3]

            def ld(e):
                return [e.dma_start(out=Y0[:], in_=ye_scr[(2 * hp) * 128:(2 * hp + 1) * 128, tok]),
                        e.dma_start(out=Y1[:], in_=ye_scr[(2 * hp + 1) * 128:(2 * hp + 2) * 128, tok]),
                        e.dma_start(out=GT[:], in_=g_scr[hp * 128:(hp + 1) * 128, tok]),
                        e.dma_start(out=BON[:], in_=bo_scr[hp * 128:(hp + 1) * 128, tok])]
            P.dma("sp", ld, reads=[("yes", 2 * hp, t), ("yes", 2 * hp + 1, t), ("gs", hp, t), ("bs", hp, t)],
                  writes=[("TT", tb_), ("TT", tb_ + 1), ("TT", tb_ + 2), ("TT", tb_ + 3)], dsem="yel", nd=4)
            bo = newbank()
            P.op("pe", lambda e: e.matmul(PS[bo][:, 0:TB], lhsT=FX[:, 2 * hp, :], rhs=Y0[:], start=True, stop=False),
                 reads=["FX", ("TT", tb_)], writes=[("ps", bo)], signal=False)
            P.op("pe", lambda e: e.matmul(PS[bo][:, 0:TB], lhsT=FX[:, 2 * hp + 1, :], rhs=Y1[:], start=False, stop=True),
                 reads=["FX", ("TT", tb_ + 1)], writes=[("ps", bo)])
            OS, DD, SQ, RS = YSk[yb_], YSk[yb_ + 1], YSk[yb_ + 2], YSk[yb_ + 3]
            P.op("act", lambda e: e.copy(out=OS[:], in_=PS[bo][:, 0:TB]), reads=[("ps", bo)], writes=[("YS", yb_)])
            bm = newbank()
            P.op("pe", lambda e: e.matmul(PS[bm][:, 0:TB], lhsT=bones, rhs=OS[:], start=True, stop=True),
                 reads=["CST", ("YS", yb_)], writes=[("ps", bm)])
            P.op("dve", lambda e: e.scalar_tensor_tensor(out=DD[:], in0=PS[bm][:, 0:TB], scalar=-1.0 / 64, in1=OS[:],
                                                         op0=ALU.mult, op1=ALU.add),
                 reads=[("ps", bm), ("YS", yb_)], writes=[("YS", yb_ + 1)])
            P.op("pool", lambda e: e.tensor_tensor(out=SQ[:], in0=DD[:], in1=DD[:], op=ALU.mult), reads=[("YS", yb_ + 1)], writes=[("YS", yb_ + 2)])
            bv = newbank()
            P.op("pe", lambda e: e.matmul(PS[bv][:, 0:TB], lhsT=bones, rhs=SQ[:], start=True, stop=True),
                 reads=["CST", ("YS", yb_ + 2)], writes=[("ps", bv)])
            rstd_from_sumsq(bv, TB, 64, 1, RS[:], ("YS", yb_ + 3))
            P.op("dve", lambda e: e.tensor_tensor(out=DD[:], in0=DD[:], in1=RS[:], op=ALU.mult),
                 reads=[("YS", yb_ + 1), ("YS", yb_ + 3)], writes=[("YS", yb_ + 1)])
            P.op("dve", lambda e: e.scalar_tensor_tensor(out=SQ[:], in0=DD[:], scalar=VEC[:, V_LNW + hp:V_LNW + hp + 1], in1=BON[:],
                                                         op0=ALU.mult, op1=ALU.add),
                 reads=[("YS", yb_ + 1), "VEC", ("TT", tb_ + 3), ("YS", yb_ + 2)], writes=[("YS", yb_ + 2)])
            P.op("pool", lambda e: e.tensor_tensor(out=G[:, hp, :], in0=SQ[:], in1=GT[:], op=ALU.mult),
                 reads=[("YS", yb_ + 2), ("TT", tb_ + 2)], writes=[("G", hp)])
        for hp in range(4):
            hpB(hp)

        def wo(n):
            s = load_w([(n * 256, 256)], w_outv)
            for mm_ in range(2):
                m = n * 2 + mm_
                b = newbank()
                for kc in range(KC):
                    rhs = YC[:, kc, :] if kc < 4 else G[:, kc - 4, :]
                    rkey = ("YC", kc) if kc < 4 else ("G", kc - 4)
                    P.op("pe", lambda e, b=b, kc=kc, rhs=rhs, mm_=mm_: e.matmul(
                        PS[b][:, 0:TB], lhsT=W13[s][:, kc, mm_ * 128:(mm_ + 1) * 128], rhs=rhs, start=(kc == 0), stop=(kc == KC - 1)),
                        reads=[("W13", s), rkey], writes=[("ps", b)], signal=(kc == KC - 1))
                P.op("act", lambda e, b=b, m=m: e.copy(out=YSk[m][:], in_=PS[b][:, 0:TB]), reads=[("ps", b)], writes=[("YS", m)])
        for n in range(4):
            wo(n)
        post_residual(xt, xkey, TB, 24)
        ffn("ffn2", [(xt, xkey, TB, MAIN)], 32, 48, 40)
        P.dma("sp", lambda e: e.dma_start(out=outTv[:, :, tok], in_=xt[:]), reads=[xkey], dsem="xo0")

    if STAGE >= 3:
        for t in range(NT):
            phaseB_tile(t)

    P.finalize_and_emit()
    st.close()
    return nc


_NC_CACHE = {}


def kernel(**inputs):
    x = np.asarray(inputs["x"], np.float32)[0]
    g = lambda k: np.asarray(inputs[k], np.float32)[0]
    cst = _consts()
    xTfull = np.ascontiguousarray(x.T)
    xpad = np.concatenate([np.zeros((D, 2), np.float32), xTfull], axis=1)
    lora_wa = np.ascontiguousarray(np.concatenate([g("w_up"), g("a_up")], axis=0))
    wts = {"w_ada": np.ascontiguousarray(g("w_ada").reshape(KC, 128, 36, 256).transpose(2, 1, 0, 3)).reshape(36 * 128, KC * 256)}
    for n in ("ffn1", "ffn2"):
        w1r = g(n + "_w1").reshape(KC, 128, JC, 128).transpose(2, 1, 0, 3)
        w3r = g(n + "_w3").reshape(KC, 128, JC, 128).transpose(2, 1, 0, 3)
        wts[n + "_w13"] = np.ascontiguousarray(np.concatenate([w1r, w3r], axis=3)).reshape(JC * 128, KC * 256)
        wts[n + "_w2"] = np.ascontiguousarray(g(n + "_w2").reshape(JC, 128, KC, 128).transpose(2, 1, 0, 3)).reshape(KC * 128, JC * 128)
    in_maps = []
    for c in range(NCORES):
        vec = np.zeros((128, NVEC), np.float32)
        vec[:, V_C:V_C + 8] = _col(g("c"), 8)
        vec[:, V_GPRE1:V_GPRE1 + 8] = _col(g("ffn1_g_pre"), 8)
        vec[:, V_GPOST1:V_GPOST1 + 8] = _col(g("ffn1_g_post"), 8)
        vec[:, V_GPRE2:V_GPRE2 + 8] = _col(g("mix_g_pre"), 8)
        vec[:, V_GPOST2:V_GPOST2 + 8] = _col(g("mix_g_post"), 8)
        vec[:, V_GPRE3:V_GPRE3 + 8] = _col(g("ffn2_g_pre"), 8)
        vec[:, V_GPOST3:V_GPOST3 + 8] = _col(g("ffn2_g_post"), 8)
        mu = g("mu_shift")
        vec[:, V_MU:V_MU + 12] = _col(mu[:1536], 12)
        vec[0:64, V_MU + 12] = mu[1536:1600]
        vec[0:96, V_MU + 13] = mu[1600:1696]
        cw = g("conv_w")
        for ch in range(4):
            for j in range(3):
                vec[:, V_CONV + ch * 3 + j] = cw[j, ch * 128:(ch + 1) * 128]
        for (off, k) in ((V_W0, "w0"), (V_A0, "a0"), (V_KK, "k_k"), (V_KA, "k_a"), (V_LNW, "ln_x_w"), (V_LNB, "ln_x_b")):
            vec[:, off:off + 4] = _col(g(k), 4)
        vec[:, V_RK:V_RK + 4] = _col(g("r_k").reshape(-1), 4)
        vec[:, V_FLAG] = 0.0 if c == 0 else 1.0
        vec[:, V_ONEHOT + c] = 1.0
        m = {
            "xT": np.ascontiguousarray(xpad[:, c * NTOK:c * NTOK + NTOK + 2]),
            "vec": vec, "cst": cst,
            "b_ada": g("b_ada").reshape(1, -1), "w_ada": wts["w_ada"],
            "ffn1_w13": wts["ffn1_w13"], "ffn1_w2": wts["ffn1_w2"],
            "ffn2_w13": wts["ffn2_w13"], "ffn2_w2": wts["ffn2_w2"],
            "w_in": g("w_in"), "w_out": g("w_out"), "lora_wa": lora_wa, "lora_g": g("g_up"),
        }
        in_maps.append(m)
    if "nc" not in _NC_CACHE:
        _NC_CACHE["nc"] = build()
    nc = _NC_CACHE["nc"]
    res = run_bass_kernel_spmd(nc, in_maps[:NRUN], core_ids=list(range(NRUN)))
    _NC_CACHE["res"] = res
    outs = [np.asarray(r["outT"], np.float32) for r in res.results]
    outs = outs + [np.zeros_like(outs[0])] * (NCORES - NRUN)
    full = np.concatenate(outs, axis=1)
    return np.ascontiguousarray(full.T)[None].astype(np.float32)
```
